# Optimizing a Trainium2 kernel written in Bass

```python
import functools
import jax, jax.numpy as jnp
from jax import lax
import numpy as np

D_MODEL = 1024
BATCH = 8
SEQ = 2048
DEPTH = 4
DEC_BATCH = 128
DEC_SEQ = 4
PAST_LEN = 2048
PAGE_SIZE = 128

N_A_LAYERS = DEPTH // 2
N_B_LAYERS = DEPTH - N_A_LAYERS
GLA_HEADS = 4
GLA_QK = D_MODEL // 2
GLA_V = D_MODEL
GLA_DK = GLA_QK // GLA_HEADS
GLA_DV = GLA_V // GLA_HEADS
GLA_LR = 16
GLA_TAU = 16.0
GLA_CHUNK = 64
GLA_IN = 2 * GLA_QK + 2 * GLA_V + GLA_LR
HEAD_DIM = 128
N_KV = 4
Q_PER_KV = 2
N_QH_GROUP = N_KV * Q_PER_KV
DIL_GROUPS = ((128, 1), (512, 4), (2048, 16))
N_GROUPS = len(DIL_GROUPS)
MAX_WINDOW = max(w for w, _ in DIL_GROUPS)
ROPE_DIM = HEAD_DIM // 4
ROPE_THETA = 500000.0
FFN_HIDDEN = ((8 * D_MODEL + 3 * 256 - 1) // (3 * 256)) * 256
PLE_DIM = 256
EPS = 1e-6
NEG_INF = -1e30

kernel_name = 'yoco_gla_dilated_swa_decode_step'


def rms_norm(x, g):
    xf = x.astype(jnp.float32)
    y = xf * lax.rsqrt(jnp.mean(xf * xf, axis=-1, keepdims=True) + EPS)
    return (y * g.astype(jnp.float32)).astype(x.dtype)


def rope_partial(x, pos):
    half = ROPE_DIM // 2
    freqs = jnp.power(ROPE_THETA, -jnp.arange(half, dtype=jnp.float32) * 2.0 / ROPE_DIM)
    ang = pos[:, None] * freqs[None, :]
    shape = (ang.shape[0],) + (1,) * (x.ndim - 3) + (half,)
    cos = jnp.cos(ang).reshape(shape)
    sin = jnp.sin(ang).reshape(shape)
    xf = x.astype(jnp.float32)
    x1 = xf[..., :half]
    x2 = xf[..., half:ROPE_DIM]
    out = jnp.concatenate([x1 * cos - x2 * sin, x2 * cos + x1 * sin, xf[..., ROPE_DIM:]], axis=-1)
    return out.astype(x.dtype)


def gla_chunked(q, k, v, log_a, s0):
    B, L, H, DK = q.shape
    DV = v.shape[-1]
    C = GLA_CHUNK if L % GLA_CHUNK == 0 else L
    n = L // C

    def blocks(t):
        return t.reshape(B, n, C, H, t.shape[-1]).transpose(1, 0, 3, 2, 4)

    qb, kb, vb, gb = blocks(q), blocks(k), blocks(v), blocks(log_a)
    cum = jnp.cumsum(gb, axis=3)
    last = cum[:, :, :, -1:, :]
    q_in = qb * jnp.exp(cum)
    k_in = kb * jnp.exp(-cum)
    k_end = kb * jnp.exp(last - cum)
    causal = jnp.tril(jnp.ones((C, C), dtype=bool))
    att = jnp.where(causal, jnp.einsum('nbhcd,nbhsd->nbhcs', q_in, k_in), 0.0)
    o_intra = jnp.einsum('nbhcs,nbhsv->nbhcv', att, vb)
    decay = jnp.exp(last[:, :, :, 0, :])

    def step(s, inp):
        qi, ke, vi, dc = inp
        o = jnp.einsum('bhcd,bhdv->bhcv', qi, s)
        s = dc[..., None] * s + jnp.einsum('bhcd,bhcv->bhdv', ke, vi)
        return s, o

    s_fin, o_inter = lax.scan(step, s0, (q_in, k_end, vb, decay))
    o = (o_intra + o_inter).transpose(1, 0, 3, 2, 4).reshape(B, L, H, DV)
    return o, s_fin


def gla_mixer(h, w_in, w_gk_up, b_gk, onorm, w_o, s0):
    B, L, _ = h.shape
    proj = h @ w_in
    q, k, v, r, glr = jnp.split(proj, [GLA_QK, 2 * GLA_QK, 2 * GLA_QK + GLA_V, 2 * GLA_QK + 2 * GLA_V], axis=-1)
    log_a = jax.nn.log_sigmoid((glr @ w_gk_up + b_gk).astype(jnp.float32)) / GLA_TAU

    def heads(t, d):
        return t.astype(jnp.float32).reshape(B, L, GLA_HEADS, d)

    o, s_new = gla_chunked(heads(q, GLA_DK) * (GLA_DK ** -0.5), heads(k, GLA_DK), heads(v, GLA_DV),
                           log_a.reshape(B, L, GLA_HEADS, GLA_DK), s0.astype(jnp.float32))
    o = rms_norm(o, onorm) * jax.nn.silu(r.astype(jnp.float32)).reshape(B, L, GLA_HEADS, GLA_DV)
    return o.reshape(B, L, GLA_V).astype(h.dtype) @ w_o, s_new


def dilated_band_attention(q, win, dil, k, v):
    B, L = q.shape[:2]
    n = win // dil
    M = L // dil
    nb = -(-M // n)
    Mp = nb * n

    def by_class(t):
        return jnp.moveaxis(t.reshape((B, M, dil) + t.shape[2:]), 2, 1)

    qc = jnp.pad(by_class(q), ((0, 0), (0, 0), (0, Mp - M), (0, 0), (0, 0), (0, 0)))
    qb = qc.reshape(B, dil, nb, n, N_KV, Q_PER_KV, HEAD_DIM)

    def band_keys(t):
        t = jnp.pad(by_class(t), ((0, 0), (0, 0), (n, Mp - M), (0, 0), (0, 0))).reshape(B, dil, nb + 1, n, N_KV, HEAD_DIM)
        return jnp.concatenate([t[:, :, :-1], t[:, :, 1:]], axis=3)

    kb, vb = band_keys(k), band_keys(v)
    s = jnp.einsum('bcnikgd,bcnjkd->bcnkgij', qb, kb, preferred_element_type=jnp.float32) * (HEAD_DIM ** -0.5)
    qi = jnp.arange(nb)[:, None, None] * n + jnp.arange(n)[None, :, None]
    kj = jnp.arange(nb)[:, None, None] * n - n + jnp.arange(2 * n)[None, None, :]
    valid = (kj <= qi) & (kj >= qi - n) & (kj >= 0)
    s = jnp.where(valid[:, None, None], s, NEG_INF)
    lse = jax.nn.logsumexp(s, axis=-1)
    p = jnp.exp(s - lse[..., None])
    o = jnp.einsum('bcnkgij,bcnjkd->bcnikgd', p, vb, preferred_element_type=jnp.float32)
    o = o.reshape(B, dil, Mp, N_KV, Q_PER_KV, HEAD_DIM)[:, :, :M]
    o = jnp.moveaxis(o, 1, 2).reshape(B, L, N_KV, Q_PER_KV, HEAD_DIM)
    lse = jnp.moveaxis(lse, -1, 3).reshape(B, dil, Mp, N_KV, Q_PER_KV)[:, :, :M]
    lse = jnp.moveaxis(lse, 1, 2).reshape(B, L, N_KV, Q_PER_KV)
    return o, lse


def dilated_gather_attention(q, win, dil, k_all, v_all):
    T = q.shape[1]
    S = k_all.shape[1]
    n = win // dil
    idx = (S - T) + jnp.arange(T)[:, None] - dil * jnp.arange(n + 1)[None, :]
    valid = idx >= 0
    idx = jnp.maximum(idx, 0)
    kg = jnp.take(k_all, idx, axis=1)
    vg = jnp.take(v_all, idx, axis=1)
    s = jnp.einsum('btkgd,btikd->btkgi', q, kg, preferred_element_type=jnp.float32) * (HEAD_DIM ** -0.5)
    s = jnp.where(valid[:, None, None, :], s, NEG_INF)
    lse = jax.nn.logsumexp(s, axis=-1)
    p = jnp.exp(s - lse[..., None])
    o = jnp.einsum('btkgi,btikd->btkgd', p, vg, preferred_element_type=jnp.float32)
    return o, lse


def dilated_mixer(h, w_q, q_norm, w_o, pos, attend):
    B, L, _ = h.shape
    q = (h @ w_q).reshape(B, L, N_GROUPS, N_KV, Q_PER_KV, HEAD_DIM)
    q = rope_partial(rms_norm(q, q_norm), pos)
    outs, lses = [], []
    for g, (win, dil) in enumerate(DIL_GROUPS):
        o, l = attend(q[:, :, g], win, dil)
        outs.append(o)
        lses.append(l)
    w = jax.nn.softmax(jnp.stack(lses), axis=0)
    o = jnp.sum(w[..., None] * jnp.stack(outs), axis=0)
    return o.reshape(B, L, N_QH_GROUP * HEAD_DIM).astype(h.dtype) @ w_o


def shared_kv(x, kv_norm, w_kv, k_norm, pos):
    B, L, _ = x.shape
    kv = (rms_norm(x, kv_norm) @ w_kv).reshape(B, L, 2, N_KV, HEAD_DIM)
    k = rope_partial(rms_norm(kv[:, :, 0], k_norm), pos)
    return k, kv[:, :, 1]


def swiglu_ffn(x, g, w_in, w_out):
    a, b = jnp.split(rms_norm(x, g) @ w_in, 2, axis=-1)
    return (jax.nn.silu(a) * b) @ w_out


def per_layer_embed(x, p, g, w_proj, w_gate):
    return (p @ w_proj) * jax.nn.sigmoid(rms_norm(x, g) @ w_gate)


def run_trunk(x, p, gla_s0, k_past, v_past, pos, a_norm, a_w_in, a_w_gk_up, a_b_gk, a_onorm, a_w_o,
              kv_norm, w_kv, k_norm, b_norm, b_w_q, b_q_norm, b_w_o, f_norm, f_w_in, f_w_out,
              e_norm, e_w_proj, e_w_gate):
    gla_states = []
    k_new = v_new = attend = None
    for i in range(DEPTH):
        if i < N_A_LAYERS:
            h, s = gla_mixer(rms_norm(x, a_norm[i]), a_w_in[i], a_w_gk_up[i], a_b_gk[i], a_onorm[i], a_w_o[i], gla_s0[i])
            gla_states.append(s)
        else:
            j = i - N_A_LAYERS
            if j == 0:
                k_new, v_new = shared_kv(x, kv_norm, w_kv, k_norm, pos)
                if k_past is None:
                    attend = functools.partial(dilated_band_attention, k=k_new, v=v_new)
                else:
                    k_all = jnp.concatenate([k_past.astype(k_new.dtype), k_new], axis=1)
                    v_all = jnp.concatenate([v_past.astype(v_new.dtype), v_new], axis=1)
                    attend = functools.partial(dilated_gather_attention, k_all=k_all, v_all=v_all)
            h = dilated_mixer(rms_norm(x, b_norm[j]), b_w_q[j], b_q_norm[j], b_w_o[j], pos, attend)
        x = x + h.astype(x.dtype)
        x = x + swiglu_ffn(x, f_norm[i], f_w_in[i], f_w_out[i]).astype(x.dtype)
        x = x + per_layer_embed(x, p[i], e_norm[i], e_w_proj[i], e_w_gate[i]).astype(x.dtype)
    return x, jnp.stack(gla_states), k_new, v_new


def setup_inputs(seed: int = 0) -> dict:
    key = jax.random.key(seed)
    keys = iter(jax.random.split(key, 32))

    def nrm(shape, scale):
        return jax.random.normal(next(keys), shape, jnp.float32) * scale

    def gain(shape):
        return 1.0 + nrm(shape, 0.02)

    buf = min(MAX_WINDOW, PAST_LEN)
    return {
        'x_prompt': nrm((BATCH, SEQ, D_MODEL), 1.0),
        'x_sample': nrm((DEC_BATCH, DEC_SEQ, D_MODEL), 1.0),
        'p_prompt': nrm((DEPTH, BATCH, SEQ, PLE_DIM), 1.0),
        'p_sample': nrm((DEPTH, DEC_BATCH, DEC_SEQ, PLE_DIM), 1.0),
        'state_gla': nrm((N_A_LAYERS, DEC_BATCH, GLA_HEADS, GLA_DK, GLA_DV), 0.5),
        'cache_k': nrm((DEC_BATCH, buf, N_KV, HEAD_DIM), 1.0),
        'cache_v': nrm((DEC_BATCH, buf, N_KV, HEAD_DIM), 1.0),
        'a_norm': gain((N_A_LAYERS, D_MODEL)),
        'a_w_in': nrm((N_A_LAYERS, D_MODEL, GLA_IN), D_MODEL ** -0.5),
        'a_w_gk_up': nrm((N_A_LAYERS, GLA_LR, GLA_QK), GLA_LR ** -0.5),
        'a_b_gk': nrm((N_A_LAYERS, GLA_QK), 0.1),
        'a_onorm': gain((N_A_LAYERS, GLA_DV)),
        'a_w_o': nrm((N_A_LAYERS, GLA_V, D_MODEL), GLA_V ** -0.5),
        'kv_norm': gain((D_MODEL,)),
        'w_kv': nrm((D_MODEL, 2 * N_KV * HEAD_DIM), D_MODEL ** -0.5),
        'k_norm': gain((HEAD_DIM,)),
        'b_norm': gain((N_B_LAYERS, D_MODEL)),
        'b_w_q': nrm((N_B_LAYERS, D_MODEL, N_GROUPS * N_QH_GROUP * HEAD_DIM), D_MODEL ** -0.5),
        'b_q_norm': gain((N_B_LAYERS, HEAD_DIM)),
        'b_w_o': nrm((N_B_LAYERS, N_QH_GROUP * HEAD_DIM, D_MODEL), (N_QH_GROUP * HEAD_DIM) ** -0.5),
        'f_norm': gain((DEPTH, D_MODEL)),
        'f_w_in': nrm((DEPTH, D_MODEL, 2 * FFN_HIDDEN), D_MODEL ** -0.5),
        'f_w_out': nrm((DEPTH, FFN_HIDDEN, D_MODEL), FFN_HIDDEN ** -0.5),
        'e_norm': gain((DEPTH, D_MODEL)),
        'e_w_proj': nrm((DEPTH, PLE_DIM, D_MODEL), PLE_DIM ** -0.5),
        'e_w_gate': nrm((DEPTH, D_MODEL, D_MODEL), D_MODEL ** -0.5),
    }


def reference(x_prompt, x_sample, p_prompt, p_sample, state_gla, cache_k, cache_v,
              a_norm, a_w_in, a_w_gk_up, a_b_gk, a_onorm, a_w_o, kv_norm, w_kv, k_norm,
              b_norm, b_w_q, b_q_norm, b_w_o, f_norm, f_w_in, f_w_out, e_norm, e_w_proj, e_w_gate):
    weights = (a_norm, a_w_in, a_w_gk_up, a_b_gk, a_onorm, a_w_o, kv_norm, w_kv, k_norm,
               b_norm, b_w_q, b_q_norm, b_w_o, f_norm, f_w_in, f_w_out, e_norm, e_w_proj, e_w_gate)
    seq = x_prompt.shape[1]
    dec_seq = x_sample.shape[1]
    pos_prompt = jnp.arange(seq, dtype=jnp.float32)
    pos_sample = PAST_LEN + jnp.arange(dec_seq, dtype=jnp.float32)
    s0 = jnp.zeros((N_A_LAYERS, x_prompt.shape[0], GLA_HEADS, GLA_DK, GLA_DV), jnp.float32)
    y_prompt, gla_state_prompt, k_p, v_p = run_trunk(x_prompt, p_prompt, s0, None, None, pos_prompt, *weights)
    y_sample, gla_state_sample, k_sample, v_sample = run_trunk(x_sample, p_sample, state_gla, cache_k, cache_v, pos_sample, *weights)
    keep = min(MAX_WINDOW, seq)
    k_prompt = k_p[:, seq - keep:]
    v_prompt = v_p[:, seq - keep:]
    return (y_prompt, y_sample, gla_state_prompt, gla_state_sample, k_prompt, v_prompt, k_sample, v_sample)
```

```python
import contextlib
import numpy as np
import concourse.bass as bass
import concourse.mybir as mybir
from concourse.bass_utils import run_bass_kernel_spmd

F32 = mybir.dt.float32
BF16 = mybir.dt.bfloat16
AF = mybir.ActivationFunctionType
ALU = mybir.AluOpType

NCORES = 8
D = 1024
KC = 8
TP = 2048
TS = 64
T = TP + TS
NSEQ = 16
EPS = 1e-6
FFN_H = 2816
GROUPS = [(0, 512), (512, 512), (1024, 512), (1536, 512), (2048, 64)]
TILES = [(i * 128, 128) for i in range(16)] + [(2048, 64)]
DILS = [(128, 1), (512, 4), (2048, 16)]


class Trk:
    __slots__ = ('name', 'w', 'r')

    def __init__(self, name='', init=None):
        self.name = name
        self.w = None
        self.r = dict(init) if init else {}


class Buf(Trk):
    def __init__(self, name, t, init=None):
        super().__init__(name, init)
        self.t = t
        self.subs = {}
        self.init = dict(init) if init else {}

    def sub(self, key):
        s = self.subs.get(key)
        if s is None:
            s = Trk(f"{self.name}.{key}", self.init)
            self.subs[key] = s
        return s

    def all(self):
        return [self] + list(self.subs.values())

    def __getitem__(self, k):
        return self.t[k]


class Prog:
    ENG = ['pe', 'act', 'dve', 'pool', 'sp']
    EPOCH = 16000
    DEPOCH = 1000

    def __init__(self, nc):
        self.nc = nc
        self.es = contextlib.ExitStack()
        self.q = {e: [] for e in self.ENG}
        self.cnt = {e: 0 for e in self.ENG}
        self.seen = {e: {} for e in self.ENG}
        self.dcnt = {}
        self.fin = {}
        self.freed = {}
        self.scopes = []
        self.psf = []
        self.psb = []
        self.ipf = 0
        self.ipb = 0
        self.held = set()

    def _merge_freed(self, b):
        for t in b.all():
            deps = dict(t.r)
            if t.w is not None:
                deps[t.w[0]] = max(deps.get(t.w[0], 0), t.w[1])
            for k, v in deps.items():
                if self.freed.get(k, 0) < v:
                    self.freed[k] = v

    @contextlib.contextmanager
    def scope(self):
        es = contextlib.ExitStack()
        bufs = []
        self.scopes.append((es, bufs))
        try:
            yield
        finally:
            self.scopes.pop()
            for b in bufs:
                self._merge_freed(b)
            es.close()

    def sbuf(self, name, shape, dt):
        es, bufs = self.scopes[-1] if self.scopes else (self.es, [])
        self.nalloc = getattr(self, 'nalloc', 0) + 1
        name = f"{name}_{self.nalloc}"
        t = es.enter_context(self.nc.sbuf_tensor(name, list(shape), dt))
        b = Buf(name, t, self.freed)
        bufs.append(b)
        return b

    def psum(self, name, shape, dt):
        t = self.es.enter_context(self.nc.psum_tensor(name, list(shape), dt))
        return Buf(name, t)

    def next_ps(self, exclude=()):
        while True:
            b = self.psf[self.ipf % len(self.psf)]
            self.ipf += 1
            if b not in exclude and b not in self.held:
                return b

    def next_pb(self):
        b = self.psb[self.ipb % len(self.psb)]
        self.ipb += 1
        return b

    def op(self, e, fn, reads=(), writes=(), dsem=None):
        deps = {}

        def add(k, v):
            if deps.get(k, 0) < v:
                deps[k] = v
        for t in reads:
            if t.w is not None:
                add(*t.w)
        for t in writes:
            if t.w is not None:
                add(*t.w)
            for k, v in t.r.items():
                add(k, v)
        waits = []
        seen = self.seen[e]
        for k, v in deps.items():
            if e == 'pe' and k.startswith('pe#'):
                continue
            if seen.get(k, 0) >= v:
                continue
            seen[k] = v
            waits.append((k, v))
        if dsem is None:
            self.cnt[e] += 1
            ep = (self.cnt[e] - 1) // self.EPOCH
            me = (f"{e}#{ep}", self.cnt[e] - ep * self.EPOCH)
            self.fin[me[0]] = (me[1], 1)
        else:
            c = self.dcnt.get(dsem, 0) + 1
            self.dcnt[dsem] = c
            ep = (c - 1) // self.DEPOCH
            me = (f"{dsem}#{ep}", 16 * (c - ep * self.DEPOCH))
            self.fin[me[0]] = (me[1], 16)
        for t in reads:
            if t.r.get(me[0], 0) < me[1]:
                t.r[me[0]] = me[1]
        for t in writes:
            t.w = me
            t.r = {}
        self.q[e].append((waits, fn, me))

    def emit(self):
        nc = self.nc
        sems = {k: self.es.enter_context(nc.semaphore("s_" + k.replace('#', '_'))) for k in self.fin}
        finals = [(k, v[0]) for k, v in self.fin.items()]

        def replay(e, eng):
            for waits, fn, me in self.q[e]:
                for k, v in waits:
                    eng.wait_ge(sems[k], v)
                ins = fn(eng)
                ins.then_inc(sems[me[0]], self.fin[me[0]][1])
            if e == 'sp':
                for k, v in finals:
                    eng.wait_ge(sems[k], v)
        with nc.Block() as block:
            @block.tensor
            def _(eng):
                replay('pe', eng)

            @block.scalar
            def _(eng):
                replay('act', eng)

            @block.vector
            def _(eng):
                replay('dve', eng)

            @block.gpsimd
            def _(eng):
                replay('pool', eng)

            @block.sync
            def _(eng):
                replay('sp', eng)
        self.es.close()


def _const_blocks():
    f = {}
    b = {}
    gl = {}
    i128 = np.arange(128)
    f['identF'] = np.eye(128, dtype=np.float32)
    s, t = np.meshgrid(i128, i128, indexing='ij')
    same64 = (s // 64 == t // 64) & (s <= t)
    gl['triNeg'] = np.where(same64, -1.0 / 16.0, 0.0).astype(np.float32)
    gl['maskC'] = same64.astype(np.float32)
    same4 = (s // 4 == t // 4) & (s <= t)
    gl['tri4Neg'] = np.where(same4, -1.0 / 16.0, 0.0).astype(np.float32)[:, :64]
    gl['triRevNeg'] = np.where((s // 64 == t // 64) & (s > t), -1.0 / 16.0, 0.0).astype(np.float32)
    gl['tri4RevNeg'] = np.where((s // 4 == t // 4) & (s > t), -1.0 / 16.0, 0.0).astype(np.float32)[:, :64]
    gl['maskC4'] = np.tile(same64.astype(np.float32), (1, 4))
    gl['mask4'] = same4.astype(np.float32)[:, :64]
    rm = np.zeros((128, 16), np.float32)
    for p in range(64):
        rm[p, p // 4] = 1.0
    gl['rowmask'] = rm
    b['ones'] = np.ones((128, 128), np.float32)
    b['identB'] = np.eye(128, dtype=np.float32)
    NEG = -30000.0
    nm2 = np.zeros((128, 256), np.float32)
    nm2[:, 0:128] = np.where(s >= t, 0.0, NEG)
    nm2[:, 128:256] = np.where(s <= t, 0.0, NEG)
    b['negmask2'] = nm2
    nf = np.full((128, 72), NEG, np.float32)
    for gq in range(2):
        for tt in range(4):
            nf[tt:, 0 * 8 + gq * 4 + tt] = 0.0
            nf[:, (1 + tt) * 8 + gq * 4 + tt] = 0.0
            nf[:, (5 + tt) * 8 + gq * 4 + tt] = 0.0
    b['negfull'] = nf
    nn = np.full((128, 16, 24), NEG, np.float32)
    for bb in range(16):
        for t2 in range(4):
            p = 4 * bb + t2
            for g in range(3):
                for gq in range(2):
                    for tt in range(4):
                        ok = (t2 <= tt) if g == 0 else (t2 == tt)
                        if ok:
                            nn[p, bb, g * 8 + gq * 4 + tt] = 0.0
    b['negNew'] = nn.reshape(128, 16 * 24)
    return f, b, gl


def _rope_table():
    half = 16
    freqs = np.power(np.float32(500000.0), -np.arange(half, dtype=np.float32) * np.float32(2.0) / np.float32(32.0)).astype(np.float32)
    pos = np.concatenate([np.arange(TP, dtype=np.float32), np.tile(TP + np.arange(4, dtype=np.float32), NSEQ)]).astype(np.float32)
    ang = (pos[None, :] * freqs[:, None]).astype(np.float32)
    C = np.ones((128, T), np.float32)
    S = np.zeros((128, T), np.float32)
    C[0:16] = np.cos(ang)
    C[16:32] = np.cos(ang)
    S[0:16] = -np.sin(ang)
    S[16:32] = np.sin(ang)
    return np.ascontiguousarray(np.stack([C, S], axis=1))


def _fm(v, nchunk):
    return np.ascontiguousarray(np.asarray(v, np.float32).reshape(nchunk, 128).T)


class _Cols:
    def __init__(self):
        self.off = {}
        self.n = 0
        self.parts = []

    def add(self, name, arr):
        arr = np.asarray(arr, np.float32)
        if arr.shape[0] < 128:
            arr = np.concatenate([arr, np.zeros((128 - arr.shape[0],) + arr.shape[1:], np.float32)], 0)
        self.off[name] = (self.n, arr.shape[1])
        self.n += arr.shape[1]
        self.parts.append(arr)

    def array(self):
        return np.ascontiguousarray(np.concatenate(self.parts, axis=1))


def _layout_consts(inputs):
    f, b, gl = _const_blocks()
    cf = _Cols()
    for k, v in f.items():
        cf.add(k, v)
    cg = _Cols()
    for k, v in gl.items():
        cg.add(k, v)
    for i in range(2):
        cf.add(f'a_norm{i}', _fm(inputs['a_norm'][i], 8))
        cf.add(f'b_norm{i}', _fm(inputs['b_norm'][i], 8))
        cf.add(f'a_onorm{i}', _fm(inputs['a_onorm'][i], 2))
        cf.add(f'b_q_norm{i}', _fm(inputs['b_q_norm'][i], 1))
    for i in range(4):
        cf.add(f'f_norm{i}', _fm(inputs['f_norm'][i], 8))
        cf.add(f'e_norm{i}', _fm(inputs['e_norm'][i], 8))
    cf.add('kv_norm', _fm(inputs['kv_norm'], 8))
    cf.add('k_norm', _fm(inputs['k_norm'], 1))
    cb = _Cols()
    for k, v in b.items():
        cb.add(k, v)
    return cf, cb, cg


def build(cf_off, ncf, cb_off, ncb, cg_off, ncg, steps=None, debug=False):
    nc = bass.Bass("TRN2", target_bir_lowering=False)

    def din(name, shape):
        return nc.dram_tensor(name, list(shape), F32, kind="ExternalInput").ap()

    def dout(name, shape):
        return nc.dram_tensor(name, list(shape), F32, kind="ExternalOutput").ap()

    xin = din("xin", [T, D])
    pin = din("pin", [4, T, 256])
    sgla = din("sgla", [2, NSEQ, 4, 128, 256])
    ck = din("ck", [NSEQ, 2048, 512])
    cv = din("cv", [NSEQ, 2048, 512])
    a_w_in = din("a_w_in", [2, D, 3088])
    a_w_o = din("a_w_o", [2, D, D])
    w_kv = din("w_kv", [D, D])
    b_w_q = din("b_w_q", [2, D, 3072])
    b_w_o = din("b_w_o", [2, D, D])
    f_w_in = din("f_w_in", [4, D, 2 * FFN_H])
    f_w_out = din("f_w_out", [4, FFN_H, D])
    e_w_proj = din("e_w_proj", [4, 256, D])
    e_w_gate = din("e_w_gate", [4, D, D])
    wgk = din("wgk", [2, 17, 512])
    cstf = din("cstf", [128, ncf])
    cstb = din("cstb", [128, ncb])
    cstg = din("cstg", [128, ncg])
    rope = din("rope", [128, 2, T])

    y = dout("y", [T, D])
    gsp = dout("gsp", [2, 4, 128, 256])
    gss = dout("gss", [2, NSEQ, 4, 128, 256])
    kout = dout("kout", [T, 512])
    vout = dout("vout", [T, 512])

    P = Prog(nc)
    for i in range(6):
        P.psf.append(P.psum(f"psf{i}", [128, 512], F32))
    for i in range(2):
        P.psb.append(P.psum(f"psb{i}", [128, 1024], BF16))

    def MM(out, lhsT, rhs, start=True, stop=True, R=(), W=()):
        P.op('pe', lambda e: e.matmul(out, lhsT, rhs, start=start, stop=stop), R, W)

    def TR(out, in_, ident, R=(), W=()):
        P.op('pe', lambda e: e.transpose(out, in_, ident), R, W)

    def ACT(out, in_, func, R=(), W=(), bias=None, scale=None):
        kw = {}
        if bias is not None:
            kw['bias'] = bias
        if scale is not None:
            kw['scale'] = scale
        P.op('act', lambda e: e.activation(out=out, in_=in_, func=func, **kw), R, W)

    def TT(eng, out, in0, in1, op, R=(), W=()):
        P.op(eng, lambda e: e.tensor_tensor(out=out, in0=in0, in1=in1, op=op), R, W)

    def TSC(eng, out, in0, s1, s2, op0, op1, R=(), W=()):
        if s2 is None:
            P.op(eng, lambda e: e.tensor_scalar(out=out, in0=in0, scalar1=s1, scalar2=None, op0=op0), R, W)
        else:
            P.op(eng, lambda e: e.tensor_scalar(out=out, in0=in0, scalar1=s1, scalar2=s2, op0=op0, op1=op1), R, W)

    def STT(eng, out, in0, scalar, in1, op0, op1, R=(), W=()):
        P.op(eng, lambda e: e.scalar_tensor_tensor(out=out, in0=in0, scalar=scalar, in1=in1, op0=op0, op1=op1), R, W)

    def CP(eng, out, in_, R=(), W=()):
        if eng == 'act':
            P.op('act', lambda e: e.copy(out=out, in_=in_), R, W)
        else:
            P.op(eng, lambda e: e.tensor_copy(out=out, in_=in_), R, W)

    def RCP(out, in_, R=(), W=()):
        P.op('dve', lambda e: e.reciprocal(out=out, in_=in_), R, W)

    def MEMSET(eng, ap, val, W=()):
        P.op(eng, lambda e: e.memset(ap, val), (), W)

    def DMA(eng, out, in_, R, W, dsem):
        P.op(eng, lambda e: e.dma_start(out=out, in_=in_), R, W, dsem=dsem)

    class Ring:
        def __init__(self, name, n, shape, dt):
            self.bufs = [P.sbuf(f"{name}{i}", shape, dt) for i in range(n)]
            self.i = 0
            self.name = name

        def next(self):
            k = self.i % len(self.bufs)
            self.i += 1
            return self.bufs[k], f"{self.name}{k}"

    xT = P.sbuf("xT", [128, KC, T], F32)
    hT = P.sbuf("hT", [128, KC, T], BF16)
    cF = P.sbuf("cF", [128, ncf], F32)
    cB = P.sbuf("cB", [128, ncb], BF16)

    class PH:
        G = None
        wring = None

    def phase_alloc(gch, wsz):
        PH.G = P.sbuf("G", [128, gch, T], BF16) if gch else None
        PH.wring = Ring("w", 3, [128, wsz], BF16)
    tmpA = Ring("tmpA", 3, [128, 512], F32)
    tmpB = tmpA
    sqr = Ring("sq", 2, [128, 512], BF16)
    rsr = Ring("rs", 2, [128, 512], F32)

    def cf(name, c0=0, n=None, rows=128):
        o, w = cf_off[name]
        if n is None:
            n = w - c0
        return cF[0:rows, o + c0:o + c0 + n]

    def cb(name, c0=0, n=None, rows=128):
        o, w = cb_off[name]
        if n is None:
            n = w - c0
        return cB[0:rows, o + c0:o + c0 + n]

    DMA('sp', cF[:, :], cstf, [], [cF], 'cF')
    DMA('pool', cB[:, :], cstb, [], [cB], 'cB')

    def grp_of(t0):
        return min(t0 // 512, 4)

    def xs(g):
        return xT.sub(g)

    def hs(g):
        return hT.sub(g)

    def gs(g):
        return PH.G.sub(g)

    ALLG = list(range(5))

    def load_w(segs, nk, ncols):
        slot, sem = PH.wring.next()
        view = slot.t[:, 0:nk * ncols].rearrange("p (k n) -> p k n", n=ncols)
        rd = [slot.sub(0), slot.sub(1)]
        for si, (src, off, w) in enumerate(segs):
            wr = [slot.sub(si)] if len(segs) == 2 else rd
            DMA('pool', view[:, :, off:off + w], src.rearrange("(k p) n -> p k n", p=128), [], wr, sem)
        return rd, view

    def rmsnorm(gname):
        for g, (g0, gn) in enumerate(GROUPS):
            ps = P.next_ps()
            for c in range(KC):
                sq, _ = sqr.next()
                ACT(sq[:, 0:gn], xT[:, c, g0:g0 + gn], AF.Square, R=[xs(g)], W=[sq])
                MM(ps[:, 0:gn], cb('ones'), sq[:, 0:gn], start=(c == 0), stop=(c == KC - 1), R=[cB, sq], W=[ps])
            rs, _ = rsr.next()
            ACT(rs[:, 0:gn], ps[:, 0:gn], AF.Ln, R=[ps], W=[rs], bias=EPS, scale=1.0 / D)
            ACT(rs[:, 0:gn], rs[:, 0:gn], AF.Exp, R=[rs], W=[rs], scale=-0.5)
            for c in range(KC):
                STT('dve', hT[:, c, g0:g0 + gn], xT[:, c, g0:g0 + gn], cf(gname, c, 1), rs[:, 0:gn],
                    ALU.mult, ALU.mult, R=[xs(g), rs, cF], W=[hs(g)])

    class RR:
        ccr = ssr = swr = None

    def rope_alloc():
        RR.ccr = Ring("cc", 2, [32, 512], F32)
        RR.ssr = Ring("ss", 2, [32, 512], F32)
        RR.swr = Ring("sw", 2, [32, 512], F32)

    def proj_norm_rope(v, slot, gname, sink_a, sink_b):
        st = [None] * len(GROUPS)

        def part1(g):
            g0, gn = GROUPS[g]
            ps = P.next_ps()
            for kc in range(KC):
                MM(ps[:, 0:gn], v[:, kc, 0:128], hT[:, kc, g0:g0 + gn], start=(kc == 0), stop=(kc == KC - 1), R=[*slot, hs(g)], W=[ps])
            sq, _ = sqr.next()
            ACT(sq[:, 0:gn], ps[:, 0:gn], AF.Square, R=[ps], W=[sq])
            pss = P.next_ps()
            MM(pss[:, 0:gn], cb('ones'), sq[:, 0:gn], R=[cB, sq], W=[pss])
            rs, _ = rsr.next()
            ACT(rs[:, 0:gn], pss[:, 0:gn], AF.Ln, R=[pss], W=[rs], bias=EPS, scale=1.0 / 128)
            ACT(rs[:, 0:gn], rs[:, 0:gn], AF.Exp, R=[rs], W=[rs], scale=-0.5)
            cs, csem = RR.ccr.next()
            DMA('sp', cs[0:32, 0:gn], rope[0:32, 0, g0:g0 + gn], [], [cs], csem)
            st[g] = (ps, rs, cs)
            P.held.add(ps)

        def part2a(g):
            g0, gn = GROUPS[g]
            ps, rs, cs = st[g]
            kn, _ = tmpA.next()
            STT('dve', kn[:, 0:gn], ps[:, 0:gn], cf(gname, 0, 1), rs[:, 0:gn], ALU.mult, ALU.mult, R=[ps, rs, cF], W=[kn])
            P.held.discard(ps)
            sw, swsem = RR.swr.next()
            DMA('sp', sw[0:16, 0:gn], kn[16:32, 0:gn], [kn], [sw.sub(0)], swsem)
            DMA('sp', sw[16:32, 0:gn], kn[0:16, 0:gn], [kn], [sw.sub(1)], swsem)
            knc, _ = tmpA.next()
            TT('pool', knc[0:32, 0:gn], kn[0:32, 0:gn], cs[0:32, 0:gn], ALU.mult, R=[kn, cs], W=[knc])
            ss, ssem = RR.ssr.next()
            DMA('sp', ss[0:32, 0:gn], rope[0:32, 1, g0:g0 + gn], [], [ss], ssem)
            st[g] = (kn, knc, sw, ss)
            sink_a(g, kn)

        def part2b(g):
            g0, gn = GROUPS[g]
            kn, knc, sw, cs = st[g]
            TT('pool', sw[0:32, 0:gn], sw[0:32, 0:gn], cs[0:32, 0:gn], ALU.mult, R=[sw.sub(0), sw.sub(1), cs], W=[sw.sub(0), sw.sub(1)])
            sink_b(g, knc, sw, [sw.sub(0), sw.sub(1)])

        ng = len(GROUPS)
        part1(0)
        for g in range(ng):
            if g + 1 < ng:
                part1(g + 1)
            part2a(g)
            if g >= 1:
                part2b(g - 1)
        part2b(ng - 1)

    def load_x():
        with P.scope():
            stg = Ring("xstg", 2, [128, D], F32)
            for ti, (t0, tn) in enumerate(TILES):
                g = grp_of(t0)
                s, sem = stg.next()
                DMA('sp', s[0:tn, :], xin[t0:t0 + tn, :], [], [s], sem)
                for hb in range(2):
                    ps = P.next_ps()
                    for j in range(4):
                        c = hb * 4 + j
                        TR(ps[:, j * 128:j * 128 + tn], s[0:tn, c * 128:(c + 1) * 128], cf('identF', 0, tn, rows=tn), R=[s, cF], W=[ps])
                    src = ps.t[:, :].rearrange("p (a b) -> p a b", b=128)[:, :, 0:tn]
                    CP('act' if hb == 0 else 'dve', xT[:, hb * 4:hb * 4 + 4, t0:t0 + tn], src, R=[ps], W=[xs(g)])

    def store_y():
        with P.scope():
            stg = Ring("ystg", 2, [128, D], F32)
            for ti, (t0, tn) in enumerate(TILES):
                g = grp_of(t0)
                s, sem = stg.next()
                for hb in range(2):
                    ps = P.next_ps()
                    for j in range(4):
                        c = hb * 4 + j
                        TR(ps[0:tn, j * 128:(j + 1) * 128], xT[:, c, t0:t0 + tn], cf('identF'), R=[xs(g), cF], W=[ps])
                    CP('act' if hb == 0 else 'dve', s[0:tn, hb * 512:(hb + 1) * 512], ps[0:tn, :], R=[ps], W=[s])
                DMA('sp', y[t0:t0 + tn, :], s[0:tn, :], [s], [], sem)

    def proj_add_x(Wd, k0, nk):
        for nb in range(4):
            slot, v = load_w([(Wd[k0:k0 + nk * 128, nb * 256:(nb + 1) * 256], 0, 256)], nk, 256)
            for hh in range(2):
                n = nb * 2 + hh
                for g, (g0, gn) in enumerate(GROUPS):
                    ps = P.next_ps()
                    for kc in range(nk):
                        MM(ps[:, 0:gn], v[:, kc, hh * 128:(hh + 1) * 128], PH.G[:, kc, g0:g0 + gn], start=(kc == 0), stop=(kc == nk - 1),
                           R=[*slot, gs(g)], W=[ps])
                    TT('dve', xT[:, n, g0:g0 + gn], xT[:, n, g0:g0 + gn], ps[:, 0:gn], ALU.add, R=[ps, xs(g)], W=[xs(g)])

    def ffn(l):
        rmsnorm(f'f_norm{l}')
        Win = f_w_in[l]
        Wout = f_w_out[l]
        j0 = 0
        for nj in (4, 4, 4, 4, 3, 3):
            for jj in range(nj):
                j = j0 + jj
                slot, v = load_w([(Win[:, j * 128:(j + 1) * 128], 0, 128), (Win[:, FFN_H + j * 128:FFN_H + (j + 1) * 128], 128, 128)], KC, 256)
                for g, (g0, gn) in enumerate(GROUPS):
                    psa = P.next_ps()
                    psb = P.next_ps()
                    for kc in range(KC):
                        MM(psa[:, 0:gn], v[:, kc, 0:128], hT[:, kc, g0:g0 + gn], start=(kc == 0), stop=(kc == KC - 1), R=[*slot, hs(g)], W=[psa])
                    for kc in range(KC):
                        MM(psb[:, 0:gn], v[:, kc, 128:256], hT[:, kc, g0:g0 + gn], start=(kc == 0), stop=(kc == KC - 1), R=[*slot, hs(g)], W=[psb])
                    sa, _ = tmpA.next()
                    ACT(sa[:, 0:gn], psa[:, 0:gn], AF.Silu, R=[psa], W=[sa])
                    TT('dve', PH.G[:, jj, g0:g0 + gn], sa[:, 0:gn], psb[:, 0:gn], ALU.mult, R=[sa, psb], W=[gs(g)])
            proj_add_x(Wout, j0 * 128, nj)
            j0 += nj

    def ple(l):
        with P.scope():
            stg = Ring("pstg", 2, [128, 256], F32)
            for ti, (t0, tn) in enumerate(TILES):
                g = grp_of(t0)
                s, sem = stg.next()
                DMA('sp', s[0:tn, :], pin[l, t0:t0 + tn, :], [], [s], sem)
                ps = P.next_ps()
                for j in range(2):
                    TR(ps[:, j * 128:j * 128 + tn], s[0:tn, j * 128:(j + 1) * 128], cf('identF', 0, tn, rows=tn), R=[s, cF], W=[ps])
                src = ps.t[:, 0:256].rearrange("p (a b) -> p a b", b=128)[:, :, 0:tn]
                CP('act' if ti % 2 == 0 else 'dve', PH.G[:, 0:2, t0:t0 + tn], src, R=[ps], W=[gs(g)])
        rmsnorm(f'e_norm{l}')
        Wg = e_w_gate[l]
        Wp = e_w_proj[l]
        for nb in range(4):
            slot, v = load_w([(Wg[:, nb * 256:(nb + 1) * 256], 0, 256)], KC, 256)
            slot2, v2 = load_w([(Wp[:, nb * 256:(nb + 1) * 256], 0, 256)], 2, 256)
            for hh in range(2):
                n = nb * 2 + hh
                for g, (g0, gn) in enumerate(GROUPS):
                    psg = P.next_ps()
                    psp = P.next_ps()
                    for kc in range(KC):
                        MM(psg[:, 0:gn], v[:, kc, hh * 128:(hh + 1) * 128], hT[:, kc, g0:g0 + gn], start=(kc == 0), stop=(kc == KC - 1), R=[*slot, hs(g)], W=[psg])
                    for kc in range(2):
                        MM(psp[:, 0:gn], v2[:, kc, hh * 128:(hh + 1) * 128], PH.G[:, kc, g0:g0 + gn], start=(kc == 0), stop=(kc == 1), R=[*slot2, gs(g)], W=[psp])
                    sg, _ = tmpA.next()
                    ACT(sg[:, 0:gn], psg[:, 0:gn], AF.Sigmoid, R=[psg], W=[sg])
                    t2, _ = tmpB.next()
                    TT('dve', t2[:, 0:gn], sg[:, 0:gn], psp[:, 0:gn], ALU.mult, R=[sg, psp], W=[t2])
                    TT('dve', xT[:, n, g0:g0 + gn], xT[:, n, g0:g0 + gn], t2[:, 0:gn], ALU.add, R=[t2, xs(g)], W=[xs(g)])

    def gla(i):
        rmsnorm(f'a_norm{i}')
        Win = a_w_in[i]
        with P.scope():
            cG = P.sbuf("cG", [128, ncg], F32)
            DMA('sp', cG[:, :], cstg, [], [cG], 'cG')

            def cf2(name, c0=0, n=None, rows=128):
                o, w = cg_off[name]
                if n is None:
                    n = w - c0
                return cG[0:rows, o + c0:o + c0 + n]
            cGb = P.sbuf("cGb", [128, ncg], BF16)
            DMA('pool', cGb[:, :], cstg, [], [cGb], 'cGb')

            def cgb(name, c0=0, n=None, rows=128):
                o, w = cg_off[name]
                if n is None:
                    n = w - c0
                return cGb[0:rows, o + c0:o + c0 + n]
            spb = P.sbuf("spb", [128, 512], BF16)
            glr = P.sbuf("glr", [32, T], F32)
            wg = P.sbuf("wgksb", [32, 512], F32)
            zt = P.sbuf("zt", [128, 512], F32)
            Em = P.sbuf("Em", [128, 512], F32)
            Ee = P.sbuf("Ee", [128, 512], F32)
            kin = P.sbuf("kin", [128, 512], BF16)
            kend = P.sbuf("kend", [128, 512], BF16)
            sets = []
            for k_ in range(2):
                sets.append(dict(Ep=P.sbuf(f"Ep{k_}", [128, 512], F32), qin=P.sbuf(f"qin{k_}", [128, 512], BF16),
                                 sr=P.sbuf(f"sr{k_}", [128, 2, 512], F32), vtok=P.sbuf(f"vtok{k_}", [128, 4, 256], BF16),
                                 ktall=P.sbuf(f"ktall{k_}", [128, 4, 128], BF16), attall=P.sbuf(f"attall{k_}", [128, 4, 128], BF16)))
            Sall = P.sbuf("Sall", [128, 9, 256], BF16)
            S32s = [P.sbuf("S32a", [128, 256], F32), P.sbuf("S32b", [128, 256], F32)]
            s0r = Ring("s0", 3, [128, 256], F32)
            s0br = Ring("s0b", 3, [128, 256], BF16)
            snr = Ring("sn", 3, [128, 256], F32)
            vmr = Ring("vm", 2, [64, 256], BF16)

            MEMSET('dve', glr[:, :], 1.0, W=[glr])
            DMA('sp', wg[0:17, :], wgk[i], [], [wg], 'wgk')
            slot, v = load_w([(Win[:, 3072:3088], 0, 16)], KC, 16)
            for g, (g0, gn) in enumerate(GROUPS):
                ps = P.next_ps()
                for kc in range(KC):
                    MM(ps[0:16, 0:gn], v[:, kc, 0:16], hT[:, kc, g0:g0 + gn], start=(kc == 0), stop=(kc == KC - 1), R=[*slot, hs(g)], W=[ps])
                CP('act', glr[0:16, g0:g0 + gn], ps[0:16, 0:gn], R=[ps], W=[glr])

            for h in range(4):
                slotA, vA = load_w([(Win[:, h * 128:(h + 1) * 128], 0, 128), (Win[:, 512 + h * 128:512 + (h + 1) * 128], 128, 128)], KC, 256)
                slotB, vB = load_w([(Win[:, 1024 + h * 256:1024 + (h + 1) * 256], 0, 256)], KC, 256)
                slotC, vC = load_w([(Win[:, 2048 + h * 256:2048 + (h + 1) * 256], 0, 256)], KC, 256)
                spar = [0]
                MEMSET('dve', S32s[0][:, :], 0.0, W=[S32s[0]])
                MEMSET('dve', Sall[:, 0, :], 0.0, W=[Sall])
                gc0 = 2 * (h % 2)

                def stageAg1(g):
                    g0, gn = GROUPS[g]
                    B_ = sets[g % 2]
                    Ep, qin, sr, vtok, ktall, attall = B_['Ep'], B_['qin'], B_['sr'], B_['vtok'], B_['ktall'], B_['attall']
                    samp = (g == 4)
                    tiles = [(0, 64)] if samp else [(tt * 128, 128) for tt in range(4)]
                    tn = tiles[0][1]
                    tri = cgb('tri4Neg', 0, 64, rows=64) if samp else cgb('triNeg')
                    trir = cgb('tri4RevNeg', 0, 64, rows=64) if samp else cgb('triRevNeg')
                    nz = 128 * len(tiles)
                    psz = P.next_ps()
                    for ti, (o0, _) in enumerate(tiles):
                        MM(psz[0:tn, ti * 128:(ti + 1) * 128], glr[0:17, g0 + o0:g0 + o0 + tn], wg[0:17, h * 128:(h + 1) * 128], R=[glr, wg], W=[psz])
                    ACT(zt[0:tn, 0:nz], psz[0:tn, 0:nz], AF.Exp, R=[psz], W=[zt], scale=-1.0)
                    ACT(spb[0:tn, 0:nz], zt[0:tn, 0:nz], AF.Ln, R=[zt], W=[spb], bias=1.0)

                def stageAg2(g):
                    g0, gn = GROUPS[g]
                    B_ = sets[g % 2]
                    Ep, qin, sr, vtok, ktall, attall = B_['Ep'], B_['qin'], B_['sr'], B_['vtok'], B_['ktall'], B_['attall']
                    samp = (g == 4)
                    tiles = [(0, 64)] if samp else [(tt * 128, 128) for tt in range(4)]
                    tn = tiles[0][1]
                    tri = cgb('tri4Neg', 0, 64, rows=64) if samp else cgb('triNeg')
                    trir = cgb('tri4RevNeg', 0, 64, rows=64) if samp else cgb('triRevNeg')
                    nz = 128 * len(tiles)
                    psc = P.next_ps()
                    psc2 = P.next_ps()
                    for ti, (o0, _) in enumerate(tiles):
                        MM(psc[:, o0:o0 + tn], spb[0:tn, ti * 128:(ti + 1) * 128], tri, R=[spb, cGb], W=[psc])
                        MM(psc2[:, o0:o0 + tn], spb[0:tn, ti * 128:(ti + 1) * 128], trir, R=[spb, cGb], W=[psc2])
                    ACT(Ep[:, 0:gn], psc[:, 0:gn], AF.Exp, R=[psc], W=[Ep])
                    ACT(Em[:, 0:gn], psc[:, 0:gn], AF.Exp, R=[psc], W=[Em], scale=-1.0)
                    ACT(Ee[:, 0:gn], psc2[:, 0:gn], AF.Exp, R=[psc2], W=[Ee])

                def stageAr(g):
                    g0, gn = GROUPS[g]
                    B_ = sets[g % 2]
                    Ep, qin, sr, vtok, ktall, attall = B_['Ep'], B_['qin'], B_['sr'], B_['vtok'], B_['ktall'], B_['attall']
                    samp = (g == 4)
                    tiles = [(0, 64)] if samp else [(tt * 128, 128) for tt in range(4)]
                    tn = tiles[0][1]
                    tri = cgb('tri4Neg', 0, 64, rows=64) if samp else cgb('triNeg')
                    trir = cgb('tri4RevNeg', 0, 64, rows=64) if samp else cgb('triRevNeg')
                    nz = 128 * len(tiles)
                    psq = P.next_ps()
                    for kc in range(KC):
                        MM(psq[:, 0:gn], vA[:, kc, 0:128], hT[:, kc, g0:g0 + gn], start=(kc == 0), stop=(kc == KC - 1), R=[*slotA, hs(g)], W=[psq])
                    STT('dve', qin[:, 0:gn], psq[:, 0:gn], float(128 ** -0.5), Ep[:, 0:gn], ALU.mult, ALU.mult, R=[psq, Ep], W=[qin])
                    psk = P.next_ps()
                    for kc in range(KC):
                        MM(psk[:, 0:gn], vA[:, kc, 128:256], hT[:, kc, g0:g0 + gn], start=(kc == 0), stop=(kc == KC - 1), R=[*slotA, hs(g)], W=[psk])
                    TT('dve', kin[:, 0:gn], psk[:, 0:gn], Em[:, 0:gn], ALU.mult, R=[psk, Em], W=[kin])
                    TT('dve', kend[:, 0:gn], psk[:, 0:gn], Ee[:, 0:gn], ALU.mult, R=[psk, Ee], W=[kend])
                    for j in range(2):
                        psr = P.next_ps()
                        for kc in range(KC):
                            MM(psr[:, 0:gn], vC[:, kc, j * 128:(j + 1) * 128], hT[:, kc, g0:g0 + gn], start=(kc == 0), stop=(kc == KC - 1), R=[*slotC, hs(g)], W=[psr])
                        ACT(sr[:, j, 0:gn], psr[:, 0:gn], AF.Silu, R=[psr], W=[sr])
                    for pr in range((len(tiles) + 1) // 2):
                        psv = P.next_ps()
                        tl = tiles[2 * pr:2 * pr + 2]
                        for k2, (o0, _) in enumerate(tl):
                            for kc in range(KC):
                                MM(psv[0:tn, k2 * 256:(k2 + 1) * 256], hT[:, kc, g0 + o0:g0 + o0 + tn], vB[:, kc, 0:256], start=(kc == 0), stop=(kc == KC - 1),
                                   R=[*slotB, hs(g)], W=[psv])
                        CP('act', vtok[0:tn, 2 * pr:2 * pr + len(tl), :], psv.t[0:tn, 0:256 * len(tl)].rearrange("p (a b) -> p a b", b=256), R=[psv], W=[vtok])
                    pb = P.next_pb()
                    for ti, (o0, _) in enumerate(tiles):
                        TR(pb[0:tn, ti * 128:(ti + 1) * 128], kend[:, o0:o0 + tn], cb('identB'), R=[kend, cB], W=[pb])
                    CP('dve', ktall[0:tn, 0:len(tiles), :], pb.t[0:tn, 0:nz].rearrange("p (a b) -> p a b", b=128), R=[pb], W=[ktall])
                    psa = P.next_ps()
                    for ti, (o0, _) in enumerate(tiles):
                        MM(psa[0:tn, ti * 128:ti * 128 + tn], kin[:, o0:o0 + tn], qin[:, o0:o0 + tn], R=[kin, qin], W=[psa])
                    if not samp:
                        TT('dve', attall.t[:, :, :].rearrange("p a b -> p (a b)"), psa[:, 0:512], cf2('maskC4'), ALU.mult, R=[psa, cG], W=[attall])
                    else:
                        TT('dve', attall[0:64, 0, 0:64], psa[0:64, 0:64], cf2('mask4', 0, 64, rows=64), ALU.mult, R=[psa, cG], W=[attall])

                def stageB1(g):
                    B_ = sets[g % 2]
                    Ep, vtok, ktall = B_['Ep'], B_['vtok'], B_['ktall']
                    if g < 4:
                        for ti in range(4):
                            pS2 = [P.next_ps(), P.next_ps()]
                            MM(pS2[0][:, 0:256], ktall[0:64, ti, :], vtok[0:64, ti, :], R=[ktall, vtok], W=[pS2[0]])
                            MM(pS2[1][:, 0:256], ktall[64:128, ti, :], vtok[64:128, ti, :], R=[ktall, vtok], W=[pS2[1]])
                            for hc in range(2):
                                col = ti * 128 + hc * 64 + 63
                                src, dst = S32s[spar[0]], S32s[1 - spar[0]]
                                STT('dve', dst[:, :], src[:, :], Ep[:, col:col + 1], pS2[hc][:, 0:256], ALU.mult, ALU.add, R=[pS2[hc], src, Ep], W=[dst])
                                spar[0] = 1 - spar[0]
                                CP('act', Sall[:, 2 * ti + hc + 1, :], dst[:, :], R=[dst], W=[Sall])
                        if g == 3:
                            DMA('sp', gsp[i, h], S32s[spar[0]][:, :], [S32s[spar[0]]], [], 'gsp')

                def stageB2(g):
                    g0, gn = GROUPS[g]
                    B_ = sets[g % 2]
                    Ep, qin, sr, vtok, ktall, attall = B_['Ep'], B_['qin'], B_['sr'], B_['vtok'], B_['ktall'], B_['attall']
                    samp = (g == 4)
                    pso = [P.next_ps(), P.next_ps()]
                    if not samp:
                        for ti in range(4):
                            o0 = ti * 128
                            for j in range(2):
                                MM(pso[j][:, o0:o0 + 128], vtok[:, ti, j * 128:(j + 1) * 128], attall[:, ti, :], start=True, stop=False, R=[vtok, attall], W=[pso[j]])
                                MM(pso[j][:, o0:o0 + 64], Sall[:, 2 * ti, j * 128:(j + 1) * 128], qin[:, o0:o0 + 64], start=False, stop=False, R=[Sall, qin], W=[pso[j]])
                                MM(pso[j][:, o0 + 64:o0 + 128], Sall[:, 2 * ti + 1, j * 128:(j + 1) * 128], qin[:, o0 + 64:o0 + 128], start=False, stop=True,
                                   R=[Sall, qin], W=[pso[j]])
                        if g < 3:
                            CP('act', Sall[:, 0, :], Sall[:, 8, :], R=[Sall], W=[Sall])
                    else:
                        for j in range(2):
                            MM(pso[j][:, 0:64], vtok[0:64, 0, j * 128:(j + 1) * 128], attall[0:64, 0, 0:64], start=True, stop=False, R=[vtok, attall], W=[pso[j]])
                        for b in range(NSEQ):
                            s0, s0sem = s0r.next()
                            DMA('sp', s0[:, :], sgla[i, b, h], [], [s0], s0sem)
                            s0b, _ = s0br.next()
                            CP('act', s0b[:, :], s0[:, :], R=[s0], W=[s0b])
                            for j in range(2):
                                MM(pso[j][:, 4 * b:4 * b + 4], s0b[:, j * 128:(j + 1) * 128], qin[:, 4 * b:4 * b + 4], start=False, stop=(b == NSEQ - 1),
                                   R=[s0b, qin], W=[pso[j]])
                            vm, _ = vmr.next()
                            TSC('dve', vm[0:64, :], vtok[0:64, 0, :], cf2('rowmask', b, 1, rows=64), None, ALU.mult, None, R=[vtok, cG], W=[vm])
                            pS = P.next_ps(exclude=pso)
                            MM(pS[:, 0:256], ktall[0:64, 0, :], vm[0:64, :], R=[ktall, vm], W=[pS])
                            sn, snsem = snr.next()
                            STT('dve', sn[:, :], s0[:, :], Ep[:, 4 * b + 3:4 * b + 4], pS[:, 0:256], ALU.mult, ALU.add, R=[s0, pS, Ep], W=[sn])
                            DMA('sp', gss[i, b, h], sn[:, :], [sn], [], snsem)
                    sqs = [sqr.next()[0], sqr.next()[0]]
                    for j in range(2):
                        ACT(sqs[j][:, 0:gn], pso[j][:, 0:gn], AF.Square, R=[pso[j]], W=[sqs[j]])
                    pss = P.next_ps(exclude=pso)
                    for j in range(2):
                        MM(pss[:, 0:gn], cb('ones'), sqs[j][:, 0:gn], start=(j == 0), stop=(j == 1), R=[cB, sqs[j]], W=[pss])
                    rso, _ = rsr.next()
                    ACT(rso[:, 0:gn], pss[:, 0:gn], AF.Ln, R=[pss], W=[rso], bias=EPS, scale=1.0 / 256)
                    ACT(rso[:, 0:gn], rso[:, 0:gn], AF.Exp, R=[rso], W=[rso], scale=-0.5)
                    for j in range(2):
                        TT('dve', sr[:, j, 0:gn], sr[:, j, 0:gn], rso[:, 0:gn], ALU.mult, R=[sr, rso], W=[sr])
                        STT('dve', PH.G[:, gc0 + j, g0:g0 + gn], pso[j][:, 0:gn], cf(f'a_onorm{i}', j, 1), sr[:, j, 0:gn], ALU.mult, ALU.mult,
                            R=[pso[j], sr, cF], W=[gs(g)])

                stageAg1(0)
                stageAg2(0)
                stageAr(0)
                for g in range(len(GROUPS)):
                    nxt = g + 1 < len(GROUPS)
                    if nxt:
                        stageAg1(g + 1)
                    stageB1(g)
                    if nxt:
                        stageAg2(g + 1)
                        stageAr(g + 1)
                    stageB2(g)
                if h % 2 == 1:
                    proj_add_x(a_w_o[i], (h // 2) * 512, 4)

    vdram = [[Trk(f"vdram{ti}_{hf}") for hf in range(2)] for ti in range(len(TILES))]

    def shared_kv(kT, v64):
        rmsnorm('kv_norm')
        with P.scope():
            kfr = Ring("kf", 3, [128, 512], F32)
            kst = Ring("kst", 2, [128, 128], F32)
            vst = Ring("vst", 2, [128, 256], F32)
            for c in range(4):
                slot, v = load_w([(w_kv[:, c * 128:(c + 1) * 128], 0, 128)], KC, 128)
                kfs = {}

                def ksink_a(g, kn, c=c):
                    g0, gn = GROUPS[g]
                    kf, _ = kfr.next()
                    kfs[g] = kf
                    CP('act', kf[:, 0:gn], kn[:, 0:gn], R=[kn], W=[kf.sub(0), kf.sub(1)])

                def ksink_b(g, knc, sw, swt, c=c):
                    g0, gn = GROUPS[g]
                    kf = kfs.pop(g)
                    TT('dve', kf[0:32, 0:gn], knc[0:32, 0:gn], sw[0:32, 0:gn], ALU.add, R=[knc, *swt], W=[kf.sub(0)])
                    kfall = [kf.sub(0), kf.sub(1)]
                    CP('act', kT[:, c, g0:g0 + gn], kf[:, 0:gn], R=kfall, W=[kT])
                    tiles = [(0, 64)] if g == 4 else [(tt * 128, 128) for tt in range(4)]
                    for (o0, tn) in tiles:
                        t0 = g0 + o0
                        pst = P.next_ps()
                        TR(pst[0:tn, 0:128], kf[:, o0:o0 + tn], cf('identF'), R=[*kfall, cF], W=[pst])
                        ks, ksem = kst.next()
                        CP('act', ks[0:tn, :], pst[0:tn, 0:128], R=[pst], W=[ks])
                        DMA('sp', kout[t0:t0 + tn, c * 128:(c + 1) * 128], ks[0:tn, :], [ks], [], ksem)
                proj_norm_rope(v, slot, 'k_norm', ksink_a, ksink_b)
            for half in range(2):
                slot, v = load_w([(w_kv[:, 512 + half * 256:512 + (half + 1) * 256], 0, 256)], KC, 256)
                for ti, (t0, tn) in enumerate(TILES):
                    g = grp_of(t0)
                    ps = P.next_ps()
                    for kc in range(KC):
                        MM(ps[0:tn, 0:256], hT[:, kc, t0:t0 + tn], v[:, kc, 0:256], start=(kc == 0), stop=(kc == KC - 1), R=[*slot, hs(g)], W=[ps])
                    vs, vsem = vst.next()
                    CP('act', vs[0:tn, :], ps[0:tn, 0:256], R=[ps], W=[vs])
                    DMA('sp', vout[t0:t0 + tn, half * 256:(half + 1) * 256], vs[0:tn, :], [vs], [vdram[ti][half]], vsem)
                    if ti == 16:
                        CP('dve', v64[0:64, half * 256:(half + 1) * 256], ps[0:64, 0:256], R=[ps], W=[v64])

    def battn(jl, kT, v64):
        rmsnorm(f'b_norm{jl}')
        Wq = b_w_q[jl]
        scale = float(128 ** -0.5)
        with P.scope():
            qts = [P.sbuf("qt", [128, T], BF16) for _ in range(2)]
            qt_alls = [[q_.sub((hl, g_)) for hl in ('hi', 'lo') for g_ in range(5)] for q_ in qts]
            qS = P.sbuf("qS", [128, 3, 8, TS], BF16)
            accO = P.sbuf("accO", [128, TP], F32)
            accL = P.sbuf("accL", [128, TP], F32)
            vtr = Ring("vt", 2, [128, 16, 128], BF16)
            wqr = Ring("wq", 2, [128, KC, 128], BF16)
            ptr = Ring("pt", 3, [128, 256], BF16)
            Kcr = Ring("Kc", 1, [128, 9, 128], BF16)
            Vcr = Ring("Vc", 2, [128, 9, 128], BF16)
            KTr = Ring("KTs", 2, [128, 9, 128], BF16)
            pfull = Ring("pfull", 2, [128, 72], BF16)
            pnr = Ring("pn", 2, [64, 24], BF16)
            rlr = Ring("rl", 2, [128, 8], F32)

            heads = [(c, gq, gi) for c in range(4) for gq in range(2) for gi in range(3)]
            prew = {}
            prev_ = {}

            def prefetch_w(i):
                c, gq, gi = heads[i]
                qcol = ((gi * 4 + c) * 2 + gq) * 128
                wq, wsem = wqr.next()
                DMA('pool', wq[:, :, :], Wq[:, qcol:qcol + 128].rearrange("(k p) n -> p k n", p=128), [], [wq], wsem)
                prew[i] = wq

            def prefetch_v(i):
                c, gq, gi = heads[i]
                win, dil = DILS[gi]
                vt, vsem = vtr.next()
                vsrc = vout[0:TP, c * 128:(c + 1) * 128]
                rd = [vdram[ti_][c // 2] for ti_ in range(16)]
                if dil == 1:
                    DMA('pool', vt[:, :, :], vsrc.rearrange("(b p) d -> p b d", p=128), rd, [vt], vsem)
                elif dil == 4:
                    DMA('pool', vt.t[:, :, :].rearrange("p (r b) d -> p r b d", b=4), vsrc.rearrange("(b p r) d -> p r b d", p=128, r=4), rd, [vt], vsem)
                else:
                    DMA('pool', vt[:, :, :], vsrc.rearrange("(p r) d -> p r d", r=16), rd, [vt], vsem)
                prev_[i] = vt

            def qproj(i):
                c, gq, gi = heads[i]
                s = 2 * c + gq
                qt = qts[i % 2]
                qt_all = qt_alls[i % 2]
                wq = prew.pop(i)

                def qsink_a(g, kn):
                    g0, gn = GROUPS[g]
                    CP('act', qt[:, g0:g0 + gn], kn[:, 0:gn], R=[kn], W=[qt.sub(('hi', g)), qt.sub(('lo', g))])

                def qsink_b(g, knc, sw, swt):
                    g0, gn = GROUPS[g]
                    TT('pool', qt[0:32, g0:g0 + gn], knc[0:32, 0:gn], sw[0:32, 0:gn], ALU.add, R=[knc, *swt], W=[qt.sub(('lo', g))])
                    if g == 4:
                        CP('pool', qS[:, gi, s, :], qt[:, TP:T], R=qt_all, W=[qS])
                proj_norm_rope(wq.t, [wq], f'b_q_norm{jl}', qsink_a, qsink_b)

            def units(i):
                c, gq, gi = heads[i]
                win, dil = DILS[gi]
                qt = qts[i % 2]
                qt_all = qt_alls[i % 2]
                vt = prev_.pop(i)

                def geom(u):
                    if dil == 1:
                        return 0, u
                    if dil == 4:
                        return u // 4, u % 4
                    return u, 0

                def cols(r, bb):
                    st = r + dil * 128 * bb
                    return slice(st, st + dil * 127 + 1, dil)

                def vidx(r, bb):
                    return bb if dil == 1 else (r * 4 + bb if dil == 4 else r)

                def s_stage(u):
                    r, b = geom(u)
                    blocks = ([b - 1] if b > 0 else []) + [b]
                    Wd = 128 * len(blocks)
                    pss = P.next_ps()
                    MM(pss[:, 0:Wd], cb('identB'), cb('negmask2', 256 - Wd, Wd), start=True, stop=False, R=[cB], W=[pss])
                    for bi, kb in enumerate(blocks):
                        MM(pss[:, bi * 128:(bi + 1) * 128], kT[:, c, cols(r, kb)], qt[:, cols(r, b)], start=False, stop=(bi == len(blocks) - 1),
                           R=[kT, *qt_all], W=[pss])
                    pt, _ = ptr.next()
                    ACT(pt[:, 0:Wd], pss[:, 0:Wd], AF.Exp, R=[pss], W=[pt], scale=scale)
                    return (r, b, blocks, pt)

                def pv_stage(stt):
                    r, b, blocks, pt = stt
                    qc = cols(r, b)
                    pso = P.next_ps()
                    psl = P.next_ps()
                    for bi, kb in enumerate(blocks):
                        MM(pso[:, 0:128], vt[:, vidx(r, kb), :], pt[:, bi * 128:(bi + 1) * 128], start=(bi == 0), stop=(bi == len(blocks) - 1), R=[vt, pt], W=[pso])
                    for bi, kb in enumerate(blocks):
                        MM(psl[:, 0:128], cb('ones'), pt[:, bi * 128:(bi + 1) * 128], start=(bi == 0), stop=(bi == len(blocks) - 1), R=[cB, pt], W=[psl])
                    if gi == 0:
                        CP('act', accO[:, qc], pso[:, 0:128], R=[pso], W=[accO])
                        CP('dve', accL[:, qc], psl[:, 0:128], R=[psl], W=[accL])
                    else:
                        TT('dve', accO[:, qc], accO[:, qc], pso[:, 0:128], ALU.add, R=[pso, accO], W=[accO])
                        TT('dve', accL[:, qc], accL[:, qc], psl[:, 0:128], ALU.add, R=[psl, accL], W=[accL])

                prev = s_stage(0)
                for u in range(1, 16):
                    cur = s_stage(u)
                    pv_stage(prev)
                    prev = cur
                pv_stage(prev)

            def sample_attn(c):
                stA = {}

                def a_stage(b):
                    Kc, ksem = Kcr.next()
                    Vc, vsem = Vcr.next()
                    for (Cd, Cb, sem) in ((ck, Kc, ksem), (cv, Vc, vsem)):
                        hs_ = Cd[b][:, c * 128:(c + 1) * 128]
                        DMA('pool', Cb[:, 0, :], hs_[1920:2048, :], [], [Cb.sub(0)], sem)
                        DMA('pool', Cb[:, 1:5, :], hs_[1536:2048, :].rearrange("(j t) d -> j t d", t=4), [], [Cb.sub(1)], sem)
                        DMA('pool', Cb[:, 5:9, :], hs_.rearrange("(j s) d -> j s d", s=16)[:, 0:4, :], [], [Cb.sub(2)], sem)
                    pb0 = P.next_pb()
                    pb1 = P.next_pb()
                    for tl in range(9):
                        pbx = pb0 if tl < 8 else pb1
                        o = (tl % 8) * 128
                        TR(pbx[:, o:o + 128], Kc[:, tl, :], cb('identB'), R=[Kc.sub(0), Kc.sub(1), Kc.sub(2), cB], W=[pbx])
                    KTs, _ = KTr.next()
                    CP('act', KTs.t[:, 0:8, :].rearrange("p a b -> p (a b)"), pb0[:, 0:1024], R=[pb0], W=[KTs])
                    CP('dve', KTs[:, 8, :], pb1[:, 0:128], R=[pb1], W=[KTs])
                    stA[b] = (KTs, Vc)

                def bc_stage(b):
                    KTs, Vc = stA.pop(b)
                    pss = P.next_ps()
                    MM(pss[:, 0:72], cb('identB'), cb('negfull'), start=True, stop=False, R=[cB], W=[pss])
                    MM(pss[:, 0:8], KTs[:, 0, :], qS[:, 0, 2 * c:2 * c + 2, 4 * b:4 * b + 4], start=False, stop=False, R=[KTs, qS], W=[pss])
                    for gi in (1, 2):
                        for tt in range(4):
                            tl = (1 if gi == 1 else 5) + tt
                            MM(pss[:, tl * 8 + tt:tl * 8 + tt + 5:4], KTs[:, tl, :], qS[:, gi, 2 * c:2 * c + 2, 4 * b + tt], start=False,
                               stop=(gi == 2 and tt == 3), R=[KTs, qS], W=[pss])
                    pf, _ = pfull.next()
                    ACT(pf[:, :], pss[:, 0:72], AF.Exp, R=[pss], W=[pf], scale=scale)
                    psn = P.next_ps()
                    MM(psn[0:64, 0:24], cb('identB', 0, 64, rows=64), cb('negNew', b * 24, 24, rows=64), start=True, stop=False, R=[cB], W=[psn])
                    MM(psn[0:64, 0:24], kT[:, c, TP:T], qS[:, :, 2 * c:2 * c + 2, 4 * b:4 * b + 4], start=False, stop=True, R=[kT, qS], W=[psn])
                    pn, _ = pnr.next()
                    ACT(pn[0:64, :], psn[0:64, 0:24], AF.Exp, R=[psn], W=[pn], scale=scale)
                    pso = P.next_ps()
                    psl = P.next_ps()
                    for tl in range(9):
                        MM(pso[:, 0:8], Vc[:, tl, :], pf[:, tl * 8:tl * 8 + 8], start=(tl == 0), stop=False, R=[Vc.sub(0), Vc.sub(1), Vc.sub(2), pf], W=[pso])
                    for gi in range(3):
                        MM(pso[:, 0:8], v64[0:64, c * 128:(c + 1) * 128], pn[0:64, gi * 8:gi * 8 + 8], start=False, stop=(gi == 2), R=[v64, pn], W=[pso])
                    for tl in range(9):
                        MM(psl[:, 0:8], cb('ones'), pf[:, tl * 8:tl * 8 + 8], start=(tl == 0), stop=False, R=[cB, pf], W=[psl])
                    for gi in range(3):
                        MM(psl[:, 0:8], cb('ones', rows=64), pn[0:64, gi * 8:gi * 8 + 8], start=False, stop=(gi == 2), R=[cB, pn], W=[psl])
                    rl, _ = rlr.next()
                    RCP(rl[:, :], psl[:, 0:8], R=[psl], W=[rl])
                    TT('dve', PH.G[:, 0:2, TP + 4 * b:TP + 4 * b + 4], pso.t[:, 0:8].rearrange("p (a b) -> p a b", b=4),
                       rl.t[:, :].rearrange("p (a b) -> p a b", b=4), ALU.mult, R=[pso, rl], W=[gs(4)])

                a_stage(0)
                for b in range(NSEQ):
                    if b + 1 < NSEQ:
                        a_stage(b + 1)
                    bc_stage(b)

            nh = len(heads)
            for i_ in range(2):
                prefetch_w(i_)
                prefetch_v(i_)
            qproj(0)
            for c in range(4):
                for gq in range(2):
                    for gi in range(3):
                        i_ = (c * 2 + gq) * 3 + gi
                        if i_ + 1 < nh:
                            qproj(i_ + 1)
                        units(i_)
                        if i_ + 2 < nh:
                            prefetch_w(i_ + 2)
                            prefetch_v(i_ + 2)
                    ACT(accL[:, :], accL[:, :], AF.Ln, R=[accL], W=[accL])
                    ACT(accL[:, :], accL[:, :], AF.Exp, R=[accL], W=[accL], scale=-1.0)
                    for g in range(4):
                        g0, gn = GROUPS[g]
                        TT('dve', PH.G[:, gq, g0:g0 + gn], accO[:, g0:g0 + gn], accL[:, g0:g0 + gn], ALU.mult, R=[accO, accL], W=[gs(g)])
                sample_attn(c)
                proj_add_x(b_w_o[jl], c * 256, 2)

    if steps is None:
        steps = ['gla0', 'ffn0', 'ple0', 'gla1', 'ffn1', 'ple1', 'kv', 'attn0', 'ffn2', 'ple2', 'attn1', 'ffn3', 'ple3']
    load_x()
    kT = v64 = None
    for st in steps:
        if st == 'kv':
            kT = P.sbuf("kT", [128, 4, T], BF16)
            v64 = P.sbuf("v64", [64, 512], BF16)
        with P.scope():
            if st.startswith('gla'):
                phase_alloc(4, 2048)
                gla(int(st[3:]))
            elif st.startswith('ffn'):
                phase_alloc(4, 2048)
                ffn(int(st[3:]))
            elif st.startswith('ple'):
                phase_alloc(4, 2048)
                ple(int(st[3:]))
            elif st == 'kv':
                phase_alloc(0, 2048)
                rope_alloc()
                shared_kv(kT, v64)
            elif st.startswith('attn'):
                phase_alloc(2, 512)
                rope_alloc()
                battn(int(st[4:]), kT, v64)
    store_y()
    P.emit()
    return nc


_CACHE = {}


def _prep(inputs, ncores=NCORES):
    cfc, cbc, cgc = _layout_consts(inputs)
    cstf = cfc.array()
    cstb = cbc.array()
    cstg = cgc.array()
    ropet = _rope_table()
    f32 = lambda a: np.ascontiguousarray(np.asarray(a, np.float32))
    wgk = f32(np.concatenate([inputs['a_w_gk_up'], inputs['a_b_gk'][:, None, :]], axis=1))
    shared = {k: f32(inputs[k]) for k in ['a_w_in', 'a_w_o', 'w_kv', 'b_w_q', 'b_w_o', 'f_w_in', 'f_w_out', 'e_w_proj', 'e_w_gate']}
    shared.update(wgk=wgk, cstf=cstf, cstb=cstb, cstg=cstg, rope=ropet)
    in_maps = []
    for i in range(ncores):
        sl = slice(NSEQ * i, NSEQ * (i + 1))
        m = dict(shared)
        m['xin'] = f32(np.concatenate([inputs['x_prompt'][i], inputs['x_sample'][sl].reshape(TS, D)], axis=0))
        m['pin'] = f32(np.concatenate([inputs['p_prompt'][:, i], inputs['p_sample'][:, sl].reshape(4, TS, 256)], axis=1))
        m['sgla'] = f32(inputs['state_gla'][:, sl])
        m['ck'] = f32(inputs['cache_k'][sl].reshape(NSEQ, 2048, 512))
        m['cv'] = f32(inputs['cache_v'][sl].reshape(NSEQ, 2048, 512))
        in_maps.append(m)
    return cfc, cbc, cgc, in_maps


def kernel(**inputs):
    cfc, cbc, cgc, in_maps = _prep(inputs)
    nc = build(cfc.off, cfc.n, cbc.off, cbc.n, cgc.off, cgc.n)
    res = run_bass_kernel_spmd(nc, in_maps, core_ids=list(range(NCORES)))
    r = res.results
    y_prompt = np.stack([r[i]['y'][0:TP] for i in range(NCORES)], 0)
    y_sample = np.concatenate([r[i]['y'][TP:T].reshape(NSEQ, 4, D) for i in range(NCORES)], 0)
    gsp = np.stack([r[i]['gsp'] for i in range(NCORES)], 1)
    gss = np.concatenate([r[i]['gss'] for i in range(NCORES)], 1)
    k_prompt = np.stack([r[i]['kout'][0:TP].reshape(TP, 4, 128) for i in range(NCORES)], 0)
    v_prompt = np.stack([r[i]['vout'][0:TP].reshape(TP, 4, 128) for i in range(NCORES)], 0)
    k_sample = np.concatenate([r[i]['kout'][TP:T].reshape(NSEQ, 4, 4, 128) for i in range(NCORES)], 0)
    v_sample = np.concatenate([r[i]['vout'][TP:T].reshape(NSEQ, 4, 4, 128) for i in range(NCORES)], 0)
    f = lambda a: np.ascontiguousarray(a, dtype=np.float32)
    return (f(y_prompt), f(y_sample), f(gsp), f(gss), f(k_prompt), f(v_prompt), f(k_sample), f(v_sample))
```

```python
import contextlib
import numpy as np
import concourse.bass as bass
import concourse.mybir as mybir
from concourse.bass_utils import run_bass_kernel_spmd

F32 = mybir.dt.float32
BF16 = mybir.dt.bfloat16
AF = mybir.ActivationFunctionType
ALU = mybir.AluOpType

NCORES = 8
D = 1024
KC = 8
TP = 2048
TS = 64
T = TP + TS
NSEQ = 16
EPS = 1e-6
FFN_H = 2816
GROUPS = [(0, 512), (512, 512), (1024, 512), (1536, 512), (2048, 64)]
TILES = [(i * 128, 128) for i in range(16)] + [(2048, 64)]
DILS = [(128, 1), (512, 4), (2048, 16)]


class Trk:
    __slots__ = ('name', 'w', 'r')

    def __init__(self, name='', init=None):
        self.name = name
        self.w = None
        self.r = dict(init) if init else {}


class Buf(Trk):
    def __init__(self, name, t, init=None):
        super().__init__(name, init)
        self.t = t
        self.subs = {}
        self.init = dict(init) if init else {}

    def sub(self, key):
        s = self.subs.get(key)
        if s is None:
            s = Trk(f"{self.name}.{key}", self.init)
            self.subs[key] = s
        return s

    def all(self):
        return [self] + list(self.subs.values())

    def __getitem__(self, k):
        return self.t[k]


class Prog:
    ENG = ['pe', 'act', 'dve', 'pool', 'sp']
    EPOCH = 16000
    DEPOCH = 1000

    def __init__(self, nc):
        self.nc = nc
        self.es = contextlib.ExitStack()
        self.q = {e: [] for e in self.ENG}
        self.cnt = {e: 0 for e in self.ENG}
        self.seen = {e: {} for e in self.ENG}
        self.dcnt = {}
        self.fin = {}
        self.freed = {}
        self.scopes = []
        self.psf = []
        self.psb = []
        self.ipf = 0
        self.ipb = 0
        self.held = set()

    def _merge_freed(self, b):
        for t in b.all():
            deps = dict(t.r)
            if t.w is not None:
                deps[t.w[0]] = max(deps.get(t.w[0], 0), t.w[1])
            for k, v in deps.items():
                if self.freed.get(k, 0) < v:
                    self.freed[k] = v

    @contextlib.contextmanager
    def scope(self):
        es = contextlib.ExitStack()
        bufs = []
        self.scopes.append((es, bufs))
        try:
            yield
        finally:
            self.scopes.pop()
            for b in bufs:
                self._merge_freed(b)
            es.close()

    def sbuf(self, name, shape, dt):
        es, bufs = self.scopes[-1] if self.scopes else (self.es, [])
        self.nalloc = getattr(self, 'nalloc', 0) + 1
        name = f"{name}_{self.nalloc}"
        t = es.enter_context(self.nc.sbuf_tensor(name, list(shape), dt))
        b = Buf(name, t, self.freed)
        bufs.append(b)
        return b

    def psum(self, name, shape, dt):
        t = self.es.enter_context(self.nc.psum_tensor(name, list(shape), dt))
        return Buf(name, t)

    def next_ps(self, exclude=()):
        while True:
            b = self.psf[self.ipf % len(self.psf)]
            self.ipf += 1
            if b not in exclude and b not in self.held:
                return b

    def next_pb(self):
        b = self.psb[self.ipb % len(self.psb)]
        self.ipb += 1
        return b

    def op(self, e, fn, reads=(), writes=(), dsem=None):
        deps = {}

        def add(k, v):
            if deps.get(k, 0) < v:
                deps[k] = v
        for t in reads:
            if t.w is not None:
                add(*t.w)
        for t in writes:
            if t.w is not None:
                add(*t.w)
            for k, v in t.r.items():
                add(k, v)
        waits = []
        seen = self.seen[e]
        for k, v in deps.items():
            if e == 'pe' and k.startswith('pe#'):
                continue
            if seen.get(k, 0) >= v:
                continue
            seen[k] = v
            waits.append((k, v))
        if dsem is None:
            self.cnt[e] += 1
            ep = (self.cnt[e] - 1) // self.EPOCH
            me = (f"{e}#{ep}", self.cnt[e] - ep * self.EPOCH)
            self.fin[me[0]] = (me[1], 1)
        else:
            c = self.dcnt.get(dsem, 0) + 1
            self.dcnt[dsem] = c
            ep = (c - 1) // self.DEPOCH
            me = (f"{dsem}#{ep}", 16 * (c - ep * self.DEPOCH))
            self.fin[me[0]] = (me[1], 16)
        for t in reads:
            if t.r.get(me[0], 0) < me[1]:
                t.r[me[0]] = me[1]
        for t in writes:
            t.w = me
            t.r = {}
        self.q[e].append((waits, fn, me))

    def emit(self):
        nc = self.nc
        sems = {k: self.es.enter_context(nc.semaphore("s_" + k.replace('#', '_'))) for k in self.fin}
        finals = [(k, v[0]) for k, v in self.fin.items()]

        def replay(e, eng):
            for waits, fn, me in self.q[e]:
                for k, v in waits:
                    eng.wait_ge(sems[k], v)
                ins = fn(eng)
                ins.then_inc(sems[me[0]], self.fin[me[0]][1])
            if e == 'sp':
                for k, v in finals:
                    eng.wait_ge(sems[k], v)
        with nc.Block() as block:
            @block.tensor
            def _(eng):
                replay('pe', eng)

            @block.scalar
            def _(eng):
                replay('act', eng)

            @block.vector
            def _(eng):
                replay('dve', eng)

            @block.gpsimd
            def _(eng):
                replay('pool', eng)

            @block.sync
            def _(eng):
                replay('sp', eng)
        self.es.close()


def _const_blocks():
    f = {}
    b = {}
    gl = {}
    i128 = np.arange(128)
    f['identF'] = np.eye(128, dtype=np.float32)
    s, t = np.meshgrid(i128, i128, indexing='ij')
    same64 = (s // 64 == t // 64) & (s <= t)
    gl['triNeg'] = np.where(same64, -1.0 / 16.0, 0.0).astype(np.float32)
    gl['maskC'] = same64.astype(np.float32)
    same4 = (s // 4 == t // 4) & (s <= t)
    gl['tri4Neg'] = np.where(same4, -1.0 / 16.0, 0.0).astype(np.float32)[:, :64]
    gl['triRevNeg'] = np.where((s // 64 == t // 64) & (s > t), -1.0 / 16.0, 0.0).astype(np.float32)
    gl['tri4RevNeg'] = np.where((s // 4 == t // 4) & (s > t), -1.0 / 16.0, 0.0).astype(np.float32)[:, :64]
    gl['maskC4'] = np.tile(same64.astype(np.float32), (1, 4))
    gl['mask4'] = same4.astype(np.float32)[:, :64]
    rm = np.zeros((128, 16), np.float32)
    for p in range(64):
        rm[p, p // 4] = 1.0
    gl['rowmask'] = rm
    b['ones'] = np.ones((128, 128), np.float32)
    b['identB'] = np.eye(128, dtype=np.float32)
    NEG = -30000.0
    nm2 = np.zeros((128, 256), np.float32)
    nm2[:, 0:128] = np.where(s >= t, 0.0, NEG)
    nm2[:, 128:256] = np.where(s <= t, 0.0, NEG)
    b['negmask2'] = nm2
    nf = np.full((128, 72), NEG, np.float32)
    for gq in range(2):
        for tt in range(4):
            nf[tt:, 0 * 8 + gq * 4 + tt] = 0.0
            nf[:, (1 + tt) * 8 + gq * 4 + tt] = 0.0
            nf[:, (5 + tt) * 8 + gq * 4 + tt] = 0.0
    b['negfull'] = nf
    nn = np.full((128, 16, 24), NEG, np.float32)
    for bb in range(16):
        for t2 in range(4):
            p = 4 * bb + t2
            for g in range(3):
                for gq in range(2):
                    for tt in range(4):
                        ok = (t2 <= tt) if g == 0 else (t2 == tt)
                        if ok:
                            nn[p, bb, g * 8 + gq * 4 + tt] = 0.0
    b['negNew'] = nn.reshape(128, 16 * 24)
    return f, b, gl


def _rope_table():
    half = 16
    freqs = np.power(np.float32(500000.0), -np.arange(half, dtype=np.float32) * np.float32(2.0) / np.float32(32.0)).astype(np.float32)
    pos = np.concatenate([np.arange(TP, dtype=np.float32), np.tile(TP + np.arange(4, dtype=np.float32), NSEQ)]).astype(np.float32)
    ang = (pos[None, :] * freqs[:, None]).astype(np.float32)
    C = np.ones((128, T), np.float32)
    S = np.zeros((128, T), np.float32)
    C[0:16] = np.cos(ang)
    C[16:32] = np.cos(ang)
    S[0:16] = -np.sin(ang)
    S[16:32] = np.sin(ang)
    return np.ascontiguousarray(np.stack([C, S], axis=1))


def _fm(v, nchunk):
    return np.ascontiguousarray(np.asarray(v, np.float32).reshape(nchunk, 128).T)


class _Cols:
    def __init__(self):
        self.off = {}
        self.n = 0
        self.parts = []

    def add(self, name, arr):
        arr = np.asarray(arr, np.float32)
        if arr.shape[0] < 128:
            arr = np.concatenate([arr, np.zeros((128 - arr.shape[0],) + arr.shape[1:], np.float32)], 0)
        self.off[name] = (self.n, arr.shape[1])
        self.n += arr.shape[1]
        self.parts.append(arr)

    def array(self):
        return np.ascontiguousarray(np.concatenate(self.parts, axis=1))


def _layout_consts(inputs):
    f, b, gl = _const_blocks()
    cf = _Cols()
    for k, v in f.items():
        cf.add(k, v)
    cg = _Cols()
    for k, v in gl.items():
        cg.add(k, v)
    for i in range(2):
        cf.add(f'a_norm{i}', _fm(inputs['a_norm'][i], 8))
        cf.add(f'b_norm{i}', _fm(inputs['b_norm'][i], 8))
        cf.add(f'a_onorm{i}', _fm(inputs['a_onorm'][i], 2))
        cf.add(f'b_q_norm{i}', _fm(inputs['b_q_norm'][i], 1))
    for i in range(4):
        cf.add(f'f_norm{i}', _fm(inputs['f_norm'][i], 8))
        cf.add(f'e_norm{i}', _fm(inputs['e_norm'][i], 8))
    cf.add('kv_norm', _fm(inputs['kv_norm'], 8))
    cf.add('k_norm', _fm(inputs['k_norm'], 1))
    cb = _Cols()
    for k, v in b.items():
        cb.add(k, v)
    return cf, cb, cg


def build(cf_off, ncf, cb_off, ncb, cg_off, ncg, steps=None, debug=False):
    nc = bass.Bass("TRN2", target_bir_lowering=False)

    def din(name, shape):
        return nc.dram_tensor(name, list(shape), F32, kind="ExternalInput").ap()

    def dout(name, shape):
        return nc.dram_tensor(name, list(shape), F32, kind="ExternalOutput").ap()

    xin = din("xin", [T, D])
    pin = din("pin", [4, T, 256])
    sgla = din("sgla", [2, NSEQ, 4, 128, 256])
    ck = din("ck", [NSEQ, 2048, 512])
    cv = din("cv", [NSEQ, 2048, 512])
    a_w_in = din("a_w_in", [2, D, 3088])
    a_w_o = din("a_w_o", [2, D, D])
    w_kv = din("w_kv", [D, D])
    b_w_q = din("b_w_q", [2, D, 3072])
    b_w_o = din("b_w_o", [2, D, D])
    f_w_in = din("f_w_in", [4, D, 2 * FFN_H])
    f_w_out = din("f_w_out", [4, FFN_H, D])
    e_w_proj = din("e_w_proj", [4, 256, D])
    e_w_gate = din("e_w_gate", [4, D, D])
    wgk = din("wgk", [2, 17, 512])
    cstf = din("cstf", [128, ncf])
    cstb = din("cstb", [128, ncb])
    cstg = din("cstg", [128, ncg])
    rope = din("rope", [128, 2, T])

    y = dout("y", [T, D])
    gsp = dout("gsp", [2, 4, 128, 256])
    gss = dout("gss", [2, NSEQ, 4, 128, 256])
    kout = dout("kout", [T, 512])
    vout = dout("vout", [T, 512])

    P = Prog(nc)
    for i in range(6):
        P.psf.append(P.psum(f"psf{i}", [128, 512], F32))
    for i in range(2):
        P.psb.append(P.psum(f"psb{i}", [128, 1024], BF16))

    def MM(out, lhsT, rhs, start=True, stop=True, R=(), W=()):
        P.op('pe', lambda e: e.matmul(out, lhsT, rhs, start=start, stop=stop), R, W)

    def TR(out, in_, ident, R=(), W=()):
        P.op('pe', lambda e: e.transpose(out, in_, ident), R, W)

    def ACT(out, in_, func, R=(), W=(), bias=None, scale=None):
        kw = {}
        if bias is not None:
            kw['bias'] = bias
        if scale is not None:
            kw['scale'] = scale
        P.op('act', lambda e: e.activation(out=out, in_=in_, func=func, **kw), R, W)

    def TT(eng, out, in0, in1, op, R=(), W=()):
        P.op(eng, lambda e: e.tensor_tensor(out=out, in0=in0, in1=in1, op=op), R, W)

    def TSC(eng, out, in0, s1, s2, op0, op1, R=(), W=()):
        if s2 is None:
            P.op(eng, lambda e: e.tensor_scalar(out=out, in0=in0, scalar1=s1, scalar2=None, op0=op0), R, W)
        else:
            P.op(eng, lambda e: e.tensor_scalar(out=out, in0=in0, scalar1=s1, scalar2=s2, op0=op0, op1=op1), R, W)

    def STT(eng, out, in0, scalar, in1, op0, op1, R=(), W=()):
        P.op(eng, lambda e: e.scalar_tensor_tensor(out=out, in0=in0, scalar=scalar, in1=in1, op0=op0, op1=op1), R, W)

    def CP(eng, out, in_, R=(), W=()):
        if eng == 'act':
            P.op('act', lambda e: e.copy(out=out, in_=in_), R, W)
        else:
            P.op(eng, lambda e: e.tensor_copy(out=out, in_=in_), R, W)

    def RCP(out, in_, R=(), W=()):
        P.op('dve', lambda e: e.reciprocal(out=out, in_=in_), R, W)

    def MEMSET(eng, ap, val, W=()):
        P.op(eng, lambda e: e.memset(ap, val), (), W)

    def DMA(eng, out, in_, R, W, dsem):
        P.op(eng, lambda e: e.dma_start(out=out, in_=in_), R, W, dsem=dsem)

    class Ring:
        def __init__(self, name, n, shape, dt):
            self.bufs = [P.sbuf(f"{name}{i}", shape, dt) for i in range(n)]
            self.i = 0
            self.name = name

        def next(self):
            k = self.i % len(self.bufs)
            self.i += 1
            return self.bufs[k], f"{self.name}{k}"

    xT = P.sbuf("xT", [128, KC, T], F32)
    hT = P.sbuf("hT", [128, KC, T], BF16)
    cF = P.sbuf("cF", [128, ncf], F32)
    cB = P.sbuf("cB", [128, ncb], BF16)

    class PH:
        G = None
        wring = None

    def phase_alloc(gch, wsz, nw=3):
        PH.G = P.sbuf("G", [128, gch, T], BF16) if gch else None
        PH.wring = Ring("w", nw, [128, wsz], BF16)
    tmpA = Ring("tmpA", 3, [128, 512], F32)
    tmpB = tmpA
    sqr = Ring("sq", 2, [128, 512], BF16)
    rsr = Ring("rs", 2, [128, 512], F32)

    def cf(name, c0=0, n=None, rows=128):
        o, w = cf_off[name]
        if n is None:
            n = w - c0
        return cF[0:rows, o + c0:o + c0 + n]

    def cb(name, c0=0, n=None, rows=128):
        o, w = cb_off[name]
        if n is None:
            n = w - c0
        return cB[0:rows, o + c0:o + c0 + n]

    DMA('sp', cF[:, :], cstf, [], [cF], 'cF')
    DMA('pool', cB[:, :], cstb, [], [cB], 'cB')

    def grp_of(t0):
        return min(t0 // 512, 4)

    def xs(g):
        return xT.sub(g)

    def hs(g):
        return hT.sub(g)

    def gs(g):
        return PH.G.sub(g)

    ALLG = list(range(5))

    def load_w(segs, nk, ncols):
        slot, sem = PH.wring.next()
        view = slot.t[:, 0:nk * ncols].rearrange("p (k n) -> p k n", n=ncols)
        rd = [slot.sub(0), slot.sub(1)]
        for si, (src, off, w) in enumerate(segs):
            wr = [slot.sub(si)] if len(segs) == 2 else rd
            DMA('pool', view[:, :, off:off + w], src.rearrange("(k p) n -> p k n", p=128), [], wr, sem)
        return rd, view

    def rmsnorm(gname):
        for g, (g0, gn) in enumerate(GROUPS):
            ps = P.next_ps()
            for c in range(KC):
                sq, _ = sqr.next()
                ACT(sq[:, 0:gn], xT[:, c, g0:g0 + gn], AF.Square, R=[xs(g)], W=[sq])
                MM(ps[:, 0:gn], cb('ones'), sq[:, 0:gn], start=(c == 0), stop=(c == KC - 1), R=[cB, sq], W=[ps])
            rs, _ = rsr.next()
            ACT(rs[:, 0:gn], ps[:, 0:gn], AF.Ln, R=[ps], W=[rs], bias=EPS, scale=1.0 / D)
            ACT(rs[:, 0:gn], rs[:, 0:gn], AF.Exp, R=[rs], W=[rs], scale=-0.5)
            for c in range(KC):
                STT('dve', hT[:, c, g0:g0 + gn], xT[:, c, g0:g0 + gn], cf(gname, c, 1), rs[:, 0:gn],
                    ALU.mult, ALU.mult, R=[xs(g), rs, cF], W=[hs(g)])

    class RR:
        ccr = ssr = swr = None

    def rope_alloc():
        RR.ccr = Ring("cc", 2, [32, 512], F32)
        RR.ssr = Ring("ss", 2, [32, 512], F32)
        RR.swr = Ring("sw", 2, [32, 512], F32)

    def proj_norm_rope(v, slot, gname, sink_a, sink_b):
        st = [None] * len(GROUPS)

        def part1(g):
            g0, gn = GROUPS[g]
            ps = P.next_ps()
            for kc in range(KC):
                MM(ps[:, 0:gn], v[:, kc, 0:128], hT[:, kc, g0:g0 + gn], start=(kc == 0), stop=(kc == KC - 1), R=[*slot, hs(g)], W=[ps])
            sq, _ = sqr.next()
            ACT(sq[:, 0:gn], ps[:, 0:gn], AF.Square, R=[ps], W=[sq])
            st[g] = (ps, sq)
            P.held.add(ps)

        def part1b(g):
            g0, gn = GROUPS[g]
            ps, sq = st[g]
            pss = P.next_ps()
            MM(pss[:, 0:gn], cb('ones'), sq[:, 0:gn], R=[cB, sq], W=[pss])
            rs, _ = rsr.next()
            ACT(rs[:, 0:gn], pss[:, 0:gn], AF.Ln, R=[pss], W=[rs], bias=EPS, scale=1.0 / 128)
            ACT(rs[:, 0:gn], rs[:, 0:gn], AF.Exp, R=[rs], W=[rs], scale=-0.5)
            cs, csem = RR.ccr.next()
            DMA('sp', cs[0:32, 0:gn], rope[0:32, 0, g0:g0 + gn], [], [cs], csem)
            st[g] = (ps, rs, cs)

        def part2a(g):
            g0, gn = GROUPS[g]
            ps, rs, cs = st[g]
            kn, _ = tmpA.next()
            STT('dve', kn[:, 0:gn], ps[:, 0:gn], cf(gname, 0, 1), rs[:, 0:gn], ALU.mult, ALU.mult, R=[ps, rs, cF], W=[kn])
            P.held.discard(ps)
            sw, swsem = RR.swr.next()
            DMA('sp', sw[0:16, 0:gn], kn[16:32, 0:gn], [kn], [sw.sub(0)], swsem)
            DMA('sp', sw[16:32, 0:gn], kn[0:16, 0:gn], [kn], [sw.sub(1)], swsem)
            knc, _ = tmpA.next()
            TT('pool', knc[0:32, 0:gn], kn[0:32, 0:gn], cs[0:32, 0:gn], ALU.mult, R=[kn, cs], W=[knc])
            ss, ssem = RR.ssr.next()
            DMA('sp', ss[0:32, 0:gn], rope[0:32, 1, g0:g0 + gn], [], [ss], ssem)
            st[g] = (kn, knc, sw, ss)
            sink_a(g, kn)

        def part2b(g):
            g0, gn = GROUPS[g]
            kn, knc, sw, cs = st[g]
            TT('pool', sw[0:32, 0:gn], sw[0:32, 0:gn], cs[0:32, 0:gn], ALU.mult, R=[sw.sub(0), sw.sub(1), cs], W=[sw.sub(0), sw.sub(1)])
            sink_b(g, knc, sw, [sw.sub(0), sw.sub(1)])

        ng = len(GROUPS)
        part1(0)
        for g in range(ng):
            if g + 1 < ng:
                part1(g + 1)
            part1b(g)
            part2a(g)
            if g >= 1:
                part2b(g - 1)
        part2b(ng - 1)

    def load_x():
        with P.scope():
            stg = Ring("xstg", 2, [128, D], F32)
            for ti, (t0, tn) in enumerate(TILES):
                g = grp_of(t0)
                s, sem = stg.next()
                DMA('sp', s[0:tn, :], xin[t0:t0 + tn, :], [], [s], sem)
                for hb in range(2):
                    ps = P.next_ps()
                    for j in range(4):
                        c = hb * 4 + j
                        TR(ps[:, j * 128:j * 128 + tn], s[0:tn, c * 128:(c + 1) * 128], cf('identF', 0, tn, rows=tn), R=[s, cF], W=[ps])
                    src = ps.t[:, :].rearrange("p (a b) -> p a b", b=128)[:, :, 0:tn]
                    CP('act' if hb == 0 else 'dve', xT[:, hb * 4:hb * 4 + 4, t0:t0 + tn], src, R=[ps], W=[xs(g)])

    def store_y():
        with P.scope():
            stg = Ring("ystg", 2, [128, D], F32)
            for ti, (t0, tn) in enumerate(TILES):
                g = grp_of(t0)
                s, sem = stg.next()
                for hb in range(2):
                    ps = P.next_ps()
                    for j in range(4):
                        c = hb * 4 + j
                        TR(ps[0:tn, j * 128:(j + 1) * 128], xT[:, c, t0:t0 + tn], cf('identF'), R=[xs(g), cF], W=[ps])
                    CP('act' if hb == 0 else 'dve', s[0:tn, hb * 512:(hb + 1) * 512], ps[0:tn, :], R=[ps], W=[s])
                DMA('sp', y[t0:t0 + tn, :], s[0:tn, :], [s], [], sem)

    def proj_add_x(Wd, k0, nk):
        for nb in range(4):
            slot, v = load_w([(Wd[k0:k0 + nk * 128, nb * 256:(nb + 1) * 256], 0, 256)], nk, 256)
            for hh in range(2):
                n = nb * 2 + hh
                for g, (g0, gn) in enumerate(GROUPS):
                    ps = P.next_ps()
                    for kc in range(nk):
                        MM(ps[:, 0:gn], v[:, kc, hh * 128:(hh + 1) * 128], PH.G[:, kc, g0:g0 + gn], start=(kc == 0), stop=(kc == nk - 1),
                           R=[*slot, gs(g)], W=[ps])
                    TT('dve', xT[:, n, g0:g0 + gn], xT[:, n, g0:g0 + gn], ps[:, 0:gn], ALU.add, R=[ps, xs(g)], W=[xs(g)])

    def ffn(l):
        rmsnorm(f'f_norm{l}')
        Win = f_w_in[l]
        Wout = f_w_out[l]
        j0 = 0
        for nj in (4, 4, 4, 4, 3, 3):
            for jj in range(nj):
                j = j0 + jj
                slot, v = load_w([(Win[:, j * 128:(j + 1) * 128], 0, 128), (Win[:, FFN_H + j * 128:FFN_H + (j + 1) * 128], 128, 128)], KC, 256)
                for g, (g0, gn) in enumerate(GROUPS):
                    psa = P.next_ps()
                    psb = P.next_ps()
                    for kc in range(KC):
                        MM(psa[:, 0:gn], v[:, kc, 0:128], hT[:, kc, g0:g0 + gn], start=(kc == 0), stop=(kc == KC - 1), R=[*slot, hs(g)], W=[psa])
                    for kc in range(KC):
                        MM(psb[:, 0:gn], v[:, kc, 128:256], hT[:, kc, g0:g0 + gn], start=(kc == 0), stop=(kc == KC - 1), R=[*slot, hs(g)], W=[psb])
                    sa, _ = tmpA.next()
                    ACT(sa[:, 0:gn], psa[:, 0:gn], AF.Silu, R=[psa], W=[sa])
                    TT('dve', PH.G[:, jj, g0:g0 + gn], sa[:, 0:gn], psb[:, 0:gn], ALU.mult, R=[sa, psb], W=[gs(g)])
            proj_add_x(Wout, j0 * 128, nj)
            j0 += nj

    def ple(l):
        with P.scope():
            stg = Ring("pstg", 2, [128, 256], F32)
            for ti, (t0, tn) in enumerate(TILES):
                g = grp_of(t0)
                s, sem = stg.next()
                DMA('sp', s[0:tn, :], pin[l, t0:t0 + tn, :], [], [s], sem)
                ps = P.next_ps()
                for j in range(2):
                    TR(ps[:, j * 128:j * 128 + tn], s[0:tn, j * 128:(j + 1) * 128], cf('identF', 0, tn, rows=tn), R=[s, cF], W=[ps])
                src = ps.t[:, 0:256].rearrange("p (a b) -> p a b", b=128)[:, :, 0:tn]
                CP('act' if ti % 2 == 0 else 'dve', PH.G[:, 0:2, t0:t0 + tn], src, R=[ps], W=[gs(g)])
        rmsnorm(f'e_norm{l}')
        Wg = e_w_gate[l]
        Wp = e_w_proj[l]
        for nb in range(4):
            slot, v = load_w([(Wg[:, nb * 256:(nb + 1) * 256], 0, 256)], KC, 256)
            slot2, v2 = load_w([(Wp[:, nb * 256:(nb + 1) * 256], 0, 256)], 2, 256)
            for hh in range(2):
                n = nb * 2 + hh
                for g, (g0, gn) in enumerate(GROUPS):
                    psg = P.next_ps()
                    psp = P.next_ps()
                    for kc in range(KC):
                        MM(psg[:, 0:gn], v[:, kc, hh * 128:(hh + 1) * 128], hT[:, kc, g0:g0 + gn], start=(kc == 0), stop=(kc == KC - 1), R=[*slot, hs(g)], W=[psg])
                    for kc in range(2):
                        MM(psp[:, 0:gn], v2[:, kc, hh * 128:(hh + 1) * 128], PH.G[:, kc, g0:g0 + gn], start=(kc == 0), stop=(kc == 1), R=[*slot2, gs(g)], W=[psp])
                    sg, _ = tmpA.next()
                    ACT(sg[:, 0:gn], psg[:, 0:gn], AF.Sigmoid, R=[psg], W=[sg])
                    t2, _ = tmpB.next()
                    TT('dve', t2[:, 0:gn], sg[:, 0:gn], psp[:, 0:gn], ALU.mult, R=[sg, psp], W=[t2])
                    TT('dve', xT[:, n, g0:g0 + gn], xT[:, n, g0:g0 + gn], t2[:, 0:gn], ALU.add, R=[t2, xs(g)], W=[xs(g)])

    def gla(i):
        rmsnorm(f'a_norm{i}')
        Win = a_w_in[i]
        with P.scope():
            cG = P.sbuf("cG", [128, ncg], F32)
            DMA('sp', cG[:, :], cstg, [], [cG], 'cG')

            def cf2(name, c0=0, n=None, rows=128):
                o, w = cg_off[name]
                if n is None:
                    n = w - c0
                return cG[0:rows, o + c0:o + c0 + n]
            cGb = P.sbuf("cGb", [128, ncg], BF16)
            DMA('pool', cGb[:, :], cstg, [], [cGb], 'cGb')

            def cgb(name, c0=0, n=None, rows=128):
                o, w = cg_off[name]
                if n is None:
                    n = w - c0
                return cGb[0:rows, o + c0:o + c0 + n]
            spb = P.sbuf("spb", [128, 512], BF16)
            glr = P.sbuf("glr", [32, T], F32)
            wg = P.sbuf("wgksb", [32, 512], F32)
            zt = P.sbuf("zt", [128, 512], F32)
            Em = P.sbuf("Em", [128, 512], F32)
            Ee = P.sbuf("Ee", [128, 512], F32)
            kin = P.sbuf("kin", [128, 512], BF16)
            kend = P.sbuf("kend", [128, 512], BF16)
            sets = []
            for k_ in range(2):
                sets.append(dict(Ep=P.sbuf(f"Ep{k_}", [128, 512], F32), qin=P.sbuf(f"qin{k_}", [128, 512], BF16),
                                 sr=P.sbuf(f"sr{k_}", [128, 2, 512], F32), vtok=P.sbuf(f"vtok{k_}", [128, 4, 256], BF16),
                                 ktall=P.sbuf(f"ktall{k_}", [128, 4, 128], BF16), attall=P.sbuf(f"attall{k_}", [128, 4, 128], BF16)))
            Sall = P.sbuf("Sall", [128, 9, 256], BF16)
            S32s = [P.sbuf("S32a", [128, 256], F32), P.sbuf("S32b", [128, 256], F32)]
            s0r = Ring("s0", 3, [128, 256], F32)
            s0br = Ring("s0b", 3, [128, 256], BF16)
            snr = Ring("sn", 3, [128, 256], F32)
            vmr = Ring("vm", 2, [64, 256], BF16)

            MEMSET('dve', glr[:, :], 1.0, W=[glr])
            DMA('sp', wg[0:17, :], wgk[i], [], [wg], 'wgk')
            slot, v = load_w([(Win[:, 3072:3088], 0, 16)], KC, 16)
            for g, (g0, gn) in enumerate(GROUPS):
                ps = P.next_ps()
                for kc in range(KC):
                    MM(ps[0:16, 0:gn], v[:, kc, 0:16], hT[:, kc, g0:g0 + gn], start=(kc == 0), stop=(kc == KC - 1), R=[*slot, hs(g)], W=[ps])
                CP('act', glr[0:16, g0:g0 + gn], ps[0:16, 0:gn], R=[ps], W=[glr])

            for h in range(4):
                slotA, vA = load_w([(Win[:, h * 128:(h + 1) * 128], 0, 128), (Win[:, 512 + h * 128:512 + (h + 1) * 128], 128, 128)], KC, 256)
                slotB, vB = load_w([(Win[:, 1024 + h * 256:1024 + (h + 1) * 256], 0, 256)], KC, 256)
                slotC, vC = load_w([(Win[:, 2048 + h * 256:2048 + (h + 1) * 256], 0, 256)], KC, 256)
                spar = [0]
                MEMSET('dve', S32s[0][:, :], 0.0, W=[S32s[0]])
                MEMSET('dve', Sall[:, 0, :], 0.0, W=[Sall])
                gc0 = 2 * (h % 2)

                def stageAg1(g):
                    g0, gn = GROUPS[g]
                    B_ = sets[g % 2]
                    Ep, qin, sr, vtok, ktall, attall = B_['Ep'], B_['qin'], B_['sr'], B_['vtok'], B_['ktall'], B_['attall']
                    samp = (g == 4)
                    tiles = [(0, 64)] if samp else [(tt * 128, 128) for tt in range(4)]
                    tn = tiles[0][1]
                    tri = cgb('tri4Neg', 0, 64, rows=64) if samp else cgb('triNeg')
                    trir = cgb('tri4RevNeg', 0, 64, rows=64) if samp else cgb('triRevNeg')
                    nz = 128 * len(tiles)
                    psz = P.next_ps()
                    for ti, (o0, _) in enumerate(tiles):
                        MM(psz[0:tn, ti * 128:(ti + 1) * 128], glr[0:17, g0 + o0:g0 + o0 + tn], wg[0:17, h * 128:(h + 1) * 128], R=[glr, wg], W=[psz])
                    ACT(zt[0:tn, 0:nz], psz[0:tn, 0:nz], AF.Exp, R=[psz], W=[zt], scale=-1.0)
                    ACT(spb[0:tn, 0:nz], zt[0:tn, 0:nz], AF.Ln, R=[zt], W=[spb], bias=1.0)

                def stageAg2(g):
                    g0, gn = GROUPS[g]
                    B_ = sets[g % 2]
                    Ep, qin, sr, vtok, ktall, attall = B_['Ep'], B_['qin'], B_['sr'], B_['vtok'], B_['ktall'], B_['attall']
                    samp = (g == 4)
                    tiles = [(0, 64)] if samp else [(tt * 128, 128) for tt in range(4)]
                    tn = tiles[0][1]
                    tri = cgb('tri4Neg', 0, 64, rows=64) if samp else cgb('triNeg')
                    trir = cgb('tri4RevNeg', 0, 64, rows=64) if samp else cgb('triRevNeg')
                    nz = 128 * len(tiles)
                    psc = P.next_ps()
                    psc2 = P.next_ps()
                    for ti, (o0, _) in enumerate(tiles):
                        MM(psc[:, o0:o0 + tn], spb[0:tn, ti * 128:(ti + 1) * 128], tri, R=[spb, cGb], W=[psc])
                        MM(psc2[:, o0:o0 + tn], spb[0:tn, ti * 128:(ti + 1) * 128], trir, R=[spb, cGb], W=[psc2])
                    ACT(Ep[:, 0:gn], psc[:, 0:gn], AF.Exp, R=[psc], W=[Ep])
                    ACT(Em[:, 0:gn], psc[:, 0:gn], AF.Exp, R=[psc], W=[Em], scale=-1.0)
                    ACT(Ee[:, 0:gn], psc2[:, 0:gn], AF.Exp, R=[psc2], W=[Ee])

                def stageAr(g):
                    g0, gn = GROUPS[g]
                    B_ = sets[g % 2]
                    Ep, qin, sr, vtok, ktall, attall = B_['Ep'], B_['qin'], B_['sr'], B_['vtok'], B_['ktall'], B_['attall']
                    samp = (g == 4)
                    tiles = [(0, 64)] if samp else [(tt * 128, 128) for tt in range(4)]
                    tn = tiles[0][1]
                    tri = cgb('tri4Neg', 0, 64, rows=64) if samp else cgb('triNeg')
                    trir = cgb('tri4RevNeg', 0, 64, rows=64) if samp else cgb('triRevNeg')
                    nz = 128 * len(tiles)
                    psq = P.next_ps()
                    for kc in range(KC):
                        MM(psq[:, 0:gn], vA[:, kc, 0:128], hT[:, kc, g0:g0 + gn], start=(kc == 0), stop=(kc == KC - 1), R=[*slotA, hs(g)], W=[psq])
                    STT('dve', qin[:, 0:gn], psq[:, 0:gn], float(128 ** -0.5), Ep[:, 0:gn], ALU.mult, ALU.mult, R=[psq, Ep], W=[qin])
                    psk = P.next_ps()
                    for kc in range(KC):
                        MM(psk[:, 0:gn], vA[:, kc, 128:256], hT[:, kc, g0:g0 + gn], start=(kc == 0), stop=(kc == KC - 1), R=[*slotA, hs(g)], W=[psk])
                    TT('dve', kin[:, 0:gn], psk[:, 0:gn], Em[:, 0:gn], ALU.mult, R=[psk, Em], W=[kin])
                    TT('dve', kend[:, 0:gn], psk[:, 0:gn], Ee[:, 0:gn], ALU.mult, R=[psk, Ee], W=[kend])
                    for j in range(2):
                        psr = P.next_ps()
                        for kc in range(KC):
                            MM(psr[:, 0:gn], vC[:, kc, j * 128:(j + 1) * 128], hT[:, kc, g0:g0 + gn], start=(kc == 0), stop=(kc == KC - 1), R=[*slotC, hs(g)], W=[psr])
                        ACT(sr[:, j, 0:gn], psr[:, 0:gn], AF.Silu, R=[psr], W=[sr])
                    for pr in range((len(tiles) + 1) // 2):
                        psv = P.next_ps()
                        tl = tiles[2 * pr:2 * pr + 2]
                        for k2, (o0, _) in enumerate(tl):
                            for kc in range(KC):
                                MM(psv[0:tn, k2 * 256:(k2 + 1) * 256], hT[:, kc, g0 + o0:g0 + o0 + tn], vB[:, kc, 0:256], start=(kc == 0), stop=(kc == KC - 1),
                                   R=[*slotB, hs(g)], W=[psv])
                        CP('act', vtok[0:tn, 2 * pr:2 * pr + len(tl), :], psv.t[0:tn, 0:256 * len(tl)].rearrange("p (a b) -> p a b", b=256), R=[psv], W=[vtok])
                    pb = P.next_pb()
                    for ti, (o0, _) in enumerate(tiles):
                        TR(pb[0:tn, ti * 128:(ti + 1) * 128], kend[:, o0:o0 + tn], cb('identB'), R=[kend, cB], W=[pb])
                    CP('dve', ktall[0:tn, 0:len(tiles), :], pb.t[0:tn, 0:nz].rearrange("p (a b) -> p a b", b=128), R=[pb], W=[ktall])
                    psa = P.next_ps()
                    for ti, (o0, _) in enumerate(tiles):
                        MM(psa[0:tn, ti * 128:ti * 128 + tn], kin[:, o0:o0 + tn], qin[:, o0:o0 + tn], R=[kin, qin], W=[psa])
                    if not samp:
                        TT('dve', attall.t[:, :, :].rearrange("p a b -> p (a b)"), psa[:, 0:512], cf2('maskC4'), ALU.mult, R=[psa, cG], W=[attall])
                    else:
                        TT('dve', attall[0:64, 0, 0:64], psa[0:64, 0:64], cf2('mask4', 0, 64, rows=64), ALU.mult, R=[psa, cG], W=[attall])

                def stageB1(g):
                    B_ = sets[g % 2]
                    Ep, vtok, ktall = B_['Ep'], B_['vtok'], B_['ktall']
                    if g < 4:
                        for ti in range(4):
                            pS2 = [P.next_ps(), P.next_ps()]
                            MM(pS2[0][:, 0:256], ktall[0:64, ti, :], vtok[0:64, ti, :], R=[ktall, vtok], W=[pS2[0]])
                            MM(pS2[1][:, 0:256], ktall[64:128, ti, :], vtok[64:128, ti, :], R=[ktall, vtok], W=[pS2[1]])
                            for hc in range(2):
                                col = ti * 128 + hc * 64 + 63
                                src, dst = S32s[spar[0]], S32s[1 - spar[0]]
                                STT('dve', dst[:, :], src[:, :], Ep[:, col:col + 1], pS2[hc][:, 0:256], ALU.mult, ALU.add, R=[pS2[hc], src, Ep], W=[dst])
                                spar[0] = 1 - spar[0]
                                CP('act', Sall[:, 2 * ti + hc + 1, :], dst[:, :], R=[dst], W=[Sall])
                        if g == 3:
                            DMA('sp', gsp[i, h], S32s[spar[0]][:, :], [S32s[spar[0]]], [], 'gsp')

                def stageB2(g):
                    g0, gn = GROUPS[g]
                    B_ = sets[g % 2]
                    Ep, qin, sr, vtok, ktall, attall = B_['Ep'], B_['qin'], B_['sr'], B_['vtok'], B_['ktall'], B_['attall']
                    samp = (g == 4)
                    pso = [P.next_ps(), P.next_ps()]
                    if not samp:
                        for ti in range(4):
                            o0 = ti * 128
                            for j in range(2):
                                MM(pso[j][:, o0:o0 + 128], vtok[:, ti, j * 128:(j + 1) * 128], attall[:, ti, :], start=True, stop=False, R=[vtok, attall], W=[pso[j]])
                                MM(pso[j][:, o0:o0 + 64], Sall[:, 2 * ti, j * 128:(j + 1) * 128], qin[:, o0:o0 + 64], start=False, stop=False, R=[Sall, qin], W=[pso[j]])
                                MM(pso[j][:, o0 + 64:o0 + 128], Sall[:, 2 * ti + 1, j * 128:(j + 1) * 128], qin[:, o0 + 64:o0 + 128], start=False, stop=True,
                                   R=[Sall, qin], W=[pso[j]])
                        if g < 3:
                            CP('act', Sall[:, 0, :], Sall[:, 8, :], R=[Sall], W=[Sall])
                    else:
                        for j in range(2):
                            MM(pso[j][:, 0:64], vtok[0:64, 0, j * 128:(j + 1) * 128], attall[0:64, 0, 0:64], start=True, stop=False, R=[vtok, attall], W=[pso[j]])
                        for b in range(NSEQ):
                            s0, s0sem = s0r.next()
                            DMA('sp', s0[:, :], sgla[i, b, h], [], [s0], s0sem)
                            s0b, _ = s0br.next()
                            CP('act', s0b[:, :], s0[:, :], R=[s0], W=[s0b])
                            for j in range(2):
                                MM(pso[j][:, 4 * b:4 * b + 4], s0b[:, j * 128:(j + 1) * 128], qin[:, 4 * b:4 * b + 4], start=False, stop=(b == NSEQ - 1),
                                   R=[s0b, qin], W=[pso[j]])
                            vm, _ = vmr.next()
                            TSC('dve', vm[0:64, :], vtok[0:64, 0, :], cf2('rowmask', b, 1, rows=64), None, ALU.mult, None, R=[vtok, cG], W=[vm])
                            pS = P.next_ps(exclude=pso)
                            MM(pS[:, 0:256], ktall[0:64, 0, :], vm[0:64, :], R=[ktall, vm], W=[pS])
                            sn, snsem = snr.next()
                            STT('dve', sn[:, :], s0[:, :], Ep[:, 4 * b + 3:4 * b + 4], pS[:, 0:256], ALU.mult, ALU.add, R=[s0, pS, Ep], W=[sn])
                            DMA('sp', gss[i, b, h], sn[:, :], [sn], [], snsem)
                    sqs = [sqr.next()[0], sqr.next()[0]]
                    for j in range(2):
                        ACT(sqs[j][:, 0:gn], pso[j][:, 0:gn], AF.Square, R=[pso[j]], W=[sqs[j]])
                    pss = P.next_ps(exclude=pso)
                    for j in range(2):
                        MM(pss[:, 0:gn], cb('ones'), sqs[j][:, 0:gn], start=(j == 0), stop=(j == 1), R=[cB, sqs[j]], W=[pss])
                    rso, _ = rsr.next()
                    ACT(rso[:, 0:gn], pss[:, 0:gn], AF.Ln, R=[pss], W=[rso], bias=EPS, scale=1.0 / 256)
                    ACT(rso[:, 0:gn], rso[:, 0:gn], AF.Exp, R=[rso], W=[rso], scale=-0.5)
                    for j in range(2):
                        TT('dve', sr[:, j, 0:gn], sr[:, j, 0:gn], rso[:, 0:gn], ALU.mult, R=[sr, rso], W=[sr])
                        STT('dve', PH.G[:, gc0 + j, g0:g0 + gn], pso[j][:, 0:gn], cf(f'a_onorm{i}', j, 1), sr[:, j, 0:gn], ALU.mult, ALU.mult,
                            R=[pso[j], sr, cF], W=[gs(g)])

                stageAg1(0)
                stageAg2(0)
                stageAr(0)
                for g in range(len(GROUPS)):
                    nxt = g + 1 < len(GROUPS)
                    if nxt:
                        stageAg1(g + 1)
                    stageB1(g)
                    if nxt:
                        stageAg2(g + 1)
                        stageAr(g + 1)
                    stageB2(g)
                if h % 2 == 1:
                    proj_add_x(a_w_o[i], (h // 2) * 512, 4)

    vdram = [[Trk(f"vdram{ti}_{hf}") for hf in range(2)] for ti in range(len(TILES))]

    def shared_kv(kT, v64):
        rmsnorm('kv_norm')
        with P.scope():
            kfr = Ring("kf", 3, [128, 512], F32)
            kst = Ring("kst", 2, [128, 128], F32)
            vst = Ring("vst", 2, [128, 256], F32)
            for c in range(4):
                slot, v = load_w([(w_kv[:, c * 128:(c + 1) * 128], 0, 128)], KC, 128)
                kfs = {}

                def ksink_a(g, kn, c=c):
                    g0, gn = GROUPS[g]
                    kf, _ = kfr.next()
                    kfs[g] = kf
                    CP('act', kf[:, 0:gn], kn[:, 0:gn], R=[kn], W=[kf.sub(0), kf.sub(1)])

                def ksink_b(g, knc, sw, swt, c=c):
                    g0, gn = GROUPS[g]
                    kf = kfs.pop(g)
                    TT('dve', kf[0:32, 0:gn], knc[0:32, 0:gn], sw[0:32, 0:gn], ALU.add, R=[knc, *swt], W=[kf.sub(0)])
                    kfall = [kf.sub(0), kf.sub(1)]
                    CP('act', kT[:, c, g0:g0 + gn], kf[:, 0:gn], R=kfall, W=[kT])
                    tiles = [(0, 64)] if g == 4 else [(tt * 128, 128) for tt in range(4)]
                    for (o0, tn) in tiles:
                        t0 = g0 + o0
                        pst = P.next_ps()
                        TR(pst[0:tn, 0:128], kf[:, o0:o0 + tn], cf('identF'), R=[*kfall, cF], W=[pst])
                        ks, ksem = kst.next()
                        CP('act', ks[0:tn, :], pst[0:tn, 0:128], R=[pst], W=[ks])
                        DMA('sp', kout[t0:t0 + tn, c * 128:(c + 1) * 128], ks[0:tn, :], [ks], [], ksem)
                proj_norm_rope(v, slot, 'k_norm', ksink_a, ksink_b)
            for half in range(2):
                slot, v = load_w([(w_kv[:, 512 + half * 256:512 + (half + 1) * 256], 0, 256)], KC, 256)
                for ti, (t0, tn) in enumerate(TILES):
                    g = grp_of(t0)
                    ps = P.next_ps()
                    for kc in range(KC):
                        MM(ps[0:tn, 0:256], hT[:, kc, t0:t0 + tn], v[:, kc, 0:256], start=(kc == 0), stop=(kc == KC - 1), R=[*slot, hs(g)], W=[ps])
                    vs, vsem = vst.next()
                    CP('act', vs[0:tn, :], ps[0:tn, 0:256], R=[ps], W=[vs])
                    DMA('sp', vout[t0:t0 + tn, half * 256:(half + 1) * 256], vs[0:tn, :], [vs], [vdram[ti][half]], vsem)
                    if ti == 16:
                        CP('dve', v64[0:64, half * 256:(half + 1) * 256], ps[0:64, 0:256], R=[ps], W=[v64])

    def battn(jl, kT, v64):
        rmsnorm(f'b_norm{jl}')
        Wq = b_w_q[jl]
        scale = float(128 ** -0.5)
        with P.scope():
            qts = [P.sbuf("qt", [128, T], BF16) for _ in range(2)]
            qt_alls = [[q_.sub((hl, g_)) for hl in ('hi', 'lo') for g_ in range(5)] for q_ in qts]
            qS = P.sbuf("qS", [128, 3, 8, TS], BF16)
            accO = P.sbuf("accO", [128, TP], F32)
            accL = P.sbuf("accL", [128, TP], F32)
            vtr = Ring("vt", 2, [128, 16, 128], BF16)
            wqr = Ring("wq", 2, [128, KC, 128], BF16)
            ptr = Ring("pt", 2, [128, 256], BF16)
            Kcr = Ring("Kc", 2, [128, 9, 128], BF16)
            Vcr = Ring("Vc", 2, [128, 9, 128], BF16)
            KTr = Ring("KTs", 2, [128, 9, 128], BF16)
            pfull = Ring("pfull", 2, [128, 72], BF16)
            pnr = Ring("pn", 2, [64, 24], BF16)
            rlr = Ring("rl", 2, [128, 8], F32)

            heads = [(c, gq, gi) for c in range(4) for gq in range(2) for gi in range(3)]
            prew = {}
            prev_ = {}

            def prefetch_w(i):
                c, gq, gi = heads[i]
                qcol = ((gi * 4 + c) * 2 + gq) * 128
                wq, wsem = wqr.next()
                DMA('pool', wq[:, :, :], Wq[:, qcol:qcol + 128].rearrange("(k p) n -> p k n", p=128), [], [wq], wsem)
                prew[i] = wq

            def prefetch_v(i):
                c, gq, gi = heads[i]
                win, dil = DILS[gi]
                vt, vsem = vtr.next()
                vsrc = vout[0:TP, c * 128:(c + 1) * 128]
                rd = [vdram[ti_][c // 2] for ti_ in range(16)]
                if dil == 1:
                    DMA('pool', vt[:, :, :], vsrc.rearrange("(b p) d -> p b d", p=128), rd, [vt], vsem)
                elif dil == 4:
                    DMA('pool', vt.t[:, :, :].rearrange("p (r b) d -> p r b d", b=4), vsrc.rearrange("(b p r) d -> p r b d", p=128, r=4), rd, [vt], vsem)
                else:
                    DMA('pool', vt[:, :, :], vsrc.rearrange("(p r) d -> p r d", r=16), rd, [vt], vsem)
                prev_[i] = vt

            def qproj(i):
                c, gq, gi = heads[i]
                s = 2 * c + gq
                qt = qts[i % 2]
                qt_all = qt_alls[i % 2]
                wq = prew.pop(i)

                def qsink_a(g, kn):
                    g0, gn = GROUPS[g]
                    CP('act', qt[:, g0:g0 + gn], kn[:, 0:gn], R=[kn], W=[qt.sub(('hi', g)), qt.sub(('lo', g))])

                def qsink_b(g, knc, sw, swt):
                    g0, gn = GROUPS[g]
                    TT('pool', qt[0:32, g0:g0 + gn], knc[0:32, 0:gn], sw[0:32, 0:gn], ALU.add, R=[knc, *swt], W=[qt.sub(('lo', g))])
                    if g == 4:
                        CP('pool', qS[:, gi, s, :], qt[:, TP:T], R=qt_all, W=[qS])
                proj_norm_rope(wq.t, [wq], f'b_q_norm{jl}', qsink_a, qsink_b)

            def units(i):
                c, gq, gi = heads[i]
                win, dil = DILS[gi]
                qt = qts[i % 2]
                qt_all = qt_alls[i % 2]
                vt = prev_.pop(i)

                def geom(u):
                    if dil == 1:
                        return 0, u
                    if dil == 4:
                        return u // 4, u % 4
                    return u, 0

                def cols(r, bb):
                    st = r + dil * 128 * bb
                    return slice(st, st + dil * 127 + 1, dil)

                def vidx(r, bb):
                    return bb if dil == 1 else (r * 4 + bb if dil == 4 else r)

                def s_stage(u):
                    r, b = geom(u)
                    blocks = ([b - 1] if b > 0 else []) + [b]
                    Wd = 128 * len(blocks)
                    pss = P.next_ps()
                    MM(pss[:, 0:Wd], cb('identB'), cb('negmask2', 256 - Wd, Wd), start=True, stop=False, R=[cB], W=[pss])
                    for bi, kb in enumerate(blocks):
                        MM(pss[:, bi * 128:(bi + 1) * 128], kT[:, c, cols(r, kb)], qt[:, cols(r, b)], start=False, stop=(bi == len(blocks) - 1),
                           R=[kT, *qt_all], W=[pss])
                    pt, _ = ptr.next()
                    ACT(pt[:, 0:Wd], pss[:, 0:Wd], AF.Exp, R=[pss], W=[pt], scale=scale)
                    return (r, b, blocks, pt)

                def pv_stage(stt):
                    r, b, blocks, pt = stt
                    qc = cols(r, b)
                    pso = P.next_ps()
                    psl = P.next_ps()
                    for bi, kb in enumerate(blocks):
                        MM(pso[:, 0:128], vt[:, vidx(r, kb), :], pt[:, bi * 128:(bi + 1) * 128], start=(bi == 0), stop=(bi == len(blocks) - 1), R=[vt, pt], W=[pso])
                    for bi, kb in enumerate(blocks):
                        MM(psl[:, 0:128], cb('ones'), pt[:, bi * 128:(bi + 1) * 128], start=(bi == 0), stop=(bi == len(blocks) - 1), R=[cB, pt], W=[psl])
                    if gi == 0:
                        CP('act', accO[:, qc], pso[:, 0:128], R=[pso], W=[accO])
                        CP('dve', accL[:, qc], psl[:, 0:128], R=[psl], W=[accL])
                    else:
                        TT('dve', accO[:, qc], accO[:, qc], pso[:, 0:128], ALU.add, R=[pso, accO], W=[accO])
                        TT('dve', accL[:, qc], accL[:, qc], psl[:, 0:128], ALU.add, R=[psl, accL], W=[accL])

                prev = s_stage(0)
                for u in range(1, 16):
                    cur = s_stage(u)
                    pv_stage(prev)
                    prev = cur
                pv_stage(prev)

            def sample_attn(c):
                stA = {}

                def a_stage(b):
                    Kc, ksem = Kcr.next()
                    Vc, vsem = Vcr.next()
                    for (Cd, Cb, sem) in ((ck, Kc, ksem), (cv, Vc, vsem)):
                        hs_ = Cd[b][:, c * 128:(c + 1) * 128]
                        DMA('pool', Cb[:, 0, :], hs_[1920:2048, :], [], [Cb.sub(0)], sem)
                        DMA('pool', Cb[:, 1:5, :], hs_[1536:2048, :].rearrange("(j t) d -> j t d", t=4), [], [Cb.sub(1)], sem)
                        DMA('pool', Cb[:, 5:9, :], hs_.rearrange("(j s) d -> j s d", s=16)[:, 0:4, :], [], [Cb.sub(2)], sem)
                    pb0 = P.next_pb()
                    pb1 = P.next_pb()
                    for tl in range(9):
                        pbx = pb0 if tl < 8 else pb1
                        o = (tl % 8) * 128
                        TR(pbx[:, o:o + 128], Kc[:, tl, :], cb('identB'), R=[Kc.sub(0), Kc.sub(1), Kc.sub(2), cB], W=[pbx])
                    KTs, _ = KTr.next()
                    CP('act', KTs.t[:, 0:8, :].rearrange("p a b -> p (a b)"), pb0[:, 0:1024], R=[pb0], W=[KTs])
                    CP('dve', KTs[:, 8, :], pb1[:, 0:128], R=[pb1], W=[KTs])
                    stA[b] = (KTs, Vc)

                def bc_stage(b):
                    KTs, Vc = stA.pop(b)
                    pss = P.next_ps()
                    MM(pss[:, 0:72], cb('identB'), cb('negfull'), start=True, stop=False, R=[cB], W=[pss])
                    MM(pss[:, 0:8], KTs[:, 0, :], qS[:, 0, 2 * c:2 * c + 2, 4 * b:4 * b + 4], start=False, stop=False, R=[KTs, qS], W=[pss])
                    for gi in (1, 2):
                        for tt in range(4):
                            tl = (1 if gi == 1 else 5) + tt
                            MM(pss[:, tl * 8 + tt:tl * 8 + tt + 5:4], KTs[:, tl, :], qS[:, gi, 2 * c:2 * c + 2, 4 * b + tt], start=False,
                               stop=(gi == 2 and tt == 3), R=[KTs, qS], W=[pss])
                    pf, _ = pfull.next()
                    ACT(pf[:, :], pss[:, 0:72], AF.Exp, R=[pss], W=[pf], scale=scale)
                    psn = P.next_ps()
                    MM(psn[0:64, 0:24], cb('identB', 0, 64, rows=64), cb('negNew', b * 24, 24, rows=64), start=True, stop=False, R=[cB], W=[psn])
                    MM(psn[0:64, 0:24], kT[:, c, TP:T], qS[:, :, 2 * c:2 * c + 2, 4 * b:4 * b + 4], start=False, stop=True, R=[kT, qS], W=[psn])
                    pn, _ = pnr.next()
                    ACT(pn[0:64, :], psn[0:64, 0:24], AF.Exp, R=[psn], W=[pn], scale=scale)
                    pso = P.next_ps()
                    psl = P.next_ps()
                    for tl in range(9):
                        MM(pso[:, 0:8], Vc[:, tl, :], pf[:, tl * 8:tl * 8 + 8], start=(tl == 0), stop=False, R=[Vc.sub(0), Vc.sub(1), Vc.sub(2), pf], W=[pso])
                    for gi in range(3):
                        MM(pso[:, 0:8], v64[0:64, c * 128:(c + 1) * 128], pn[0:64, gi * 8:gi * 8 + 8], start=False, stop=(gi == 2), R=[v64, pn], W=[pso])
                    for tl in range(9):
                        MM(psl[:, 0:8], cb('ones'), pf[:, tl * 8:tl * 8 + 8], start=(tl == 0), stop=False, R=[cB, pf], W=[psl])
                    for gi in range(3):
                        MM(psl[:, 0:8], cb('ones', rows=64), pn[0:64, gi * 8:gi * 8 + 8], start=False, stop=(gi == 2), R=[cB, pn], W=[psl])
                    rl, _ = rlr.next()
                    RCP(rl[:, :], psl[:, 0:8], R=[psl], W=[rl])
                    TT('dve', PH.G[:, 0:2, TP + 4 * b:TP + 4 * b + 4], pso.t[:, 0:8].rearrange("p (a b) -> p a b", b=4),
                       rl.t[:, :].rearrange("p (a b) -> p a b", b=4), ALU.mult, R=[pso, rl], W=[gs(4)])

                a_stage(0)
                for b in range(NSEQ):
                    if b + 1 < NSEQ:
                        a_stage(b + 1)
                    bc_stage(b)

            nh = len(heads)
            for i_ in range(2):
                prefetch_w(i_)
                prefetch_v(i_)
            qproj(0)
            for c in range(4):
                for gq in range(2):
                    for gi in range(3):
                        i_ = (c * 2 + gq) * 3 + gi
                        if i_ + 1 < nh:
                            qproj(i_ + 1)
                        units(i_)
                        if i_ + 2 < nh:
                            prefetch_w(i_ + 2)
                            prefetch_v(i_ + 2)
                    ACT(accL[:, :], accL[:, :], AF.Ln, R=[accL], W=[accL])
                    ACT(accL[:, :], accL[:, :], AF.Exp, R=[accL], W=[accL], scale=-1.0)
                    for g in range(4):
                        g0, gn = GROUPS[g]
                        TT('dve', PH.G[:, gq, g0:g0 + gn], accO[:, g0:g0 + gn], accL[:, g0:g0 + gn], ALU.mult, R=[accO, accL], W=[gs(g)])
                sample_attn(c)
                proj_add_x(b_w_o[jl], c * 256, 2)

    if steps is None:
        steps = ['gla0', 'ffn0', 'ple0', 'gla1', 'ffn1', 'ple1', 'kv', 'attn0', 'ffn2', 'ple2', 'attn1', 'ffn3', 'ple3']
    load_x()
    kT = v64 = None
    for st in steps:
        if st == 'kv':
            kT = P.sbuf("kT", [128, 4, T], BF16)
            v64 = P.sbuf("v64", [64, 512], BF16)
        with P.scope():
            if st.startswith('gla'):
                phase_alloc(4, 2048)
                gla(int(st[3:]))
            elif st.startswith('ffn'):
                phase_alloc(4, 2048)
                ffn(int(st[3:]))
            elif st.startswith('ple'):
                phase_alloc(4, 2048)
                ple(int(st[3:]))
            elif st == 'kv':
                phase_alloc(0, 2048)
                rope_alloc()
                shared_kv(kT, v64)
            elif st.startswith('attn'):
                phase_alloc(2, 512, nw=2)
                rope_alloc()
                battn(int(st[4:]), kT, v64)
    store_y()
    P.emit()
    return nc


_CACHE = {}


def _prep(inputs, ncores=NCORES):
    cfc, cbc, cgc = _layout_consts(inputs)
    cstf = cfc.array()
    cstb = cbc.array()
    cstg = cgc.array()
    ropet = _rope_table()
    f32 = lambda a: np.ascontiguousarray(np.asarray(a, np.float32))
    wgk = f32(np.concatenate([inputs['a_w_gk_up'], inputs['a_b_gk'][:, None, :]], axis=1))
    shared = {k: f32(inputs[k]) for k in ['a_w_in', 'a_w_o', 'w_kv', 'b_w_q', 'b_w_o', 'f_w_in', 'f_w_out', 'e_w_proj', 'e_w_gate']}
    shared.update(wgk=wgk, cstf=cstf, cstb=cstb, cstg=cstg, rope=ropet)
    in_maps = []
    for i in range(ncores):
        sl = slice(NSEQ * i, NSEQ * (i + 1))
        m = dict(shared)
        m['xin'] = f32(np.concatenate([inputs['x_prompt'][i], inputs['x_sample'][sl].reshape(TS, D)], axis=0))
        m['pin'] = f32(np.concatenate([inputs['p_prompt'][:, i], inputs['p_sample'][:, sl].reshape(4, TS, 256)], axis=1))
        m['sgla'] = f32(inputs['state_gla'][:, sl])
        m['ck'] = f32(inputs['cache_k'][sl].reshape(NSEQ, 2048, 512))
        m['cv'] = f32(inputs['cache_v'][sl].reshape(NSEQ, 2048, 512))
        in_maps.append(m)
    return cfc, cbc, cgc, in_maps


def kernel(**inputs):
    cfc, cbc, cgc, in_maps = _prep(inputs)
    nc = build(cfc.off, cfc.n, cbc.off, cbc.n, cgc.off, cgc.n)
    res = run_bass_kernel_spmd(nc, in_maps, core_ids=list(range(NCORES)))
    r = res.results
    y_prompt = np.stack([r[i]['y'][0:TP] for i in range(NCORES)], 0)
    y_sample = np.concatenate([r[i]['y'][TP:T].reshape(NSEQ, 4, D) for i in range(NCORES)], 0)
    gsp = np.stack([r[i]['gsp'] for i in range(NCORES)], 1)
    gss = np.concatenate([r[i]['gss'] for i in range(NCORES)], 1)
    k_prompt = np.stack([r[i]['kout'][0:TP].reshape(TP, 4, 128) for i in range(NCORES)], 0)
    v_prompt = np.stack([r[i]['vout'][0:TP].reshape(TP, 4, 128) for i in range(NCORES)], 0)
    k_sample = np.concatenate([r[i]['kout'][TP:T].reshape(NSEQ, 4, 4, 128) for i in range(NCORES)], 0)
    v_sample = np.concatenate([r[i]['vout'][TP:T].reshape(NSEQ, 4, 4, 128) for i in range(NCORES)], 0)
    f = lambda a: np.ascontiguousarray(a, dtype=np.float32)
    return (f(y_prompt), f(y_sample), f(gsp), f(gss), f(k_prompt), f(v_prompt), f(k_sample), f(v_sample))
```

```python
import contextlib
import numpy as np
import concourse.bass as bass
import concourse.mybir as mybir
from concourse.bass_utils import run_bass_kernel_spmd

F32 = mybir.dt.float32
BF16 = mybir.dt.bfloat16
AF = mybir.ActivationFunctionType
ALU = mybir.AluOpType

NCORES = 8
D = 1024
KC = 8
TP = 2048
TS = 64
T = TP + TS
NSEQ = 16
EPS = 1e-6
FFN_H = 2816
GROUPS = [(0, 512), (512, 512), (1024, 512), (1536, 512), (2048, 64)]
TILES = [(i * 128, 128) for i in range(16)] + [(2048, 64)]
DILS = [(128, 1), (512, 4), (2048, 16)]


class Trk:
    __slots__ = ('name', 'w', 'r')

    def __init__(self, name='', init=None):
        self.name = name
        self.w = None
        self.r = dict(init) if init else {}


class Buf(Trk):
    def __init__(self, name, t, init=None):
        super().__init__(name, init)
        self.t = t
        self.subs = {}
        self.init = dict(init) if init else {}

    def sub(self, key):
        s = self.subs.get(key)
        if s is None:
            s = Trk(f"{self.name}.{key}", self.init)
            self.subs[key] = s
        return s

    def all(self):
        return [self] + list(self.subs.values())

    def __getitem__(self, k):
        return self.t[k]


class Prog:
    ENG = ['pe', 'act', 'dve', 'pool', 'sp']
    EPOCH = 16000
    DEPOCH = 1000

    def __init__(self, nc):
        self.nc = nc
        self.es = contextlib.ExitStack()
        self.q = {e: [] for e in self.ENG}
        self.cnt = {e: 0 for e in self.ENG}
        self.seen = {e: {} for e in self.ENG}
        self.dcnt = {}
        self.fin = {}
        self.freed = {}
        self.scopes = []
        self.psf = []
        self.psb = []
        self.ipf = 0
        self.ipb = 0
        self.held = set()

    def _merge_freed(self, b):
        for t in b.all():
            deps = dict(t.r)
            if t.w is not None:
                deps[t.w[0]] = max(deps.get(t.w[0], 0), t.w[1])
            for k, v in deps.items():
                if self.freed.get(k, 0) < v:
                    self.freed[k] = v

    @contextlib.contextmanager
    def scope(self):
        es = contextlib.ExitStack()
        bufs = []
        self.scopes.append((es, bufs))
        try:
            yield
        finally:
            self.scopes.pop()
            for b in bufs:
                self._merge_freed(b)
            es.close()

    def sbuf(self, name, shape, dt):
        es, bufs = self.scopes[-1] if self.scopes else (self.es, [])
        self.nalloc = getattr(self, 'nalloc', 0) + 1
        name = f"{name}_{self.nalloc}"
        t = es.enter_context(self.nc.sbuf_tensor(name, list(shape), dt))
        b = Buf(name, t, self.freed)
        bufs.append(b)
        return b

    def psum(self, name, shape, dt):
        t = self.es.enter_context(self.nc.psum_tensor(name, list(shape), dt))
        return Buf(name, t)

    def next_ps(self, exclude=()):
        while True:
            b = self.psf[self.ipf % len(self.psf)]
            self.ipf += 1
            if b not in exclude and b not in self.held:
                return b

    def next_pb(self):
        b = self.psb[self.ipb % len(self.psb)]
        self.ipb += 1
        return b

    def op(self, e, fn, reads=(), writes=(), dsem=None):
        deps = {}

        def add(k, v):
            if deps.get(k, 0) < v:
                deps[k] = v
        for t in reads:
            if t.w is not None:
                add(*t.w)
        for t in writes:
            if t.w is not None:
                add(*t.w)
            for k, v in t.r.items():
                add(k, v)
        waits = []
        seen = self.seen[e]
        for k, v in deps.items():
            if e == 'pe' and k.startswith('pe#'):
                continue
            if seen.get(k, 0) >= v:
                continue
            seen[k] = v
            waits.append((k, v))
        if dsem is None:
            self.cnt[e] += 1
            ep = (self.cnt[e] - 1) // self.EPOCH
            me = (f"{e}#{ep}", self.cnt[e] - ep * self.EPOCH)
            self.fin[me[0]] = (me[1], 1)
        else:
            c = self.dcnt.get(dsem, 0) + 1
            self.dcnt[dsem] = c
            ep = (c - 1) // self.DEPOCH
            me = (f"{dsem}#{ep}", 16 * (c - ep * self.DEPOCH))
            self.fin[me[0]] = (me[1], 16)
        for t in reads:
            if t.r.get(me[0], 0) < me[1]:
                t.r[me[0]] = me[1]
        for t in writes:
            t.w = me
            t.r = {}
        self.q[e].append((waits, fn, me))

    def emit(self):
        nc = self.nc
        sems = {k: self.es.enter_context(nc.semaphore("s_" + k.replace('#', '_'))) for k in self.fin}
        finals = [(k, v[0]) for k, v in self.fin.items()]

        def replay(e, eng):
            for waits, fn, me in self.q[e]:
                for k, v in waits:
                    eng.wait_ge(sems[k], v)
                ins = fn(eng)
                ins.then_inc(sems[me[0]], self.fin[me[0]][1])
            if e == 'sp':
                for k, v in finals:
                    eng.wait_ge(sems[k], v)
        with nc.Block() as block:
            @block.tensor
            def _(eng):
                replay('pe', eng)

            @block.scalar
            def _(eng):
                replay('act', eng)

            @block.vector
            def _(eng):
                replay('dve', eng)

            @block.gpsimd
            def _(eng):
                replay('pool', eng)

            @block.sync
            def _(eng):
                replay('sp', eng)
        self.es.close()


def _const_blocks():
    f = {}
    b = {}
    gl = {}
    i128 = np.arange(128)
    f['identF'] = np.eye(128, dtype=np.float32)
    s, t = np.meshgrid(i128, i128, indexing='ij')
    same64 = (s // 64 == t // 64) & (s <= t)
    gl['triNeg'] = np.where(same64, -1.0 / 16.0, 0.0).astype(np.float32)
    gl['maskC'] = same64.astype(np.float32)
    same4 = (s // 4 == t // 4) & (s <= t)
    gl['tri4Neg'] = np.where(same4, -1.0 / 16.0, 0.0).astype(np.float32)[:, :64]
    gl['triRevNeg'] = np.where((s // 64 == t // 64) & (s > t), -1.0 / 16.0, 0.0).astype(np.float32)
    gl['tri4RevNeg'] = np.where((s // 4 == t // 4) & (s > t), -1.0 / 16.0, 0.0).astype(np.float32)[:, :64]
    gl['maskC4'] = np.tile(same64.astype(np.float32), (1, 4))
    gl['mask4'] = same4.astype(np.float32)[:, :64]
    rm = np.zeros((128, 16), np.float32)
    for p in range(64):
        rm[p, p // 4] = 1.0
    gl['rowmask'] = rm
    b['ones'] = np.ones((128, 128), np.float32)
    b['identB'] = np.eye(128, dtype=np.float32)
    NEG = -30000.0
    nm2 = np.zeros((128, 256), np.float32)
    nm2[:, 0:128] = np.where(s >= t, 0.0, NEG)
    nm2[:, 128:256] = np.where(s <= t, 0.0, NEG)
    b['negmask2'] = nm2
    nf = np.full((128, 72), NEG, np.float32)
    for gq in range(2):
        for tt in range(4):
            nf[tt:, 0 * 8 + gq * 4 + tt] = 0.0
            nf[:, (1 + tt) * 8 + gq * 4 + tt] = 0.0
            nf[:, (5 + tt) * 8 + gq * 4 + tt] = 0.0
    b['negfull'] = nf
    nn = np.full((128, 16, 24), NEG, np.float32)
    for bb in range(16):
        for t2 in range(4):
            p = 4 * bb + t2
            for g in range(3):
                for gq in range(2):
                    for tt in range(4):
                        ok = (t2 <= tt) if g == 0 else (t2 == tt)
                        if ok:
                            nn[p, bb, g * 8 + gq * 4 + tt] = 0.0
    b['negNew'] = nn.reshape(128, 16 * 24)
    return f, b, gl


def _rope_table():
    half = 16
    freqs = np.power(np.float32(500000.0), -np.arange(half, dtype=np.float32) * np.float32(2.0) / np.float32(32.0)).astype(np.float32)
    pos = np.concatenate([np.arange(TP, dtype=np.float32), np.tile(TP + np.arange(4, dtype=np.float32), NSEQ)]).astype(np.float32)
    ang = (pos[None, :] * freqs[:, None]).astype(np.float32)
    C = np.ones((128, T), np.float32)
    S = np.zeros((128, T), np.float32)
    C[0:16] = np.cos(ang)
    C[16:32] = np.cos(ang)
    S[0:16] = -np.sin(ang)
    S[16:32] = np.sin(ang)
    return np.ascontiguousarray(np.stack([C, S], axis=1))


def _fm(v, nchunk):
    return np.ascontiguousarray(np.asarray(v, np.float32).reshape(nchunk, 128).T)


class _Cols:
    def __init__(self):
        self.off = {}
        self.n = 0
        self.parts = []

    def add(self, name, arr):
        arr = np.asarray(arr, np.float32)
        if arr.shape[0] < 128:
            arr = np.concatenate([arr, np.zeros((128 - arr.shape[0],) + arr.shape[1:], np.float32)], 0)
        self.off[name] = (self.n, arr.shape[1])
        self.n += arr.shape[1]
        self.parts.append(arr)

    def array(self):
        return np.ascontiguousarray(np.concatenate(self.parts, axis=1))


def _layout_consts(inputs):
    f, b, gl = _const_blocks()
    cf = _Cols()
    for k, v in f.items():
        cf.add(k, v)
    cg = _Cols()
    for k, v in gl.items():
        cg.add(k, v)
    for i in range(2):
        cf.add(f'a_norm{i}', _fm(inputs['a_norm'][i], 8))
        cf.add(f'b_norm{i}', _fm(inputs['b_norm'][i], 8))
        cf.add(f'a_onorm{i}', _fm(inputs['a_onorm'][i], 2))
        cf.add(f'b_q_norm{i}', _fm(inputs['b_q_norm'][i], 1))
    for i in range(4):
        cf.add(f'f_norm{i}', _fm(inputs['f_norm'][i], 8))
        cf.add(f'e_norm{i}', _fm(inputs['e_norm'][i], 8))
    cf.add('kv_norm', _fm(inputs['kv_norm'], 8))
    cf.add('k_norm', _fm(inputs['k_norm'], 1))
    cb = _Cols()
    for k, v in b.items():
        cb.add(k, v)
    return cf, cb, cg


def build(cf_off, ncf, cb_off, ncb, cg_off, ncg, steps=None, debug=False):
    nc = bass.Bass("TRN2", target_bir_lowering=False)

    def din(name, shape):
        return nc.dram_tensor(name, list(shape), F32, kind="ExternalInput").ap()

    def dout(name, shape):
        return nc.dram_tensor(name, list(shape), F32, kind="ExternalOutput").ap()

    xin = din("xin", [T, D])
    pin = din("pin", [4, T, 256])
    sgla = din("sgla", [2, NSEQ, 4, 128, 256])
    ck = din("ck", [NSEQ, 2048, 512])
    cv = din("cv", [NSEQ, 2048, 512])
    a_w_in = din("a_w_in", [2, D, 3088])
    a_w_o = din("a_w_o", [2, D, D])
    w_kv = din("w_kv", [D, D])
    b_w_q = din("b_w_q", [2, D, 3072])
    b_w_o = din("b_w_o", [2, D, D])
    f_w_in = din("f_w_in", [4, D, 2 * FFN_H])
    f_w_out = din("f_w_out", [4, FFN_H, D])
    e_w_proj = din("e_w_proj", [4, 256, D])
    e_w_gate = din("e_w_gate", [4, D, D])
    wgk = din("wgk", [2, 17, 512])
    cstf = din("cstf", [128, ncf])
    cstb = din("cstb", [128, ncb])
    cstg = din("cstg", [128, ncg])
    rope = din("rope", [128, 2, T])

    y = dout("y", [T, D])
    gsp = dout("gsp", [2, 4, 128, 256])
    gss = dout("gss", [2, NSEQ, 4, 128, 256])
    kout = dout("kout", [T, 512])
    vout = dout("vout", [T, 512])

    P = Prog(nc)
    for i in range(6):
        P.psf.append(P.psum(f"psf{i}", [128, 512], F32))
    for i in range(2):
        P.psb.append(P.psum(f"psb{i}", [128, 1024], BF16))

    def MM(out, lhsT, rhs, start=True, stop=True, R=(), W=()):
        P.op('pe', lambda e: e.matmul(out, lhsT, rhs, start=start, stop=stop), R, W)

    def TR(out, in_, ident, R=(), W=()):
        P.op('pe', lambda e: e.transpose(out, in_, ident), R, W)

    def ACT(out, in_, func, R=(), W=(), bias=None, scale=None):
        kw = {}
        if bias is not None:
            kw['bias'] = bias
        if scale is not None:
            kw['scale'] = scale
        P.op('act', lambda e: e.activation(out=out, in_=in_, func=func, **kw), R, W)

    def TT(eng, out, in0, in1, op, R=(), W=()):
        P.op(eng, lambda e: e.tensor_tensor(out=out, in0=in0, in1=in1, op=op), R, W)

    def TSC(eng, out, in0, s1, s2, op0, op1, R=(), W=()):
        if s2 is None:
            P.op(eng, lambda e: e.tensor_scalar(out=out, in0=in0, scalar1=s1, scalar2=None, op0=op0), R, W)
        else:
            P.op(eng, lambda e: e.tensor_scalar(out=out, in0=in0, scalar1=s1, scalar2=s2, op0=op0, op1=op1), R, W)

    def STT(eng, out, in0, scalar, in1, op0, op1, R=(), W=()):
        P.op(eng, lambda e: e.scalar_tensor_tensor(out=out, in0=in0, scalar=scalar, in1=in1, op0=op0, op1=op1), R, W)

    def CP(eng, out, in_, R=(), W=()):
        if eng == 'act':
            P.op('act', lambda e: e.copy(out=out, in_=in_), R, W)
        else:
            P.op(eng, lambda e: e.tensor_copy(out=out, in_=in_), R, W)

    def RCP(out, in_, R=(), W=()):
        P.op('dve', lambda e: e.reciprocal(out=out, in_=in_), R, W)

    def MEMSET(eng, ap, val, W=()):
        P.op(eng, lambda e: e.memset(ap, val), (), W)

    def DMA(eng, out, in_, R, W, dsem):
        P.op(eng, lambda e: e.dma_start(out=out, in_=in_), R, W, dsem=dsem)

    class Ring:
        def __init__(self, name, n, shape, dt):
            self.bufs = [P.sbuf(f"{name}{i}", shape, dt) for i in range(n)]
            self.i = 0
            self.name = name

        def next(self):
            k = self.i % len(self.bufs)
            self.i += 1
            return self.bufs[k], f"{self.name}{k}"

    xT = P.sbuf("xT", [128, KC, T], F32)
    hT = P.sbuf("hT", [128, KC, T], BF16)
    cF = P.sbuf("cF", [128, ncf], F32)
    cB = P.sbuf("cB", [128, ncb], BF16)

    class PH:
        G = None
        wring = None

    def phase_alloc(gch, wsz):
        PH.G = P.sbuf("G", [128, gch, T], BF16) if gch else None
        PH.wring = Ring("w", 3, [128, wsz], BF16)
    tmpA = Ring("tmpA", 3, [128, 512], F32)
    tmpB = tmpA
    sqr = Ring("sq", 2, [128, 512], BF16)
    rsr = Ring("rs", 2, [128, 512], F32)

    def cf(name, c0=0, n=None, rows=128):
        o, w = cf_off[name]
        if n is None:
            n = w - c0
        return cF[0:rows, o + c0:o + c0 + n]

    def cb(name, c0=0, n=None, rows=128):
        o, w = cb_off[name]
        if n is None:
            n = w - c0
        return cB[0:rows, o + c0:o + c0 + n]

    DMA('sp', cF[:, :], cstf, [], [cF], 'cF')
    DMA('pool', cB[:, :], cstb, [], [cB], 'cB')

    def grp_of(t0):
        return min(t0 // 512, 4)

    def xs(g):
        return xT.sub(g)

    def hs(g):
        return hT.sub(g)

    def gs(g):
        return PH.G.sub(g)

    ALLG = list(range(5))

    def load_w(segs, nk, ncols):
        slot, sem = PH.wring.next()
        view = slot.t[:, 0:nk * ncols].rearrange("p (k n) -> p k n", n=ncols)
        rd = [slot.sub(0), slot.sub(1)]
        for si, (src, off, w) in enumerate(segs):
            wr = [slot.sub(si)] if len(segs) == 2 else rd
            DMA('pool', view[:, :, off:off + w], src.rearrange("(k p) n -> p k n", p=128), [], wr, sem)
        return rd, view

    def rmsnorm(gname):
        for g, (g0, gn) in enumerate(GROUPS):
            ps = P.next_ps()
            for c in range(KC):
                sq, _ = sqr.next()
                ACT(sq[:, 0:gn], xT[:, c, g0:g0 + gn], AF.Square, R=[xs(g)], W=[sq])
                MM(ps[:, 0:gn], cb('ones'), sq[:, 0:gn], start=(c == 0), stop=(c == KC - 1), R=[cB, sq], W=[ps])
            rs, _ = rsr.next()
            ACT(rs[:, 0:gn], ps[:, 0:gn], AF.Ln, R=[ps], W=[rs], bias=EPS, scale=1.0 / D)
            ACT(rs[:, 0:gn], rs[:, 0:gn], AF.Exp, R=[rs], W=[rs], scale=-0.5)
            for c in range(KC):
                STT('dve', hT[:, c, g0:g0 + gn], xT[:, c, g0:g0 + gn], cf(gname, c, 1), rs[:, 0:gn],
                    ALU.mult, ALU.mult, R=[xs(g), rs, cF], W=[hs(g)])

    class RR:
        ccr = ssr = swr = None

    def rope_alloc():
        RR.ccr = Ring("cc", 2, [32, 512], F32)
        RR.ssr = Ring("ss", 2, [32, 512], F32)
        RR.swr = Ring("sw", 2, [32, 512], F32)

    def proj_norm_rope(v, slot, gname, sink_a, sink_b):
        st = [None] * len(GROUPS)

        def part1(g):
            g0, gn = GROUPS[g]
            ps = P.next_ps()
            for kc in range(KC):
                MM(ps[:, 0:gn], v[:, kc, 0:128], hT[:, kc, g0:g0 + gn], start=(kc == 0), stop=(kc == KC - 1), R=[*slot, hs(g)], W=[ps])
            sq, _ = sqr.next()
            ACT(sq[:, 0:gn], ps[:, 0:gn], AF.Square, R=[ps], W=[sq])
            pss = P.next_ps()
            MM(pss[:, 0:gn], cb('ones'), sq[:, 0:gn], R=[cB, sq], W=[pss])
            rs, _ = rsr.next()
            ACT(rs[:, 0:gn], pss[:, 0:gn], AF.Ln, R=[pss], W=[rs], bias=EPS, scale=1.0 / 128)
            ACT(rs[:, 0:gn], rs[:, 0:gn], AF.Exp, R=[rs], W=[rs], scale=-0.5)
            cs, csem = RR.ccr.next()
            DMA('sp', cs[0:32, 0:gn], rope[0:32, 0, g0:g0 + gn], [], [cs], csem)
            st[g] = (ps, rs, cs)
            P.held.add(ps)

        def part2a(g):
            g0, gn = GROUPS[g]
            ps, rs, cs = st[g]
            kn, _ = tmpA.next()
            STT('dve', kn[:, 0:gn], ps[:, 0:gn], cf(gname, 0, 1), rs[:, 0:gn], ALU.mult, ALU.mult, R=[ps, rs, cF], W=[kn])
            P.held.discard(ps)
            sw, swsem = RR.swr.next()
            DMA('sp', sw[0:16, 0:gn], kn[16:32, 0:gn], [kn], [sw.sub(0)], swsem)
            DMA('sp', sw[16:32, 0:gn], kn[0:16, 0:gn], [kn], [sw.sub(1)], swsem)
            knc, _ = tmpA.next()
            TT('pool', knc[0:32, 0:gn], kn[0:32, 0:gn], cs[0:32, 0:gn], ALU.mult, R=[kn, cs], W=[knc])
            ss, ssem = RR.ssr.next()
            DMA('sp', ss[0:32, 0:gn], rope[0:32, 1, g0:g0 + gn], [], [ss], ssem)
            st[g] = (kn, knc, sw, ss)
            sink_a(g, kn)

        def part2b(g):
            g0, gn = GROUPS[g]
            kn, knc, sw, cs = st[g]
            TT('dve', sw[0:32, 0:gn], sw[0:32, 0:gn], cs[0:32, 0:gn], ALU.mult, R=[sw.sub(0), sw.sub(1), cs], W=[sw.sub(0), sw.sub(1)])
            sink_b(g, knc, sw, [sw.sub(0), sw.sub(1)])

        ng = len(GROUPS)
        part1(0)
        for g in range(ng):
            if g + 1 < ng:
                part1(g + 1)
            part2a(g)
            if g >= 1:
                part2b(g - 1)
        part2b(ng - 1)

    def load_x():
        with P.scope():
            stg = Ring("xstg", 2, [128, D], F32)
            for ti, (t0, tn) in enumerate(TILES):
                g = grp_of(t0)
                s, sem = stg.next()
                DMA('sp', s[0:tn, :], xin[t0:t0 + tn, :], [], [s], sem)
                for hb in range(2):
                    ps = P.next_ps()
                    for j in range(4):
                        c = hb * 4 + j
                        TR(ps[:, j * 128:j * 128 + tn], s[0:tn, c * 128:(c + 1) * 128], cf('identF', 0, tn, rows=tn), R=[s, cF], W=[ps])
                    src = ps.t[:, :].rearrange("p (a b) -> p a b", b=128)[:, :, 0:tn]
                    CP('act' if hb == 0 else 'dve', xT[:, hb * 4:hb * 4 + 4, t0:t0 + tn], src, R=[ps], W=[xs(g)])

    def store_y():
        with P.scope():
            stg = Ring("ystg", 2, [128, D], F32)
            for ti, (t0, tn) in enumerate(TILES):
                g = grp_of(t0)
                s, sem = stg.next()
                for hb in range(2):
                    ps = P.next_ps()
                    for j in range(4):
                        c = hb * 4 + j
                        TR(ps[0:tn, j * 128:(j + 1) * 128], xT[:, c, t0:t0 + tn], cf('identF'), R=[xs(g), cF], W=[ps])
                    CP('act' if hb == 0 else 'dve', s[0:tn, hb * 512:(hb + 1) * 512], ps[0:tn, :], R=[ps], W=[s])
                DMA('sp', y[t0:t0 + tn, :], s[0:tn, :], [s], [], sem)

    def proj_add_x(Wd, k0, nk):
        for nb in range(4):
            slot, v = load_w([(Wd[k0:k0 + nk * 128, nb * 256:(nb + 1) * 256], 0, 256)], nk, 256)
            for hh in range(2):
                n = nb * 2 + hh
                for g, (g0, gn) in enumerate(GROUPS):
                    ps = P.next_ps()
                    for kc in range(nk):
                        MM(ps[:, 0:gn], v[:, kc, hh * 128:(hh + 1) * 128], PH.G[:, kc, g0:g0 + gn], start=(kc == 0), stop=(kc == nk - 1),
                           R=[*slot, gs(g)], W=[ps])
                    TT('dve', xT[:, n, g0:g0 + gn], xT[:, n, g0:g0 + gn], ps[:, 0:gn], ALU.add, R=[ps, xs(g)], W=[xs(g)])

    def ffn(l):
        rmsnorm(f'f_norm{l}')
        Win = f_w_in[l]
        Wout = f_w_out[l]
        j0 = 0
        for nj in (4, 4, 4, 4, 3, 3):
            for jj in range(nj):
                j = j0 + jj
                slot, v = load_w([(Win[:, j * 128:(j + 1) * 128], 0, 128), (Win[:, FFN_H + j * 128:FFN_H + (j + 1) * 128], 128, 128)], KC, 256)
                for g, (g0, gn) in enumerate(GROUPS):
                    psa = P.next_ps()
                    psb = P.next_ps()
                    for kc in range(KC):
                        MM(psa[:, 0:gn], v[:, kc, 0:128], hT[:, kc, g0:g0 + gn], start=(kc == 0), stop=(kc == KC - 1), R=[*slot, hs(g)], W=[psa])
                    for kc in range(KC):
                        MM(psb[:, 0:gn], v[:, kc, 128:256], hT[:, kc, g0:g0 + gn], start=(kc == 0), stop=(kc == KC - 1), R=[*slot, hs(g)], W=[psb])
                    sa, _ = tmpA.next()
                    ACT(sa[:, 0:gn], psa[:, 0:gn], AF.Silu, R=[psa], W=[sa])
                    TT('dve', PH.G[:, jj, g0:g0 + gn], sa[:, 0:gn], psb[:, 0:gn], ALU.mult, R=[sa, psb], W=[gs(g)])
            proj_add_x(Wout, j0 * 128, nj)
            j0 += nj

    def ple(l):
        with P.scope():
            stg = Ring("pstg", 2, [128, 256], F32)
            for ti, (t0, tn) in enumerate(TILES):
                g = grp_of(t0)
                s, sem = stg.next()
                DMA('sp', s[0:tn, :], pin[l, t0:t0 + tn, :], [], [s], sem)
                ps = P.next_ps()
                for j in range(2):
                    TR(ps[:, j * 128:j * 128 + tn], s[0:tn, j * 128:(j + 1) * 128], cf('identF', 0, tn, rows=tn), R=[s, cF], W=[ps])
                src = ps.t[:, 0:256].rearrange("p (a b) -> p a b", b=128)[:, :, 0:tn]
                CP('act' if ti % 2 == 0 else 'dve', PH.G[:, 0:2, t0:t0 + tn], src, R=[ps], W=[gs(g)])
        rmsnorm(f'e_norm{l}')
        Wg = e_w_gate[l]
        Wp = e_w_proj[l]
        for nb in range(4):
            slot, v = load_w([(Wg[:, nb * 256:(nb + 1) * 256], 0, 256)], KC, 256)
            slot2, v2 = load_w([(Wp[:, nb * 256:(nb + 1) * 256], 0, 256)], 2, 256)
            for hh in range(2):
                n = nb * 2 + hh
                for g, (g0, gn) in enumerate(GROUPS):
                    psg = P.next_ps()
                    psp = P.next_ps()
                    for kc in range(KC):
                        MM(psg[:, 0:gn], v[:, kc, hh * 128:(hh + 1) * 128], hT[:, kc, g0:g0 + gn], start=(kc == 0), stop=(kc == KC - 1), R=[*slot, hs(g)], W=[psg])
                    for kc in range(2):
                        MM(psp[:, 0:gn], v2[:, kc, hh * 128:(hh + 1) * 128], PH.G[:, kc, g0:g0 + gn], start=(kc == 0), stop=(kc == 1), R=[*slot2, gs(g)], W=[psp])
                    sg, _ = tmpA.next()
                    ACT(sg[:, 0:gn], psg[:, 0:gn], AF.Sigmoid, R=[psg], W=[sg])
                    t2, _ = tmpB.next()
                    TT('dve', t2[:, 0:gn], sg[:, 0:gn], psp[:, 0:gn], ALU.mult, R=[sg, psp], W=[t2])
                    TT('dve', xT[:, n, g0:g0 + gn], xT[:, n, g0:g0 + gn], t2[:, 0:gn], ALU.add, R=[t2, xs(g)], W=[xs(g)])

    def gla(i):
        rmsnorm(f'a_norm{i}')
        Win = a_w_in[i]
        with P.scope():
            cG = P.sbuf("cG", [128, ncg], F32)
            DMA('sp', cG[:, :], cstg, [], [cG], 'cG')

            def cf2(name, c0=0, n=None, rows=128):
                o, w = cg_off[name]
                if n is None:
                    n = w - c0
                return cG[0:rows, o + c0:o + c0 + n]
            cGb = P.sbuf("cGb", [128, ncg], BF16)
            DMA('pool', cGb[:, :], cstg, [], [cGb], 'cGb')

            def cgb(name, c0=0, n=None, rows=128):
                o, w = cg_off[name]
                if n is None:
                    n = w - c0
                return cGb[0:rows, o + c0:o + c0 + n]
            spb = P.sbuf("spb", [128, 512], BF16)
            glr = P.sbuf("glr", [32, T], F32)
            wg = P.sbuf("wgksb", [32, 512], F32)
            zt = P.sbuf("zt", [128, 512], F32)
            Em = P.sbuf("Em", [128, 512], F32)
            Ee = P.sbuf("Ee", [128, 512], F32)
            kin = P.sbuf("kin", [128, 512], BF16)
            kend = P.sbuf("kend", [128, 512], BF16)
            sets = []
            for k_ in range(2):
                sets.append(dict(Ep=P.sbuf(f"Ep{k_}", [128, 512], F32), qin=P.sbuf(f"qin{k_}", [128, 512], BF16),
                                 sr=P.sbuf(f"sr{k_}", [128, 2, 512], F32), vtok=P.sbuf(f"vtok{k_}", [128, 4, 256], BF16),
                                 ktall=P.sbuf(f"ktall{k_}", [128, 4, 128], BF16), attall=P.sbuf(f"attall{k_}", [128, 4, 128], BF16)))
            Sall = P.sbuf("Sall", [128, 9, 256], BF16)
            S32s = [P.sbuf("S32a", [128, 256], F32), P.sbuf("S32b", [128, 256], F32)]
            s0r = Ring("s0", 3, [128, 256], F32)
            s0br = Ring("s0b", 3, [128, 256], BF16)
            snr = Ring("sn", 3, [128, 256], F32)
            vmr = Ring("vm", 2, [64, 256], BF16)

            MEMSET('dve', glr[:, :], 1.0, W=[glr])
            DMA('sp', wg[0:17, :], wgk[i], [], [wg], 'wgk')
            slot, v = load_w([(Win[:, 3072:3088], 0, 16)], KC, 16)
            for g, (g0, gn) in enumerate(GROUPS):
                ps = P.next_ps()
                for kc in range(KC):
                    MM(ps[0:16, 0:gn], v[:, kc, 0:16], hT[:, kc, g0:g0 + gn], start=(kc == 0), stop=(kc == KC - 1), R=[*slot, hs(g)], W=[ps])
                CP('act', glr[0:16, g0:g0 + gn], ps[0:16, 0:gn], R=[ps], W=[glr])

            for h in range(4):
                slotA, vA = load_w([(Win[:, h * 128:(h + 1) * 128], 0, 128), (Win[:, 512 + h * 128:512 + (h + 1) * 128], 128, 128)], KC, 256)
                slotB, vB = load_w([(Win[:, 1024 + h * 256:1024 + (h + 1) * 256], 0, 256)], KC, 256)
                slotC, vC = load_w([(Win[:, 2048 + h * 256:2048 + (h + 1) * 256], 0, 256)], KC, 256)
                spar = [0]
                MEMSET('dve', S32s[0][:, :], 0.0, W=[S32s[0]])
                MEMSET('dve', Sall[:, 0, :], 0.0, W=[Sall])
                gc0 = 2 * (h % 2)

                def stageAg1(g):
                    g0, gn = GROUPS[g]
                    B_ = sets[g % 2]
                    Ep, qin, sr, vtok, ktall, attall = B_['Ep'], B_['qin'], B_['sr'], B_['vtok'], B_['ktall'], B_['attall']
                    samp = (g == 4)
                    tiles = [(0, 64)] if samp else [(tt * 128, 128) for tt in range(4)]
                    tn = tiles[0][1]
                    tri = cgb('tri4Neg', 0, 64, rows=64) if samp else cgb('triNeg')
                    trir = cgb('tri4RevNeg', 0, 64, rows=64) if samp else cgb('triRevNeg')
                    nz = 128 * len(tiles)
                    psz = P.next_ps()
                    for ti, (o0, _) in enumerate(tiles):
                        MM(psz[0:tn, ti * 128:(ti + 1) * 128], glr[0:17, g0 + o0:g0 + o0 + tn], wg[0:17, h * 128:(h + 1) * 128], R=[glr, wg], W=[psz])
                    ACT(zt[0:tn, 0:nz], psz[0:tn, 0:nz], AF.Exp, R=[psz], W=[zt], scale=-1.0)
                    ACT(spb[0:tn, 0:nz], zt[0:tn, 0:nz], AF.Ln, R=[zt], W=[spb], bias=1.0)

                def stageAg2(g):
                    g0, gn = GROUPS[g]
                    B_ = sets[g % 2]
                    Ep, qin, sr, vtok, ktall, attall = B_['Ep'], B_['qin'], B_['sr'], B_['vtok'], B_['ktall'], B_['attall']
                    samp = (g == 4)
                    tiles = [(0, 64)] if samp else [(tt * 128, 128) for tt in range(4)]
                    tn = tiles[0][1]
                    tri = cgb('tri4Neg', 0, 64, rows=64) if samp else cgb('triNeg')
                    trir = cgb('tri4RevNeg', 0, 64, rows=64) if samp else cgb('triRevNeg')
                    nz = 128 * len(tiles)
                    psc = P.next_ps()
                    psc2 = P.next_ps()
                    for ti, (o0, _) in enumerate(tiles):
                        MM(psc[:, o0:o0 + tn], spb[0:tn, ti * 128:(ti + 1) * 128], tri, R=[spb, cGb], W=[psc])
                        MM(psc2[:, o0:o0 + tn], spb[0:tn, ti * 128:(ti + 1) * 128], trir, R=[spb, cGb], W=[psc2])
                    ACT(Ep[:, 0:gn], psc[:, 0:gn], AF.Exp, R=[psc], W=[Ep])
                    ACT(Em[:, 0:gn], psc[:, 0:gn], AF.Exp, R=[psc], W=[Em], scale=-1.0)
                    ACT(Ee[:, 0:gn], psc2[:, 0:gn], AF.Exp, R=[psc2], W=[Ee])

                def stageAr(g):
                    g0, gn = GROUPS[g]
                    B_ = sets[g % 2]
                    Ep, qin, sr, vtok, ktall, attall = B_['Ep'], B_['qin'], B_['sr'], B_['vtok'], B_['ktall'], B_['attall']
                    samp = (g == 4)
                    tiles = [(0, 64)] if samp else [(tt * 128, 128) for tt in range(4)]
                    tn = tiles[0][1]
                    tri = cgb('tri4Neg', 0, 64, rows=64) if samp else cgb('triNeg')
                    trir = cgb('tri4RevNeg', 0, 64, rows=64) if samp else cgb('triRevNeg')
                    nz = 128 * len(tiles)
                    psq = P.next_ps()
                    for kc in range(KC):
                        MM(psq[:, 0:gn], vA[:, kc, 0:128], hT[:, kc, g0:g0 + gn], start=(kc == 0), stop=(kc == KC - 1), R=[*slotA, hs(g)], W=[psq])
                    STT('dve', qin[:, 0:gn], psq[:, 0:gn], float(128 ** -0.5), Ep[:, 0:gn], ALU.mult, ALU.mult, R=[psq, Ep], W=[qin])
                    psk = P.next_ps()
                    for kc in range(KC):
                        MM(psk[:, 0:gn], vA[:, kc, 128:256], hT[:, kc, g0:g0 + gn], start=(kc == 0), stop=(kc == KC - 1), R=[*slotA, hs(g)], W=[psk])
                    TT('dve', kin[:, 0:gn], psk[:, 0:gn], Em[:, 0:gn], ALU.mult, R=[psk, Em], W=[kin])
                    TT('dve', kend[:, 0:gn], psk[:, 0:gn], Ee[:, 0:gn], ALU.mult, R=[psk, Ee], W=[kend])
                    for j in range(2):
                        psr = P.next_ps()
                        for kc in range(KC):
                            MM(psr[:, 0:gn], vC[:, kc, j * 128:(j + 1) * 128], hT[:, kc, g0:g0 + gn], start=(kc == 0), stop=(kc == KC - 1), R=[*slotC, hs(g)], W=[psr])
                        ACT(sr[:, j, 0:gn], psr[:, 0:gn], AF.Silu, R=[psr], W=[sr])
                    for pr in range((len(tiles) + 1) // 2):
                        psv = P.next_ps()
                        tl = tiles[2 * pr:2 * pr + 2]
                        for k2, (o0, _) in enumerate(tl):
                            for kc in range(KC):
                                MM(psv[0:tn, k2 * 256:(k2 + 1) * 256], hT[:, kc, g0 + o0:g0 + o0 + tn], vB[:, kc, 0:256], start=(kc == 0), stop=(kc == KC - 1),
                                   R=[*slotB, hs(g)], W=[psv])
                        CP('act', vtok[0:tn, 2 * pr:2 * pr + len(tl), :], psv.t[0:tn, 0:256 * len(tl)].rearrange("p (a b) -> p a b", b=256), R=[psv], W=[vtok])
                    pb = P.next_pb()
                    for ti, (o0, _) in enumerate(tiles):
                        TR(pb[0:tn, ti * 128:(ti + 1) * 128], kend[:, o0:o0 + tn], cb('identB'), R=[kend, cB], W=[pb])
                    CP('dve', ktall[0:tn, 0:len(tiles), :], pb.t[0:tn, 0:nz].rearrange("p (a b) -> p a b", b=128), R=[pb], W=[ktall])
                    psa = P.next_ps()
                    for ti, (o0, _) in enumerate(tiles):
                        MM(psa[0:tn, ti * 128:ti * 128 + tn], kin[:, o0:o0 + tn], qin[:, o0:o0 + tn], R=[kin, qin], W=[psa])
                    if not samp:
                        TT('dve', attall.t[:, :, :].rearrange("p a b -> p (a b)"), psa[:, 0:512], cf2('maskC4'), ALU.mult, R=[psa, cG], W=[attall])
                    else:
                        TT('dve', attall[0:64, 0, 0:64], psa[0:64, 0:64], cf2('mask4', 0, 64, rows=64), ALU.mult, R=[psa, cG], W=[attall])

                def stageB1(g):
                    B_ = sets[g % 2]
                    Ep, vtok, ktall = B_['Ep'], B_['vtok'], B_['ktall']
                    if g < 4:
                        for ti in range(4):
                            pS2 = [P.next_ps(), P.next_ps()]
                            MM(pS2[0][:, 0:256], ktall[0:64, ti, :], vtok[0:64, ti, :], R=[ktall, vtok], W=[pS2[0]])
                            MM(pS2[1][:, 0:256], ktall[64:128, ti, :], vtok[64:128, ti, :], R=[ktall, vtok], W=[pS2[1]])
                            for hc in range(2):
                                col = ti * 128 + hc * 64 + 63
                                src, dst = S32s[spar[0]], S32s[1 - spar[0]]
                                STT('dve', dst[:, :], src[:, :], Ep[:, col:col + 1], pS2[hc][:, 0:256], ALU.mult, ALU.add, R=[pS2[hc], src, Ep], W=[dst])
                                spar[0] = 1 - spar[0]
                                CP('act', Sall[:, 2 * ti + hc + 1, :], dst[:, :], R=[dst], W=[Sall])
                        if g == 3:
                            DMA('sp', gsp[i, h], S32s[spar[0]][:, :], [S32s[spar[0]]], [], 'gsp')

                def stageB2(g):
                    g0, gn = GROUPS[g]
                    B_ = sets[g % 2]
                    Ep, qin, sr, vtok, ktall, attall = B_['Ep'], B_['qin'], B_['sr'], B_['vtok'], B_['ktall'], B_['attall']
                    samp = (g == 4)
                    pso = [P.next_ps(), P.next_ps()]
                    if not samp:
                        for ti in range(4):
                            o0 = ti * 128
                            for j in range(2):
                                MM(pso[j][:, o0:o0 + 128], vtok[:, ti, j * 128:(j + 1) * 128], attall[:, ti, :], start=True, stop=False, R=[vtok, attall], W=[pso[j]])
                                MM(pso[j][:, o0:o0 + 64], Sall[:, 2 * ti, j * 128:(j + 1) * 128], qin[:, o0:o0 + 64], start=False, stop=False, R=[Sall, qin], W=[pso[j]])
                                MM(pso[j][:, o0 + 64:o0 + 128], Sall[:, 2 * ti + 1, j * 128:(j + 1) * 128], qin[:, o0 + 64:o0 + 128], start=False, stop=True,
                                   R=[Sall, qin], W=[pso[j]])
                        if g < 3:
                            CP('act', Sall[:, 0, :], Sall[:, 8, :], R=[Sall], W=[Sall])
                    else:
                        for j in range(2):
                            MM(pso[j][:, 0:64], vtok[0:64, 0, j * 128:(j + 1) * 128], attall[0:64, 0, 0:64], start=True, stop=False, R=[vtok, attall], W=[pso[j]])
                        for b in range(NSEQ):
                            s0, s0sem = s0r.next()
                            DMA('sp', s0[:, :], sgla[i, b, h], [], [s0], s0sem)
                            s0b, _ = s0br.next()
                            CP('act', s0b[:, :], s0[:, :], R=[s0], W=[s0b])
                            for j in range(2):
                                MM(pso[j][:, 4 * b:4 * b + 4], s0b[:, j * 128:(j + 1) * 128], qin[:, 4 * b:4 * b + 4], start=False, stop=(b == NSEQ - 1),
                                   R=[s0b, qin], W=[pso[j]])
                            vm, _ = vmr.next()
                            TSC('dve', vm[0:64, :], vtok[0:64, 0, :], cf2('rowmask', b, 1, rows=64), None, ALU.mult, None, R=[vtok, cG], W=[vm])
                            pS = P.next_ps(exclude=pso)
                            MM(pS[:, 0:256], ktall[0:64, 0, :], vm[0:64, :], R=[ktall, vm], W=[pS])
                            sn, snsem = snr.next()
                            STT('dve', sn[:, :], s0[:, :], Ep[:, 4 * b + 3:4 * b + 4], pS[:, 0:256], ALU.mult, ALU.add, R=[s0, pS, Ep], W=[sn])
                            DMA('sp', gss[i, b, h], sn[:, :], [sn], [], snsem)
                    sqs = [sqr.next()[0], sqr.next()[0]]
                    for j in range(2):
                        ACT(sqs[j][:, 0:gn], pso[j][:, 0:gn], AF.Square, R=[pso[j]], W=[sqs[j]])
                    pss = P.next_ps(exclude=pso)
                    for j in range(2):
                        MM(pss[:, 0:gn], cb('ones'), sqs[j][:, 0:gn], start=(j == 0), stop=(j == 1), R=[cB, sqs[j]], W=[pss])
                    rso, _ = rsr.next()
                    ACT(rso[:, 0:gn], pss[:, 0:gn], AF.Ln, R=[pss], W=[rso], bias=EPS, scale=1.0 / 256)
                    ACT(rso[:, 0:gn], rso[:, 0:gn], AF.Exp, R=[rso], W=[rso], scale=-0.5)
                    for j in range(2):
                        TT('dve', sr[:, j, 0:gn], sr[:, j, 0:gn], rso[:, 0:gn], ALU.mult, R=[sr, rso], W=[sr])
                        STT('dve', PH.G[:, gc0 + j, g0:g0 + gn], pso[j][:, 0:gn], cf(f'a_onorm{i}', j, 1), sr[:, j, 0:gn], ALU.mult, ALU.mult,
                            R=[pso[j], sr, cF], W=[gs(g)])

                stageAg1(0)
                stageAg2(0)
                stageAr(0)
                for g in range(len(GROUPS)):
                    nxt = g + 1 < len(GROUPS)
                    if nxt:
                        stageAg1(g + 1)
                    stageB1(g)
                    if nxt:
                        stageAg2(g + 1)
                        stageAr(g + 1)
                    stageB2(g)
                if h % 2 == 1:
                    proj_add_x(a_w_o[i], (h // 2) * 512, 4)

    vdram = [[Trk(f"vdram{ti}_{hf}") for hf in range(2)] for ti in range(len(TILES))]

    def shared_kv(kT, v64):
        rmsnorm('kv_norm')
        with P.scope():
            kfr = Ring("kf", 3, [128, 512], F32)
            kst = Ring("kst", 2, [128, 128], F32)
            vst = Ring("vst", 2, [128, 256], F32)
            for c in range(4):
                slot, v = load_w([(w_kv[:, c * 128:(c + 1) * 128], 0, 128)], KC, 128)
                kfs = {}

                def ksink_a(g, kn, c=c):
                    g0, gn = GROUPS[g]
                    kf, _ = kfr.next()
                    kfs[g] = kf
                    CP('act', kf[:, 0:gn], kn[:, 0:gn], R=[kn], W=[kf.sub(0), kf.sub(1)])

                def ksink_b(g, knc, sw, swt, c=c):
                    g0, gn = GROUPS[g]
                    kf = kfs.pop(g)
                    TT('dve', kf[0:32, 0:gn], knc[0:32, 0:gn], sw[0:32, 0:gn], ALU.add, R=[knc, *swt], W=[kf.sub(0)])
                    kfall = [kf.sub(0), kf.sub(1)]
                    CP('act', kT[:, c, g0:g0 + gn], kf[:, 0:gn], R=kfall, W=[kT])
                    tiles = [(0, 64)] if g == 4 else [(tt * 128, 128) for tt in range(4)]
                    for (o0, tn) in tiles:
                        t0 = g0 + o0
                        pst = P.next_ps()
                        TR(pst[0:tn, 0:128], kf[:, o0:o0 + tn], cf('identF'), R=[*kfall, cF], W=[pst])
                        ks, ksem = kst.next()
                        CP('act', ks[0:tn, :], pst[0:tn, 0:128], R=[pst], W=[ks])
                        DMA('sp', kout[t0:t0 + tn, c * 128:(c + 1) * 128], ks[0:tn, :], [ks], [], ksem)
                proj_norm_rope(v, slot, 'k_norm', ksink_a, ksink_b)
            for half in range(2):
                slot, v = load_w([(w_kv[:, 512 + half * 256:512 + (half + 1) * 256], 0, 256)], KC, 256)
                for ti, (t0, tn) in enumerate(TILES):
                    g = grp_of(t0)
                    ps = P.next_ps()
                    for kc in range(KC):
                        MM(ps[0:tn, 0:256], hT[:, kc, t0:t0 + tn], v[:, kc, 0:256], start=(kc == 0), stop=(kc == KC - 1), R=[*slot, hs(g)], W=[ps])
                    vs, vsem = vst.next()
                    CP('act', vs[0:tn, :], ps[0:tn, 0:256], R=[ps], W=[vs])
                    DMA('sp', vout[t0:t0 + tn, half * 256:(half + 1) * 256], vs[0:tn, :], [vs], [vdram[ti][half]], vsem)
                    if ti == 16:
                        CP('dve', v64[0:64, half * 256:(half + 1) * 256], ps[0:64, 0:256], R=[ps], W=[v64])

    def battn(jl, kT, v64):
        rmsnorm(f'b_norm{jl}')
        Wq = b_w_q[jl]
        scale = float(128 ** -0.5)
        with P.scope():
            qt = P.sbuf("qt", [128, T], BF16)
            qt_all = [qt.sub((hl, g_)) for hl in ('hi', 'lo') for g_ in range(5)]
            qS = P.sbuf("qS", [128, 3, 8, TS], BF16)
            accO = P.sbuf("accO", [128, TP], F32)
            accL = P.sbuf("accL", [128, TP], F32)
            vtr = Ring("vt", 2, [128, 16, 128], BF16)
            wqr = Ring("wq", 2, [128, KC, 128], BF16)
            ptr = Ring("pt", 3, [128, 256], BF16)
            Kcr = Ring("Kc", 2, [128, 9, 128], BF16)
            Vcr = Ring("Vc", 2, [128, 9, 128], BF16)
            KTr = Ring("KTs", 2, [128, 9, 128], BF16)
            pfull = Ring("pfull", 2, [128, 72], BF16)
            pnr = Ring("pn", 2, [64, 24], BF16)
            rlr = Ring("rl", 2, [128, 8], F32)

            heads = [(c, gq, gi) for c in range(4) for gq in range(2) for gi in range(3)]
            pre = {}

            def prefetch(i):
                c, gq, gi = heads[i]
                win, dil = DILS[gi]
                qcol = ((gi * 4 + c) * 2 + gq) * 128
                wq, wsem = wqr.next()
                DMA('pool', wq[:, :, :], Wq[:, qcol:qcol + 128].rearrange("(k p) n -> p k n", p=128), [], [wq], wsem)
                vt, vsem = vtr.next()
                vsrc = vout[0:TP, c * 128:(c + 1) * 128]
                rd = [vdram[ti_][c // 2] for ti_ in range(16)]
                if dil == 1:
                    DMA('pool', vt[:, :, :], vsrc.rearrange("(b p) d -> p b d", p=128), rd, [vt], vsem)
                elif dil == 4:
                    DMA('pool', vt.t[:, :, :].rearrange("p (r b) d -> p r b d", b=4), vsrc.rearrange("(b p r) d -> p r b d", p=128, r=4), rd, [vt], vsem)
                else:
                    DMA('pool', vt[:, :, :], vsrc.rearrange("(p r) d -> p r d", r=16), rd, [vt], vsem)
                pre[i] = (wq, vt)

            def qhead(i):
                c, gq, gi = heads[i]
                s = 2 * c + gq
                win, dil = DILS[gi]
                wq, vt = pre.pop(i)
                if i + 1 < len(heads):
                    prefetch(i + 1)

                def qsink_a(g, kn):
                    g0, gn = GROUPS[g]
                    CP('act', qt[:, g0:g0 + gn], kn[:, 0:gn], R=[kn], W=[qt.sub(('hi', g)), qt.sub(('lo', g))])

                def qsink_b(g, knc, sw, swt):
                    g0, gn = GROUPS[g]
                    TT('pool', qt[0:32, g0:g0 + gn], knc[0:32, 0:gn], sw[0:32, 0:gn], ALU.add, R=[knc, *swt], W=[qt.sub(('lo', g))])
                    if g == 4:
                        CP('act', qS[:, gi, s, :], qt[:, TP:T], R=qt_all, W=[qS])
                proj_norm_rope(wq.t, [wq], f'b_q_norm{jl}', qsink_a, qsink_b)

                def geom(u):
                    if dil == 1:
                        return 0, u
                    if dil == 4:
                        return u % 4, u // 4
                    return u, 0

                def qdeps(b):
                    gl_ = [b // 4] if dil == 1 else ([b] if dil == 4 else [0, 1, 2, 3])
                    return [qt.sub((hl, g_)) for hl in ('hi', 'lo') for g_ in gl_]

                def cols(r, bb):
                    st = r + dil * 128 * bb
                    return slice(st, st + dil * 127 + 1, dil)

                def vidx(r, bb):
                    return bb if dil == 1 else (r * 4 + bb if dil == 4 else r)

                def s_stage(u):
                    r, b = geom(u)
                    blocks = ([b - 1] if b > 0 else []) + [b]
                    Wd = 128 * len(blocks)
                    pss = P.next_ps()
                    MM(pss[:, 0:Wd], cb('identB'), cb('negmask2', 256 - Wd, Wd), start=True, stop=False, R=[cB], W=[pss])
                    for bi, kb in enumerate(blocks):
                        MM(pss[:, bi * 128:(bi + 1) * 128], kT[:, c, cols(r, kb)], qt[:, cols(r, b)], start=False, stop=(bi == len(blocks) - 1),
                           R=[kT, *qdeps(b)], W=[pss])
                    pt, _ = ptr.next()
                    ACT(pt[:, 0:Wd], pss[:, 0:Wd], AF.Exp, R=[pss], W=[pt], scale=scale)
                    return (r, b, blocks, pt)

                def pv_stage(stt):
                    r, b, blocks, pt = stt
                    qc = cols(r, b)
                    pso = P.next_ps()
                    psl = P.next_ps()
                    for bi, kb in enumerate(blocks):
                        MM(pso[:, 0:128], vt[:, vidx(r, kb), :], pt[:, bi * 128:(bi + 1) * 128], start=(bi == 0), stop=(bi == len(blocks) - 1), R=[vt, pt], W=[pso])
                    for bi, kb in enumerate(blocks):
                        MM(psl[:, 0:128], cb('ones'), pt[:, bi * 128:(bi + 1) * 128], start=(bi == 0), stop=(bi == len(blocks) - 1), R=[cB, pt], W=[psl])
                    if gi == 0:
                        CP('act', accO[:, qc], pso[:, 0:128], R=[pso], W=[accO])
                        CP('dve', accL[:, qc], psl[:, 0:128], R=[psl], W=[accL])
                    else:
                        TT('dve', accO[:, qc], accO[:, qc], pso[:, 0:128], ALU.add, R=[pso, accO], W=[accO])
                        TT('dve', accL[:, qc], accL[:, qc], psl[:, 0:128], ALU.add, R=[psl, accL], W=[accL])

                prev = s_stage(0)
                for u in range(1, 16):
                    cur = s_stage(u)
                    pv_stage(prev)
                    prev = cur
                pv_stage(prev)

            def sample_attn(c):
                stA = {}

                def a_stage(b):
                    Kc, ksem = Kcr.next()
                    Vc, vsem = Vcr.next()
                    for (Cd, Cb, sem) in ((ck, Kc, ksem), (cv, Vc, vsem)):
                        hs_ = Cd[b][:, c * 128:(c + 1) * 128]
                        DMA('pool', Cb[:, 0, :], hs_[1920:2048, :], [], [Cb.sub(0)], sem)
                        DMA('pool', Cb[:, 1:5, :], hs_[1536:2048, :].rearrange("(j t) d -> j t d", t=4), [], [Cb.sub(1)], sem)
                        DMA('pool', Cb[:, 5:9, :], hs_.rearrange("(j s) d -> j s d", s=16)[:, 0:4, :], [], [Cb.sub(2)], sem)
                    pb0 = P.next_pb()
                    pb1 = P.next_pb()
                    for tl in range(9):
                        pbx = pb0 if tl < 8 else pb1
                        o = (tl % 8) * 128
                        TR(pbx[:, o:o + 128], Kc[:, tl, :], cb('identB'), R=[Kc.sub(0), Kc.sub(1), Kc.sub(2), cB], W=[pbx])
                    KTs, _ = KTr.next()
                    CP('act', KTs.t[:, 0:8, :].rearrange("p a b -> p (a b)"), pb0[:, 0:1024], R=[pb0], W=[KTs])
                    CP('dve', KTs[:, 8, :], pb1[:, 0:128], R=[pb1], W=[KTs])
                    stA[b] = (KTs, Vc)

                def bc_stage(b):
                    KTs, Vc = stA.pop(b)
                    pss = P.next_ps()
                    MM(pss[:, 0:72], cb('identB'), cb('negfull'), start=True, stop=False, R=[cB], W=[pss])
                    MM(pss[:, 0:8], KTs[:, 0, :], qS[:, 0, 2 * c:2 * c + 2, 4 * b:4 * b + 4], start=False, stop=False, R=[KTs, qS], W=[pss])
                    for gi in (1, 2):
                        for tt in range(4):
                            tl = (1 if gi == 1 else 5) + tt
                            MM(pss[:, tl * 8 + tt:tl * 8 + tt + 5:4], KTs[:, tl, :], qS[:, gi, 2 * c:2 * c + 2, 4 * b + tt], start=False,
                               stop=(gi == 2 and tt == 3), R=[KTs, qS], W=[pss])
                    pf, _ = pfull.next()
                    ACT(pf[:, :], pss[:, 0:72], AF.Exp, R=[pss], W=[pf], scale=scale)
                    psn = P.next_ps()
                    MM(psn[0:64, 0:24], cb('identB', 0, 64, rows=64), cb('negNew', b * 24, 24, rows=64), start=True, stop=False, R=[cB], W=[psn])
                    MM(psn[0:64, 0:24], kT[:, c, TP:T], qS[:, :, 2 * c:2 * c + 2, 4 * b:4 * b + 4], start=False, stop=True, R=[kT, qS], W=[psn])
                    pn, _ = pnr.next()
                    ACT(pn[0:64, :], psn[0:64, 0:24], AF.Exp, R=[psn], W=[pn], scale=scale)
                    pso = P.next_ps()
                    psl = P.next_ps()
                    for tl in range(9):
                        MM(pso[:, 0:8], Vc[:, tl, :], pf[:, tl * 8:tl * 8 + 8], start=(tl == 0), stop=False, R=[Vc.sub(0), Vc.sub(1), Vc.sub(2), pf], W=[pso])
                    for gi in range(3):
                        MM(pso[:, 0:8], v64[0:64, c * 128:(c + 1) * 128], pn[0:64, gi * 8:gi * 8 + 8], start=False, stop=(gi == 2), R=[v64, pn], W=[pso])
                    for tl in range(9):
                        MM(psl[:, 0:8], cb('ones'), pf[:, tl * 8:tl * 8 + 8], start=(tl == 0), stop=False, R=[cB, pf], W=[psl])
                    for gi in range(3):
                        MM(psl[:, 0:8], cb('ones', rows=64), pn[0:64, gi * 8:gi * 8 + 8], start=False, stop=(gi == 2), R=[cB, pn], W=[psl])
                    rl, _ = rlr.next()
                    RCP(rl[:, :], psl[:, 0:8], R=[psl], W=[rl])
                    TT('dve', PH.G[:, 0:2, TP + 4 * b:TP + 4 * b + 4], pso.t[:, 0:8].rearrange("p (a b) -> p a b", b=4),
                       rl.t[:, :].rearrange("p (a b) -> p a b", b=4), ALU.mult, R=[pso, rl], W=[gs(4)])

                a_stage(0)
                for b in range(NSEQ):
                    if b + 1 < NSEQ:
                        a_stage(b + 1)
                    bc_stage(b)

            prefetch(0)
            for c in range(4):
                for gq in range(2):
                    for gi in range(3):
                        qhead((c * 2 + gq) * 3 + gi)
                    ACT(accL[:, :], accL[:, :], AF.Ln, R=[accL], W=[accL])
                    ACT(accL[:, :], accL[:, :], AF.Exp, R=[accL], W=[accL], scale=-1.0)
                    for g in range(4):
                        g0, gn = GROUPS[g]
                        TT('dve', PH.G[:, gq, g0:g0 + gn], accO[:, g0:g0 + gn], accL[:, g0:g0 + gn], ALU.mult, R=[accO, accL], W=[gs(g)])
                sample_attn(c)
                proj_add_x(b_w_o[jl], c * 256, 2)

    if steps is None:
        steps = ['gla0', 'ffn0', 'ple0', 'gla1', 'ffn1', 'ple1', 'kv', 'attn0', 'ffn2', 'ple2', 'attn1', 'ffn3', 'ple3']
    load_x()
    kT = v64 = None
    for st in steps:
        if st == 'kv':
            kT = P.sbuf("kT", [128, 4, T], BF16)
            v64 = P.sbuf("v64", [64, 512], BF16)
        with P.scope():
            if st.startswith('gla'):
                phase_alloc(4, 2048)
                gla(int(st[3:]))
            elif st.startswith('ffn'):
                phase_alloc(4, 2048)
                ffn(int(st[3:]))
            elif st.startswith('ple'):
                phase_alloc(4, 2048)
                ple(int(st[3:]))
            elif st == 'kv':
                phase_alloc(0, 2048)
                rope_alloc()
                shared_kv(kT, v64)
            elif st.startswith('attn'):
                phase_alloc(2, 512)
                rope_alloc()
                battn(int(st[4:]), kT, v64)
    store_y()
    P.emit()
    return nc


_CACHE = {}


def _prep(inputs, ncores=NCORES):
    cfc, cbc, cgc = _layout_consts(inputs)
    cstf = cfc.array()
    cstb = cbc.array()
    cstg = cgc.array()
    ropet = _rope_table()
    f32 = lambda a: np.ascontiguousarray(np.asarray(a, np.float32))
    wgk = f32(np.concatenate([inputs['a_w_gk_up'], inputs['a_b_gk'][:, None, :]], axis=1))
    shared = {k: f32(inputs[k]) for k in ['a_w_in', 'a_w_o', 'w_kv', 'b_w_q', 'b_w_o', 'f_w_in', 'f_w_out', 'e_w_proj', 'e_w_gate']}
    shared.update(wgk=wgk, cstf=cstf, cstb=cstb, cstg=cstg, rope=ropet)
    in_maps = []
    for i in range(ncores):
        sl = slice(NSEQ * i, NSEQ * (i + 1))
        m = dict(shared)
        m['xin'] = f32(np.concatenate([inputs['x_prompt'][i], inputs['x_sample'][sl].reshape(TS, D)], axis=0))
        m['pin'] = f32(np.concatenate([inputs['p_prompt'][:, i], inputs['p_sample'][:, sl].reshape(4, TS, 256)], axis=1))
        m['sgla'] = f32(inputs['state_gla'][:, sl])
        m['ck'] = f32(inputs['cache_k'][sl].reshape(NSEQ, 2048, 512))
        m['cv'] = f32(inputs['cache_v'][sl].reshape(NSEQ, 2048, 512))
        in_maps.append(m)
    return cfc, cbc, cgc, in_maps


def kernel(**inputs):
    cfc, cbc, cgc, in_maps = _prep(inputs)
    nc = build(cfc.off, cfc.n, cbc.off, cbc.n, cgc.off, cgc.n)
    res = run_bass_kernel_spmd(nc, in_maps, core_ids=list(range(NCORES)))
    r = res.results
    y_prompt = np.stack([r[i]['y'][0:TP] for i in range(NCORES)], 0)
    y_sample = np.concatenate([r[i]['y'][TP:T].reshape(NSEQ, 4, D) for i in range(NCORES)], 0)
    gsp = np.stack([r[i]['gsp'] for i in range(NCORES)], 1)
    gss = np.concatenate([r[i]['gss'] for i in range(NCORES)], 1)
    k_prompt = np.stack([r[i]['kout'][0:TP].reshape(TP, 4, 128) for i in range(NCORES)], 0)
    v_prompt = np.stack([r[i]['vout'][0:TP].reshape(TP, 4, 128) for i in range(NCORES)], 0)
    k_sample = np.concatenate([r[i]['kout'][TP:T].reshape(NSEQ, 4, 4, 128) for i in range(NCORES)], 0)
    v_sample = np.concatenate([r[i]['vout'][TP:T].reshape(NSEQ, 4, 4, 128) for i in range(NCORES)], 0)
    f = lambda a: np.ascontiguousarray(a, dtype=np.float32)
    return (f(y_prompt), f(y_sample), f(gsp), f(gss), f(k_prompt), f(v_prompt), f(k_sample), f(v_sample))
```
